# Optimizing a Trainium2 kernel written in Bass

```python
import math
import jax, jax.numpy as jnp
from jax import lax
import numpy as np

D_MODEL = 1024
BATCH = 2
SEQ = 8192
DEPTH = 2

HEAD_DIM = 64
D_MIX = D_MODEL
N_HEADS_TOTAL = D_MIX // HEAD_DIM
N_HEADS_DIL = N_HEADS_TOTAL // 2
N_LRU_BLOCKS = N_HEADS_TOTAL // 4
N_HEADS_SB = N_HEADS_TOTAL // 4
W_DIL = N_HEADS_DIL * HEAD_DIM
W_LRU = N_LRU_BLOCKS * HEAD_DIM
W_SB = N_HEADS_SB * HEAD_DIM
D_IN = 4 * W_DIL + 2 * W_LRU + 4 * W_SB
ROPE_DIM = HEAD_DIM // 4
ROPE_THETA = 500000.0
DILATED_PATTERNS = ((128, 1), (512, 4), (2048, 16))
CONV_WIDTH = 4
LRU_C = 8.0
Q_BLOCK = 128
EPS = 1e-6

kernel_name = 'hybrid_dilated_rglru_stickbreaking'


def _in_split_points():
    sizes = [W_DIL] * 4 + [W_LRU] * 2 + [W_SB] * 4
    points, acc = [], 0
    for s in sizes[:-1]:
        acc += s
        points.append(acc)
    return points


def rms_norm(x, gain):
    xf = x.astype(jnp.float32)
    xf = xf * lax.rsqrt(jnp.mean(xf * xf, axis=-1, keepdims=True) + EPS)
    return (xf * gain.astype(jnp.float32)).astype(x.dtype)


def rope_tables(seq_len):
    pos = jnp.arange(seq_len, dtype=jnp.float32)
    inv_freq = ROPE_THETA ** (-jnp.arange(0, ROPE_DIM, 2, dtype=jnp.float32) / ROPE_DIM)
    ang = pos[:, None] * inv_freq[None, :]
    return jnp.cos(ang), jnp.sin(ang)


def apply_partial_rope(x, cos, sin):
    half = ROPE_DIM // 2
    xf = x.astype(jnp.float32)
    x1 = xf[..., :half]
    x2 = xf[..., half:ROPE_DIM]
    c = cos[None, :, None, :]
    s = sin[None, :, None, :]
    out = jnp.concatenate([x1 * c - x2 * s, x2 * c + x1 * s, xf[..., ROPE_DIM:]], axis=-1)
    return out.astype(x.dtype)


def dilated_attention(q, k, v):
    b, h, seq, _ = q.shape
    scale = 1.0 / math.sqrt(HEAD_DIM)
    offs = jnp.arange(Q_BLOCK)

    def block(blk):
        t = blk * Q_BLOCK + offs
        q_blk = lax.dynamic_slice_in_dim(q, blk * Q_BLOCK, Q_BLOCK, axis=2).astype(jnp.float32) * scale
        lses, outs = [], []
        for window, dilation in DILATED_PATTERNS:
            n_keys = window // dilation + 1
            idx = t[:, None] - dilation * jnp.arange(n_keys)[None, :]
            valid = idx >= 0
            idx = jnp.maximum(idx, 0)
            k_g = jnp.take(k, idx, axis=2).astype(jnp.float32)
            v_g = jnp.take(v, idx, axis=2).astype(jnp.float32)
            s = jnp.einsum('bhqd,bhqjd->bhqj', q_blk, k_g)
            s = jnp.where(valid[None, None], s, -jnp.inf)
            lse = jax.nn.logsumexp(s, axis=-1)
            p = jnp.exp(s - lse[..., None])
            outs.append(jnp.einsum('bhqj,bhqjd->bhqd', p, v_g))
            lses.append(lse)
        mix = jax.nn.softmax(jnp.stack(lses), axis=0)
        return jnp.sum(mix[..., None] * jnp.stack(outs), axis=0)

    out = lax.map(block, jnp.arange(seq // Q_BLOCK))
    return out.transpose(1, 2, 0, 3, 4).reshape(b, h, seq, HEAD_DIM).astype(q.dtype)


def stick_breaking_attention(q, k, v):
    b, h, seq, _ = q.shape
    scale = 1.0 / math.sqrt(HEAD_DIM)
    offs = jnp.arange(Q_BLOCK)
    key_pos = jnp.arange(seq)
    kf = k.astype(jnp.float32)
    vf = v.astype(jnp.float32)

    def block(blk):
        t = blk * Q_BLOCK + offs
        q_blk = lax.dynamic_slice_in_dim(q, blk * Q_BLOCK, Q_BLOCK, axis=2).astype(jnp.float32)
        z = jnp.einsum('bhqd,bhkd->bhqk', q_blk, kf) * scale
        causal = key_pos[None, :] < t[:, None]
        log_keep = jnp.where(causal, jax.nn.log_sigmoid(-z), 0.0)
        tail = lax.cumsum(log_keep, axis=3, reverse=True)
        between = jnp.concatenate([tail[..., 1:], jnp.zeros_like(tail[..., :1])], axis=-1)
        weights = jnp.where(causal, jnp.exp(jax.nn.log_sigmoid(z) + between), 0.0)
        return jnp.einsum('bhqk,bhkd->bhqd', weights, vf)

    out = lax.map(block, jnp.arange(seq // Q_BLOCK))
    return out.transpose(1, 2, 0, 3, 4).reshape(b, h, seq, HEAD_DIM).astype(q.dtype)


def _linear_recurrence_combine(left, right):
    a1, b1 = left
    a2, b2 = right
    return a1 * a2, a2 * b1 + b2


def rg_lru_branch(x, conv_w, conv_b, gate_a_w, gate_a_b, gate_x_w, gate_x_b, lru_lambda):
    b, s, w = x.shape
    xc = lax.conv_general_dilated(
        x, conv_w[:, None, :].astype(x.dtype), window_strides=(1,), padding=[(CONV_WIDTH - 1, 0)],
        dimension_numbers=('NWC', 'WIO', 'NWC'), feature_group_count=w) + conv_b
    xg = xc.reshape(b, s, N_LRU_BLOCKS, HEAD_DIM)
    r = jax.nn.sigmoid(jnp.einsum('bsnd,nde->bsne', xg, gate_a_w) + gate_a_b).reshape(b, s, w)
    i = jax.nn.sigmoid(jnp.einsum('bsnd,nde->bsne', xg, gate_x_w) + gate_x_b).reshape(b, s, w)
    log_a = LRU_C * r.astype(jnp.float32) * jax.nn.log_sigmoid(lru_lambda.astype(jnp.float32))
    a = jnp.exp(log_a)
    u = jnp.sqrt(-jnp.expm1(2.0 * log_a)) * (i * xc).astype(jnp.float32)
    _, h = lax.associative_scan(_linear_recurrence_combine, (a, u), axis=1)
    return h.astype(x.dtype)


def hybrid_layer(x, cos, sin, norm_gain, w_in, conv_w, conv_b, gate_a_w, gate_a_b,
                 gate_x_w, gate_x_b, lru_lambda, w_out):
    b, s, _ = x.shape
    hn = rms_norm(x, norm_gain)
    proj = jnp.einsum('bsd,de->bse', hn, w_in)
    a_q, a_k, a_v, a_g, b_x, b_g, c_q, c_k, c_v, c_g = jnp.split(proj, _in_split_points(), axis=-1)

    qa = apply_partial_rope(a_q.reshape(b, s, N_HEADS_DIL, HEAD_DIM), cos, sin).transpose(0, 2, 1, 3)
    ka = apply_partial_rope(a_k.reshape(b, s, N_HEADS_DIL, HEAD_DIM), cos, sin).transpose(0, 2, 1, 3)
    va = a_v.reshape(b, s, N_HEADS_DIL, HEAD_DIM).transpose(0, 2, 1, 3)
    y_a = dilated_attention(qa, ka, va).transpose(0, 2, 1, 3).reshape(b, s, W_DIL) * jax.nn.silu(a_g)

    y_b = rg_lru_branch(b_x, conv_w, conv_b, gate_a_w, gate_a_b, gate_x_w, gate_x_b, lru_lambda) * jax.nn.silu(b_g)

    qc = c_q.reshape(b, s, N_HEADS_SB, HEAD_DIM).transpose(0, 2, 1, 3)
    kc = c_k.reshape(b, s, N_HEADS_SB, HEAD_DIM).transpose(0, 2, 1, 3)
    vc = c_v.reshape(b, s, N_HEADS_SB, HEAD_DIM).transpose(0, 2, 1, 3)
    y_c = stick_breaking_attention(qc, kc, vc).transpose(0, 2, 1, 3).reshape(b, s, W_SB) * jax.nn.silu(c_g)

    y = jnp.concatenate([y_a, y_b, y_c], axis=-1)
    return x + jnp.einsum('bse,ed->bsd', y, w_out)


def setup_inputs(seed: int = 0) -> dict:
    key = jax.random.key(seed)
    ks = jax.random.split(key, 13)
    f32 = jnp.float32
    x = jax.random.normal(ks[0], (BATCH, SEQ, D_MODEL), f32)
    norm_gain = 1.0 + 0.02 * jax.random.normal(ks[1], (DEPTH, D_MODEL), f32)
    w_in = jax.random.normal(ks[2], (DEPTH, D_MODEL, D_IN), f32) * D_MODEL ** -0.5
    conv_w = jax.random.normal(ks[3], (DEPTH, CONV_WIDTH, W_LRU), f32) * CONV_WIDTH ** -0.5
    conv_b = 0.01 * jax.random.normal(ks[4], (DEPTH, W_LRU), f32)
    gate_a_w = jax.random.normal(ks[5], (DEPTH, N_LRU_BLOCKS, HEAD_DIM, HEAD_DIM), f32) * HEAD_DIM ** -0.5
    gate_a_b = 0.01 * jax.random.normal(ks[6], (DEPTH, N_LRU_BLOCKS, HEAD_DIM), f32)
    gate_x_w = jax.random.normal(ks[7], (DEPTH, N_LRU_BLOCKS, HEAD_DIM, HEAD_DIM), f32) * HEAD_DIM ** -0.5
    gate_x_b = 0.01 * jax.random.normal(ks[8], (DEPTH, N_LRU_BLOCKS, HEAD_DIM), f32)
    u = jax.random.uniform(ks[9], (DEPTH, W_LRU), f32, minval=0.9, maxval=0.999)
    a0 = u ** (1.0 / LRU_C)
    lru_lambda = jnp.log(a0) - jnp.log1p(-a0)
    w_out = jax.random.normal(ks[10], (DEPTH, D_MIX, D_MODEL), f32) * D_MIX ** -0.5
    final_gain = 1.0 + 0.02 * jax.random.normal(ks[11], (D_MODEL,), f32)
    return {'x': x, 'norm_gain': norm_gain, 'w_in': w_in, 'conv_w': conv_w, 'conv_b': conv_b,
            'gate_a_w': gate_a_w, 'gate_a_b': gate_a_b, 'gate_x_w': gate_x_w, 'gate_x_b': gate_x_b,
            'lru_lambda': lru_lambda, 'w_out': w_out, 'final_gain': final_gain}


def reference(x, norm_gain, w_in, conv_w, conv_b, gate_a_w, gate_a_b, gate_x_w, gate_x_b,
              lru_lambda, w_out, final_gain):
    cos, sin = rope_tables(x.shape[1])
    h = x
    for l in range(DEPTH):
        h = hybrid_layer(h, cos, sin, norm_gain[l], w_in[l], conv_w[l], conv_b[l], gate_a_w[l],
                         gate_a_b[l], gate_x_w[l], gate_x_b[l], lru_lambda[l], w_out[l])
    return rms_norm(h, final_gain)
```

```python
import math
from contextlib import ExitStack

import numpy as np
import ml_dtypes

import concourse.bass as bass
import concourse.mybir as mybir
from concourse.bass_utils import run_bass_kernel_spmd

F32 = mybir.dt.float32
BF16 = mybir.dt.bfloat16
AF = mybir.ActivationFunctionType
ALU = mybir.AluOpType

D = 1024
B = 2
SEQ = 8192
DEPTH = 2
HD = 64
NQ = 4
TQ = SEQ // NQ
TILE = 512
NCOL = 7 * 128
F32R = mybir.dt.float32r
EPS = 1e-6
ROPE_THETA = 500000.0
ENGS = ("sp", "act", "dve", "pool", "pe")


class Buf:
    __slots__ = ("name", "w", "r", "dsem", "dcount", "dma_w", "dma_r", "excl")

    def __init__(self, name):
        self.name = name
        self.excl = False
        self.w = None
        self.r = {}
        self.dsem = None
        self.dcount = 0
        self.dma_w = 0
        self.dma_r = 0


class DramBuf:
    def __init__(self, name):
        self.name = name
        self.events = []


class Sched:
    def __init__(self, nc, stack):
        self.nc = nc
        self.stack = stack
        self.ops = {e: [] for e in ENGS}
        self.count = {e: 0 for e in ENGS}
        self.seen = {e: {f: 0 for f in ENGS} for e in ENGS}
        self.seen_d = {e: {} for e in ENGS}
        self.sem = {}
        for e in ("act", "dve", "pool", "pe"):
            self.sem[e] = stack.enter_context(nc.semaphore("s_" + e))
        self.nbuf = 0
        self.dbufs = []

    def dram(self, name):
        return DramBuf(name)

    def buf(self, name=None):
        self.nbuf += 1
        return Buf(name or ("b%d" % self.nbuf))

    def bufs(self, n, name="b"):
        return [self.buf("%s%d" % (name, i)) for i in range(n)]

    def alias(self, old, name=None):
        b = self.buf(name)
        b.w = old.w
        b.r = dict(old.r)
        b.dsem = old.dsem
        b.dcount = old.dcount
        b.dma_w = old.dma_w
        b.dma_r = max(old.dma_r, 0)
        return b

    def _dsem(self, b):
        if b.dsem is None:
            self.nsem = getattr(self, "nsem", 0) + 1
            b.dsem = self.stack.enter_context(self.nc.semaphore("d%d_%s" % (self.nsem, b.name)))
            self.dbufs.append(b)
        return b.dsem

    def _need(self, eng, waits, dep):
        if dep is None:
            return
        e2, idx = dep
        if e2 == eng and eng == "pe":
            return
        if self.seen[eng][e2] >= idx:
            return
        self.seen[eng][e2] = idx
        waits.append((self.sem[e2], idx))

    def _need_d(self, eng, waits, b, val):
        if val <= 0:
            return
        key = id(b.dsem)
        if self.seen_d[eng].get(key, 0) >= val:
            return
        self.seen_d[eng][key] = val
        waits.append((b.dsem, val))

    def _deps(self, eng, reads, writes):
        waits = []
        for b in reads:
            if b.w is not None:
                self._need(eng, waits, b.w)
            if b.dma_w:
                self._need_d(eng, waits, b, b.dma_w)
            if b.excl:
                for e2, idx in b.r.items():
                    if e2 != eng:
                        self._need(eng, waits, (e2, idx))
        for b in writes:
            if b.w is not None:
                self._need(eng, waits, b.w)
            if b.dma_w:
                self._need_d(eng, waits, b, b.dma_w)
            for e2, idx in b.r.items():
                self._need(eng, waits, (e2, idx))
            if b.dma_r:
                self._need_d(eng, waits, b, b.dma_r)
        return waits

    def op(self, eng, fn, reads=(), writes=()):
        waits = self._deps(eng, reads, writes)
        self.count[eng] += 1
        idx = self.count[eng]
        self.ops[eng].append((waits, fn, (self.sem[eng], 1)))
        for b in reads:
            b.r[eng] = idx
        for b in writes:
            b.w = (eng, idx)
            b.r = {}
            b.dma_w = 0
            b.dma_r = 0
        return idx

    def _need_ev(self, eng, waits, sem, val):
        key = id(sem)
        if self.seen_d[eng].get(key, 0) >= val:
            return
        self.seen_d[eng][key] = val
        waits.append((sem, val))

    def dma(self, eng, fn, sb, sb_is_dst, dram_r=None, dram_w=None, extra_reads=()):
        self._dsem(sb)
        reads = list(extra_reads) + ([] if sb_is_dst else [sb])
        writes = [sb] if sb_is_dst else []
        waits = self._deps(eng, reads, writes)
        if dram_r is not None:
            for dr in (dram_r if isinstance(dram_r, (list, tuple)) else [dram_r]):
                for sem, val in dr.events:
                    self._need_ev(eng, waits, sem, val)
        sb.dcount += 16
        self.ops[eng].append((waits, fn, (sb.dsem, 16)))
        if sb_is_dst:
            sb.w = None
            sb.r = {}
            sb.dma_w = sb.dcount
            sb.dma_r = 0
        else:
            sb.dma_r = sb.dcount
        if dram_w is not None:
            dram_w.events = [(s_, v_) for (s_, v_) in dram_w.events if s_ is not sb.dsem]
            dram_w.events.append((sb.dsem, sb.dcount))

    def collective(self, fn, src, dst):
        waits = []
        for sem, val in src.events:
            self._need_ev("pool", waits, sem, val)
        sem = self.stack.enter_context(self.nc.semaphore("cc_" + dst.name))
        self.ops["pool"].append((waits, fn, (sem, 1)))
        dst.events = [(sem, 1)]

    def barrier(self):
        for eng in ENGS:
            waits = []
            for e2 in ("act", "dve", "pool", "pe"):
                if e2 != eng and self.count[e2] > self.seen[eng][e2]:
                    self.seen[eng][e2] = self.count[e2]
                    waits.append((self.sem[e2], self.count[e2]))
            for b in self.dbufs:
                if b.dcount:
                    self._need_ev(eng, waits, b.dsem, b.dcount)
            if waits:
                self.ops[eng].append((waits, None, None))

    def wait_all_dma(self, eng, bufs):
        waits = []
        seen = set()
        for b in bufs:
            if b.dsem is not None and b.dcount and id(b.dsem) not in seen:
                seen.add(id(b.dsem))
                waits.append((b.dsem, b.dcount))
        self.ops[eng].append((waits, None, None))

    def emit(self):
        nc = self.nc

        def run(engobj, lst):
            for waits, fn, inc in lst:
                for s_, v_ in waits:
                    engobj.wait_ge(s_, v_)
                if fn is not None:
                    fn(engobj).then_inc(inc[0], inc[1])

        with nc.Block() as block:
            if self.ops["sp"]:
                @block.sync
                def _(e):
                    run(e, self.ops["sp"])
            if self.ops["act"]:
                @block.scalar
                def _(e):
                    run(e, self.ops["act"])
            if self.ops["dve"]:
                @block.vector
                def _(e):
                    run(e, self.ops["dve"])
            if self.ops["pool"]:
                @block.gpsimd
                def _(e):
                    run(e, self.ops["pool"])
            if self.ops["pe"]:
                @block.tensor
                def _(e):
                    run(e, self.ops["pe"])


class Ctx:
    def __init__(self, nc, stack):
        self.nc = nc
        self.st = stack
        self.S = Sched(nc, stack)
        self.uid = 0

    def sb(self, name, shape, dt):
        self.uid += 1
        return self.st.enter_context(self.nc.sbuf_tensor("%s_%d" % (name, self.uid), shape, dt))

    def internal(self, name, shape, dt):
        return self.nc.dram_tensor(name, shape, dt)

    def din(self, name, shape, dt):
        return self.nc.dram_tensor(name, shape, dt, kind="ExternalInput").ap()

    def dout(self, name, shape, dt):
        return self.nc.dram_tensor(name, shape, dt, kind="ExternalOutput").ap()


def emit_phase_N(C, ps, bank, x_in, x_in_d, y_ag, y_ag_ds, idx_sb, b_idx, wo, gain, hn_tile, after_store,
                 x_out, x_out_d, final):
    S = C.S
    nt = TQ // TILE
    has_proj = y_ag is not None
    ones = C.sb("n_ones", [128, 128], F32)
    b_ones = S.buf("n_ones")
    S.op("pool", lambda e: e.memset(ones[:], 1.0), writes=[b_ones])
    g_sb = C.sb("n_gain", [128, 8], F32)
    b_g = S.buf("n_gain")
    S.dma("sp", lambda e: e.dma_start(out=g_sb[:], in_=gain), b_g, True)
    if has_proj:
        ysb = C.sb("n_ysb", [128, 8, TQ], BF16)
        b_ysb = [S.buf("n_ysb%d" % jj) for jj in range(TQ // TILE)]
        for jj in range(TQ // TILE):
            if jj == TQ // TILE - 1 and getattr(C, "deferred_ag", None) is not None:
                C.deferred_ag()
                C.deferred_ag = None
            for kc in range(8):
                S.dma("pool", lambda e, kc=kc, jj=jj: e.indirect_dma_start(
                    out=ysb[:, kc, jj * TILE:(jj + 1) * TILE], out_offset=None, in_=y_ag[:, :],
                    in_offset=bass.IndirectOffsetOnAxis(ap=idx_sb[:, kc * 4 + jj:kc * 4 + jj + 1], axis=0)),
                    b_ysb[jj], True, dram_r=y_ag_ds[jj], extra_reads=[b_idx])
        wob, b_wob = wo
    xt = [C.sb("n_x%d" % i, [128, 8, TILE], F32) for i in range(2)]
    b_xt = S.bufs(2, "n_x")
    sq = [C.sb("n_sq%d" % i, [128, TILE], F32) for i in range(2)]
    b_sq = S.bufs(2, "n_sq")
    rstd = C.sb("n_rstd", [128, TILE], F32)
    b_rstd = S.buf("n_rstd")
    odt = F32 if final else BF16
    ht = [C.sb("n_h%d" % i, [128, 8, TILE], odt) for i in range(2)]
    b_ht = S.bufs(2, "n_h")
    xTr = x_in.rearrange("(c p) t -> p c t", p=128)
    if x_out is not None:
        xor = x_out.rearrange("(c p) t -> p c t", p=128)
    nb = 0
    for t in range(nt):
        j = t % 2
        ts = slice(t * TILE, (t + 1) * TILE)
        S.dma("sp", lambda e, j=j, ts=ts: e.dma_start(out=xt[j][:], in_=xTr[:, :, ts]), b_xt[j], True, dram_r=x_in_d)
        if has_proj:
            for fc in range(8):
                bk = nb % 4
                nb += 1
                for kc in range(8):
                    S.op("pe", lambda e, bk=bk, kc=kc, fc=fc, ts=ts: e.matmul(
                        ps[:, bk, :], wob[:, kc, fc * 128:(fc + 1) * 128], ysb[:, kc, ts],
                        start=(kc == 0), stop=(kc == 7)),
                        reads=[b_wob, b_ysb[t]], writes=[bank[bk]])
                S.op("dve", lambda e, bk=bk, fc=fc, j=j: e.tensor_tensor(
                    xt[j][:, fc, :], ps[:, bk, :], xt[j][:, fc, :], ALU.add),
                    reads=[bank[bk], b_xt[j]], writes=[b_xt[j]])
            if x_out is not None:
                S.dma("act", lambda e, j=j, ts=ts: e.dma_start(out=xor[:, :, ts], in_=xt[j][:]), b_xt[j], False,
                      dram_w=x_out_d)
        for fc in range(8):
            k = fc % 2
            S.op("act", lambda e, k=k, fc=fc, j=j: e.activation(sq[k][:], xt[j][:, fc, :], AF.Square),
                 reads=[b_xt[j]], writes=[b_sq[k]])
            S.op("pe", lambda e, k=k, fc=fc: e.matmul(ps[:, 4, :], ones[:], sq[k][:], start=(fc == 0), stop=(fc == 7)),
                 reads=[b_ones, b_sq[k]], writes=[bank[4]])
        S.op("act", lambda e: e.activation(rstd[:], ps[:, 4, :], AF.Ln, bias=EPS, scale=1.0 / D),
             reads=[bank[4]], writes=[b_rstd])
        S.op("act", lambda e: e.activation(rstd[:], rstd[:], AF.Exp, scale=-0.5), reads=[b_rstd], writes=[b_rstd])
        for fc in range(8):
            S.op("dve", lambda e, fc=fc, j=j: e.scalar_tensor_tensor(
                ht[j][:, fc, :], xt[j][:, fc, :], g_sb[:, fc:fc + 1], rstd[:], ALU.mult, ALU.mult),
                reads=[b_xt[j], b_g, b_rstd], writes=[b_ht[j]])
        h_ap, h_d = hn_tile(t)
        S.dma("act", lambda e, j=j, h_ap=h_ap: e.dma_start(out=h_ap, in_=ht[j][:]), b_ht[j], False, dram_w=h_d)
        if after_store is not None:
            after_store(t)
    return b_xt + b_ht


def emit_phase_M(C, ps, bank, hn_ag, hn_ag_ds, win, ctab, stab, lru_p, gw, y_src, y_src_ds, after_y):
    S = C.S
    nc = C.nc
    NT = SEQ // TILE
    psb_ = [ps[:, 6, :].bitcast(BF16), ps[:, 7, :].bitcast(BF16)]

    class _PSB:
        def __getitem__(self, idx):
            p, b, c = idx
            return psb_[b][p, c]
    psb = _PSB()

    ident = C.sb("ident", [128, 128], BF16); b_ident = S.buf("ident")
    S.op("pool", lambda e: e.memset(ident[:], 1.0), writes=[b_ident])
    S.op("pool", lambda e: e.affine_select(ident[:], ident[:], [[-1, 128]], ALU.is_equal, 0.0, base=0, channel_multiplier=1),
         reads=[b_ident], writes=[b_ident])
    ntri = C.sb("ntri", [128, 128], BF16); b_ntri = S.buf("ntri")
    S.op("pool", lambda e: e.memset(ntri[:], -1.0), writes=[b_ntri])
    S.op("pool", lambda e: e.affine_select(ntri[:], ntri[:], [[-1, 128]], ALU.is_ge, 0.0, base=0, channel_multiplier=1),
         reads=[b_ntri], writes=[b_ntri])
    nones = C.sb("nones", [128, 128], BF16); b_nones = S.buf("nones")
    S.op("pool", lambda e: e.memset(nones[:], -1.0), writes=[b_nones])
    dmask = C.sb("dmask", [128, 256], BF16); b_dmask = S.buf("dmask")
    S.op("pool", lambda e: e.memset(dmask[:], 1.0), writes=[b_dmask])
    S.op("pool", lambda e: e.affine_select(dmask[:, 0:128], dmask[:, 0:128], [[-1, 128]], ALU.is_ge, 0.0, base=0, channel_multiplier=1),
         reads=[b_dmask], writes=[b_dmask])
    S.op("pool", lambda e: e.affine_select(dmask[:, 128:256], dmask[:, 128:256], [[1, 128]], ALU.is_ge, 0.0, base=0, channel_multiplier=-1),
         reads=[b_dmask], writes=[b_dmask])
    wm = C.sb("wm", [128, 896], BF16); b_wm = S.buf("wm")
    S.op("pool", lambda e: e.memset(wm[:], 1.0), writes=[b_wm])
    S.op("pool", lambda e: e.affine_select(wm[:], wm[:], [[1, 896]], ALU.is_ge, 0.0, base=-385, channel_multiplier=-1),
         reads=[b_wm], writes=[b_wm])

    lp = C.sb("lp", [64, 8], F32); b_lp = S.buf("lp")
    S.dma("sp", lambda e: e.dma_start(out=lp[:], in_=lru_p[:, :]), b_lp, True)
    lc = C.sb("lc", [64, 4], F32); b_lc = S.buf("lc")
    S.op("act", lambda e: e.activation(lc[:, 2:3], lp[:, 7:8], AF.Exp, scale=-1.0), reads=[b_lp], writes=[b_lc])
    S.op("act", lambda e: e.activation(lc[:, 3:4], lc[:, 2:3], AF.Ln, bias=1.0), reads=[b_lc], writes=[b_lc])
    S.op("act", lambda e: e.mul(lc[:, 0:1], lc[:, 3:4], -8.0), reads=[b_lc], writes=[b_lc])
    S.op("act", lambda e: e.mul(lc[:, 1:2], lc[:, 3:4], -16.0), reads=[b_lc], writes=[b_lc])
    gwf = C.sb("gwf", [64, 128], F32); b_gwf = S.buf("gwf")
    S.dma("sp", lambda e: e.dma_start(out=gwf[:], in_=gw[:, :]), b_gwf, True)
    gwb = C.sb("gwb", [64, 128], BF16); b_gwb = S.buf("gwb")
    S.op("act", lambda e: e.copy(gwb[:], gwf[:]), reads=[b_gwf], writes=[b_gwb])

    rmat = C.sb("rmat", [128, 128], F32); b_rmat = S.buf("rmat")
    S.dma("sp", lambda e: e.dma_start(out=rmat[:], in_=C.rmat_d[:, :]), b_rmat, True)
    qr = C.sb("qr", [128, TILE], F32); b_qr = S.buf("qr")

    W, b_W = win

    dq = C.sb("dq", [128, SEQ], BF16); b_dq = S.buf("dq")
    dk = C.sb("dk", [128, SEQ], BF16); b_dk = S.buf("dk")
    dv = C.sb("dv", [128, SEQ], BF16); b_dv = S.buf("dv")
    dgs = C.sb("dgs", [128, SEQ], BF16); b_dgs = S.buf("dgs")
    T6 = C.sb("T6", [128, SEQ], BF16); b_T6 = S.buf("T6")
    T7 = C.sb("T7", [128, SEQ], BF16); b_T7 = S.buf("T7")
    T8, b_T8 = C.G16, C.b_G16
    svb = C.sb("svb", [128, SEQ // 128, HD], BF16); b_svb = S.buf("svb")

    hnb = [C.sb("hnb%d" % i, [128, 8, TILE], BF16) for i in range(2)]
    b_hnb = S.bufs(2, "hnb")
    ctb = [C.sb("ctb%d" % i, [128, TILE], F32) for i in range(2)]
    b_ctb = S.bufs(2, "ctb")
    stb = [C.sb("stb%d" % i, [128, TILE], F32) for i in range(2)]
    b_stb = S.bufs(2, "stb")
    rt1 = [C.sb("rt1_%d" % i, [128, TILE], F32) for i in range(2)]
    b_rt1 = S.bufs(2, "rt1")
    rt2 = [C.sb("rt2_%d" % i, [128, TILE], F32) for i in range(2)]
    b_rt2 = S.bufs(2, "rt2")
    XT = [C.sb("XT%d" % i, [64, TILE + 3], F32) for i in range(2)]
    b_XT = S.bufs(2, "XT")
    xc = C.sb("xc", [64, TILE], F32); b_xc = S.buf("xc")
    xcb = C.sb("xcb", [64, TILE], BF16); b_xcb = S.buf("xcb")
    lr = C.sb("lr", [64, TILE], F32); b_lr = S.buf("lr")
    li = C.sb("li", [64, TILE], F32); b_li = S.buf("li")
    la = C.sb("la", [64, TILE], F32); b_la = S.buf("la")
    la2 = C.sb("la2", [64, TILE], F32); b_la2 = S.buf("la2")
    lh = [C.sb("lh%d" % i, [64, TILE], F32) for i in range(2)]
    b_lh = S.bufs(2, "lh")
    S.op("pool", lambda e: e.memset(XT[0][:, 0:3], 0.0), writes=[b_XT[0]])

    hnq = hn_ag.rearrange("t q (c p) w -> t q p c w", p=128)

    def lru_and_v(t):
        j = t % 2
        ts = slice(t * TILE, (t + 1) * TILE)
        pb = t % 2
        for q in range(4):
            c0 = t * TILE + q * 128
            S.op("pe", lambda e, q=q, c0=c0, pb=pb: e.transpose(
                psb[:, pb, q * 64:(q + 1) * 64], T8[64:128, c0:c0 + 128], ident[64:128, 64:128]),
                reads=[b_T8, b_ident], writes=[bank[6 + pb]])
        S.op("dve", lambda e, t=t, pb=pb: e.tensor_copy(
            svb[:, t * 4:(t + 1) * 4, :], psb[:, pb, 0:256].rearrange("p (a b) -> p a b", a=4)),
            reads=[bank[6 + pb]], writes=[b_svb])
        X = XT[j]
        if t + 1 < NT:
            S.op("pool", lambda e, j=j: e.tensor_copy(XT[1 - j][:, 0:3], XT[j][:, TILE:TILE + 3]),
                 reads=[b_XT[j]], writes=[b_XT[1 - j]])
        S.op("dve", lambda e, X=X: e.tensor_scalar(xc[:], X[:, 3:TILE + 3], lp[:, 3:4], lp[:, 4:5], ALU.mult, ALU.add),
             reads=[b_XT[j], b_lp], writes=[b_xc])
        for k in range(3):
            S.op("dve", lambda e, X=X, k=k: e.scalar_tensor_tensor(
                xc[:], X[:, k:k + TILE], lp[:, k:k + 1], xc[:], ALU.mult, ALU.add),
                reads=[b_XT[j], b_lp, b_xc], writes=[b_xc])
        S.op("pool", lambda e: e.tensor_copy(xcb[:], xc[:]), reads=[b_xc], writes=[b_xcb])
        S.op("pe", lambda e: e.matmul(ps[0:64, 4, :], gwb[:, 0:64], xcb[:], start=True, stop=True),
             reads=[b_gwb, b_xcb], writes=[bank[4]])
        S.op("pe", lambda e: e.matmul(ps[0:64, 5, :], gwb[:, 64:128], xcb[:], start=True, stop=True),
             reads=[b_gwb, b_xcb], writes=[bank[5]])
        S.op("act", lambda e: e.activation(lr[:], ps[0:64, 4, :], AF.Sigmoid, bias=lp[:, 5:6]),
             reads=[bank[4], b_lp], writes=[b_lr])
        S.op("act", lambda e: e.activation(li[:], ps[0:64, 5, :], AF.Sigmoid, bias=lp[:, 6:7]),
             reads=[bank[5], b_lp], writes=[b_li])
        S.op("act", lambda e: e.activation(la[:], lr[:], AF.Exp, scale=lc[:, 0:1]), reads=[b_lr, b_lc], writes=[b_la])
        S.op("act", lambda e: e.activation(la2[:], lr[:], AF.Exp, scale=lc[:, 1:2]), reads=[b_lr, b_lc], writes=[b_la2])
        S.op("dve", lambda e: e.tensor_scalar_min(la2[:], la2[:], 1.0 - 1e-7), reads=[b_la2], writes=[b_la2])
        S.op("act", lambda e: e.activation(la2[:], la2[:], AF.Ln, bias=1.0, scale=-1.0), reads=[b_la2], writes=[b_la2])
        S.op("act", lambda e: e.activation(la2[:], la2[:], AF.Exp, scale=0.5), reads=[b_la2], writes=[b_la2])
        S.op("dve", lambda e: e.tensor_tensor(li[:], li[:], xc[:], ALU.mult), reads=[b_li, b_xc], writes=[b_li])
        S.op("dve", lambda e: e.tensor_tensor(li[:], li[:], la2[:], ALU.mult), reads=[b_li, b_la2], writes=[b_li])
        if t == 0:
            S.op("dve", lambda e, j=j: e.tensor_tensor_scan(lh[j][:], la[:], li[:], 0.0, ALU.mult, ALU.add),
                 reads=[b_la, b_li], writes=[b_lh[j]])
        else:
            S.op("dve", lambda e, j=j: e.tensor_tensor_scan(lh[j][:], la[:], li[:], lh[1 - j][:, TILE - 1:TILE], ALU.mult, ALU.add),
                 reads=[b_la, b_li, b_lh[1 - j]], writes=[b_lh[j]])
        S.op("pool", lambda e, j=j, ts=ts: e.tensor_tensor(T7[0:64, ts], lh[j][:], T8[0:64, ts], ALU.mult),
             reads=[b_lh[j], b_T8], writes=[b_T7])

    nbk = 0
    for t in range(NT):
        j = t % 2
        ts = slice(t * TILE, (t + 1) * TILE)
        S.dma("sp", lambda e, j=j, t=t: e.dma_start(
            out=hnb[j][:], in_=hnq[t // 4][t % 4]), b_hnb[j], True, dram_r=hn_ag_ds[t // 4])
        S.dma("sp", lambda e, j=j, ts=ts: e.dma_start(out=ctb[j][:], in_=ctab[:, ts]), b_ctb[j], True)
        S.dma("sp", lambda e, j=j, ts=ts: e.dma_start(out=stb[j][:], in_=stab[:, ts]), b_stb[j], True)
        pending = []
        for gi in range(7):
            bk = nbk % 4
            nbk += 1
            for kc in range(8):
                S.op("pe", lambda e, bk=bk, kc=kc, gi=gi, j=j: e.matmul(
                    ps[:, bk, :], W[:, kc, gi * 128:(gi + 1) * 128], hnb[j][:, kc, :],
                    start=(kc == 0), stop=(kc == 7)),
                    reads=[b_W, b_hnb[j]], writes=[bank[bk]])
            for fn in pending:
                fn()
            pending = []
            if gi in (0, 1):
                dst, bdst = (dq, b_dq) if gi == 0 else (dk, b_dk)
                S.op("dve", lambda e, bk=bk, j=j: e.tensor_tensor(rt1[j][:], ps[:, bk, :], ctb[j][:], ALU.mult),
                     reads=[bank[bk], b_ctb[j]], writes=[b_rt1[j]])
                S.op("act", lambda e, bk=bk: e.copy(qr[:], ps[:, bk, :]), reads=[bank[bk]], writes=[b_qr])

                def rot(j=j, ts=ts, dst=dst, bdst=bdst):
                    nonlocal nbk
                    bkr = nbk % 4
                    nbk += 1
                    S.op("pe", lambda e, bkr=bkr: e.matmul(
                        ps[:, bkr, :], rmat[:], qr[:], start=True, stop=True),
                        reads=[b_rmat, b_qr], writes=[bank[bkr]])
                    S.op("dve", lambda e, bkr=bkr, j=j: e.tensor_tensor(rt2[j][:], ps[:, bkr, :], stb[j][:], ALU.mult),
                         reads=[bank[bkr], b_stb[j]], writes=[b_rt2[j]])
                    S.op("pool", lambda e, j=j, ts=ts, dst=dst: e.tensor_tensor(dst[:, ts], rt1[j][:], rt2[j][:], ALU.add),
                         reads=[b_rt1[j], b_rt2[j]], writes=[bdst])
                pending.append(rot)
            elif gi == 2:
                S.op("act", lambda e, bk=bk, ts=ts: e.copy(dv[:, ts], ps[:, bk, :]), reads=[bank[bk]], writes=[b_dv])
            elif gi == 3:
                S.op("act", lambda e, bk=bk, ts=ts: e.activation(dgs[:, ts], ps[:, bk, :], AF.Silu),
                     reads=[bank[bk]], writes=[b_dgs])
            elif gi == 4:
                S.op("act", lambda e, bk=bk, ts=ts: e.activation(T6[0:64, ts], ps[0:64, bk, :], AF.Silu),
                     reads=[bank[bk]], writes=[b_T6])
                S.op("dve", lambda e, bk=bk, ts=ts: e.tensor_scalar_mul(T6[64:128, ts], ps[64:128, bk, :], 0.125),
                     reads=[bank[bk]], writes=[b_T6])
            elif gi == 5:
                S.op("act", lambda e, bk=bk, j=j: e.copy(XT[j][:, 3:TILE + 3], ps[0:64, bk, :]),
                     reads=[bank[bk]], writes=[b_XT[j]])
                S.op("dve", lambda e, bk=bk, ts=ts: e.tensor_copy(T7[64:128, ts], ps[64:128, bk, :]),
                     reads=[bank[bk]], writes=[b_T7])
            elif gi == 6:
                S.op("act", lambda e, bk=bk, ts=ts: e.activation(T8[0:64, ts], ps[0:64, bk, :], AF.Silu),
                     reads=[bank[bk]], writes=[b_T8])
                S.op("dve", lambda e, bk=bk, ts=ts: e.tensor_copy(T8[64:128, ts], ps[64:128, bk, :]),
                     reads=[bank[bk]], writes=[b_T8])
            if gi == 3 and t >= 1:
                lru_and_v(t - 1)
    lru_and_v(NT - 1)

    C.prefetch_wo()

    vaug = [C.sb("vaug%d" % i, [128, 2, 128], BF16) for i in range(3)]
    b_vaug = S.bufs(3, "vaug")
    for i in range(3):
        S.op("pool", lambda e, i=i: e.memset(vaug[i][:, :, 64:128], 1.0), writes=[b_vaug[i]])
    pt = [C.sb("pt%d" % i, [128, 256], BF16) for i in range(2)]
    b_pt = S.bufs(2, "pt")
    rcp_l = [rt2[0], rt1[1]]; b_rcp_l = [S.alias(b_rt2[0], "rcp0"), S.alias(b_rt1[1], "rcp1")]
    nrm_l = [rt2[1], ctb[0]]; b_nrm_l = [S.alias(b_rt2[1], "nrm0"), S.alias(b_ctb[0], "nrm1")]
    SEG = 2048
    acc_banks = [0, 1, 2, 3]
    for seg in range(SEQ // SEG):
        for h in range(2):
            hr = slice(h * 64, (h + 1) * 64)
            qbs = []
            for d in (1, 4, 16):
                nmb = SEG // d // 128
                for r in range(d):
                    for mbl in range(nmb):
                        qbs.append((d, r, seg * nmb + mbl))
            started = [False] * 4

            def cols(d, r, mb):
                s0 = d * 128 * mb + r
                return slice(s0, s0 + d * 127 + 1, d)

            def stage_a(k):
                d, r, mb = qbs[k]
                sb_ = 4 + (k % 2)
                vb = k % 2
                qc = cols(d, r, mb)
                blocks = ([(0, mb - 1)] if mb >= 1 else []) + [(1, mb)]
                for half, kmb in blocks:
                    kc_ = cols(d, r, kmb)
                    S.op("pe", lambda e, sb_=sb_, half=half, kc_=kc_, qc=qc, hr=hr: e.matmul(
                        ps[:, sb_, half * 128:(half + 1) * 128], dk[hr, kc_], dq[hr, qc], start=True, stop=True),
                        reads=[b_dk, b_dq], writes=[bank[sb_]])
                for half, kmb in blocks:
                    kc_ = cols(d, r, kmb)
                    S.op("pe", lambda e, vb=vb, half=half, kc_=kc_, hr=hr: e.transpose(
                        psb[:, vb, half * 64:(half + 1) * 64], dv[hr, kc_], ident[hr, hr]),
                        reads=[b_dv, b_ident], writes=[bank[6 + vb]])
                lo = 0 if mb >= 1 else 1
                v3 = k % 3
                S.op("dve", lambda e, vb=vb, lo=lo, v3=v3: e.tensor_copy(
                    vaug[v3][:, lo:2, 0:64], psb[:, vb, lo * 64:128].rearrange("p (a b) -> p a b", b=64)),
                    reads=[bank[6 + vb]], writes=[b_vaug[v3]])

            def stage_b1(k):
                d, r, mb = qbs[k]
                sb_ = 4 + (k % 2)
                vb = k % 2
                lo = 0 if mb >= 1 else 128
                S.op("act", lambda e, sb_=sb_, vb=vb, lo=lo: e.activation(
                    pt[vb][:, lo:256], ps[:, sb_, lo:256], AF.Exp, scale=0.125),
                    reads=[bank[sb_]], writes=[b_pt[vb]])
                S.op("dve", lambda e, vb=vb, lo=lo: e.tensor_tensor(
                    pt[vb][:, lo:256], pt[vb][:, lo:256], dmask[:, lo:256], ALU.mult),
                    reads=[b_pt[vb], b_dmask], writes=[b_pt[vb]])

            def stage_b2(k):
                d, r, mb = qbs[k]
                vb = k % 2
                halves = ([0] if mb >= 1 else []) + [1]
                t0 = d * 128 * mb + r - seg * SEG
                if d == 16:
                    pieces = [(bkk, slice(r, 512, 16), slice(32 * bkk, 32 * bkk + 32)) for bkk in range(4)]
                else:
                    bkk = t0 // 512
                    o0 = t0 - bkk * 512
                    pieces = [(bkk, slice(o0, o0 + d * 127 + 1, d), slice(0, 128))]
                for half in halves:
                    for bkk, oc, pc in pieces:
                        st_ = not started[bkk]
                        started[bkk] = True
                        S.op("pe", lambda e, vb=vb, half=half, bkk=bkk, oc=oc, pc=pc, st_=st_, v3=k % 3: e.matmul(
                            ps[:, bkk, oc], vaug[v3][:, half, :], pt[vb][:, half * 128 + pc.start:half * 128 + pc.stop],
                            start=st_, stop=False, skip_group_check=True),
                            reads=[b_vaug[k % 3], b_pt[vb]], writes=[bank[bkk]])

            nq_ = len(qbs)
            stage_a(0)
            stage_a(1)
            stage_b1(0)
            for k in range(nq_):
                if k + 2 < nq_:
                    stage_a(k + 2)
                if k + 1 < nq_:
                    stage_b1(k + 1)
                stage_b2(k)
            for bkk in range(4):
                cs = slice(seg * SEG + bkk * 512, seg * SEG + (bkk + 1) * 512)
                ro = slice(0, 64) if h == 0 else slice(64, 128)
                rcp, b_rcp, nrm, b_nrm = rcp_l[bkk % 2], b_rcp_l[bkk % 2], nrm_l[bkk % 2], b_nrm_l[bkk % 2]
                S.op("act", lambda e, bkk=bkk, ro=ro, rcp=rcp: e.activation(rcp[ro, :], ps[64:128, bkk, :], AF.Ln),
                     reads=[bank[bkk]], writes=[b_rcp])
                S.op("act", lambda e, ro=ro, rcp=rcp: e.activation(rcp[ro, :], rcp[ro, :], AF.Exp, scale=-1.0),
                     reads=[b_rcp], writes=[b_rcp])
                S.op("dve", lambda e, bkk=bkk, ro=ro, rcp=rcp, nrm=nrm: e.tensor_tensor(nrm[ro, :], ps[0:64, bkk, :], rcp[ro, :], ALU.mult),
                     reads=[bank[bkk], b_rcp], writes=[b_nrm])
                S.op("pool", lambda e, cs=cs, ro=ro, nrm=nrm: e.tensor_tensor(dgs[ro, cs], nrm[ro, :], dgs[ro, cs], ALU.mult),
                     reads=[b_nrm, b_dgs], writes=[b_dgs])

    hv = [hnb[i][:].rearrange("p a b -> p (a b)") for i in range(2)]
    E = [hv[0][:, 2048 * i:2048 * (i + 1)].bitcast(F32).rearrange("p (a b) -> p a b", a=2) for i in range(2)]
    b_E = [S.alias(b_hnb[0], "sbE%d" % i) for i in range(2)]
    SP = [hv[1][:, 1024 * i:1024 * (i + 1)].rearrange("p (a b) -> p a b", a=2) for i in range(3)]
    b_SP = [S.alias(b_hnb[1], "sbSP%d" % i) for i in range(3)]
    WT = [ctb[i][:].bitcast(BF16).rearrange("p (a b) -> p a b", a=2) for i in range(2)]
    b_WT = [S.alias(b_nrm_l[1] if i == 0 else b_ctb[i], "sbWT%d" % i) for i in range(2)]
    RR = []
    b_RR = []
    for src, bsrc in ((stb[0], b_stb[0]), (stb[1], b_stb[1]), (rt1[0], b_rt1[0])):
        v = src[:].bitcast(BF16)
        for hh in range(2):
            RR.append(v[:, hh * TILE:(hh + 1) * TILE])
            b_RR.append(S.alias(bsrc, "sbR%d" % len(RR)))
    groups = []
    for i in range(NT):
        ng = 2 * i + 2
        for gidx in range(ng):
            ka = 4 * i + 3 - 2 * gidx
            groups.append((i, gidx, ng, ka))
    NG = len(groups)
    rstate = {}

    def zb(k):
        return 2 * (k % 3)

    def sb_z(k):
        i, gidx, ng, ka = groups[k]
        qs = slice(i * TILE, (i + 1) * TILE)
        for u in range(2):
            kb = ka - u
            S.op("pe", lambda e, k=k, u=u, kb=kb, qs=qs: e.matmul(
                ps[:, zb(k) + u, :], T7[64:128, kb * 128:(kb + 1) * 128], T6[64:128, qs], start=True, stop=True),
                reads=[b_T7, b_T6], writes=[bank[zb(k) + u]])

    def sb_a(k):
        i, gidx, ng, ka = groups[k]
        e_ = E[k % 2]; be = b_E[k % 2]
        sp = SP[k % 3]; bsp = b_SP[k % 3]
        z0 = zb(k)
        S.op("act", lambda e: e.activation(e_[:], ps[:, z0:z0 + 2, :], AF.Exp),
             reads=[bank[z0], bank[z0 + 1]], writes=[be])
        S.op("act", lambda e: e.activation(sp[:], e_[:], AF.Ln, bias=1.0), reads=[be], writes=[bsp])
        diag = gidx < 2
        if diag:
            for u in range(2):
                a = (ka - u) - 4 * i
                S.op("dve", lambda e, u=u, a=a: e.tensor_tensor(
                    sp[:, u, :], sp[:, u, :], wm[:, 384 - 128 * a:384 - 128 * a + TILE], ALU.mult),
                    reads=[bsp, b_wm], writes=[bsp])
        if gidx == 0:
            r0 = None
            r1 = (sp[:, 0, :], bsp)
        else:
            r0 = rstate["cur"]
            nr = rstate["n"] % 6
            rstate["n"] += 1
            S.op("pool", lambda e, r0=r0, nr=nr: e.tensor_tensor(RR[nr][:], r0[0], sp[:, 0, :], ALU.add),
                 reads=[r0[1], bsp], writes=[b_RR[nr]])
            r1 = (RR[nr][:], b_RR[nr])
        if gidx + 1 < ng:
            nr2 = rstate["n"] % 6
            rstate["n"] += 1
            S.op("dve", lambda e, r1=r1, nr2=nr2: e.tensor_tensor(RR[nr2][:], r1[0], sp[:, 1, :], ALU.add),
                 reads=[r1[1], bsp], writes=[b_RR[nr2]])
            rstate["cur"] = (RR[nr2][:], b_RR[nr2])
        S.op("pe", lambda e: e.matmul(ps[:, z0, :], ntri[:], sp[:, 0, :], start=False, stop=False, skip_group_check=True),
             reads=[b_ntri, bsp], writes=[bank[z0]])
        if r0 is not None:
            S.op("pe", lambda e, r0=r0: e.matmul(ps[:, z0, :], nones[:], r0[0], start=False, stop=False, skip_group_check=True),
                 reads=[b_nones, r0[1]], writes=[bank[z0]])
        S.op("pe", lambda e: e.matmul(ps[:, z0 + 1, :], ntri[:], sp[:, 1, :], start=False, stop=False, skip_group_check=True),
             reads=[b_ntri, bsp], writes=[bank[z0 + 1]])
        S.op("pe", lambda e, r1=r1: e.matmul(ps[:, z0 + 1, :], nones[:], r1[0], start=False, stop=False, skip_group_check=True),
             reads=[b_nones, r1[1]], writes=[bank[z0 + 1]])

    def sb_b(k):
        i, gidx, ng, ka = groups[k]
        wt = WT[k % 2]; bwt = b_WT[k % 2]
        z0 = zb(k)
        ob = 6 + (i % 2)
        qs = slice(i * TILE, (i + 1) * TILE)
        S.op("act", lambda e: e.activation(wt[:], ps[:, z0:z0 + 2, :], AF.Exp),
             reads=[bank[z0], bank[z0 + 1]], writes=[bwt])
        if gidx < 2:
            for u in range(2):
                a = (ka - u) - 4 * i
                S.op("dve", lambda e, u=u, a=a: e.tensor_tensor(
                    wt[:, u, :], wt[:, u, :], wm[:, 384 - 128 * a:384 - 128 * a + TILE], ALU.mult),
                    reads=[bwt, b_wm], writes=[bwt])
        for u in range(2):
            kb = ka - u
            S.op("pe", lambda e, u=u, kb=kb, ob=ob, first=(gidx == 0 and u == 0), last=(gidx == ng - 1 and u == 1): e.matmul(
                ps[0:64, ob, :], svb[:, kb, :], wt[:, u, :], start=first, stop=last),
                reads=[b_svb, bwt], writes=[bank[ob]])
        if gidx == ng - 1:
            S.op("dve", lambda e, ob=ob, qs=qs: e.tensor_tensor(T6[0:64, qs], ps[0:64, ob, :], T6[0:64, qs], ALU.mult),
                 reads=[bank[ob], b_T6], writes=[b_T6])
            if i % 4 == 3:
                q = i // 4
                cq = slice(q * TQ, (q + 1) * TQ)
                ysv = y_src[q].rearrange("u f w -> f u w")
                S.dma("sp", lambda e, ysv=ysv, cq=cq: e.dma_start(
                    out=ysv[0:128], in_=dgs[:, cq].rearrange("p (u w) -> p u w", u=4)), b_dgs, False, dram_w=y_src_ds[q])
                S.dma("sp", lambda e, ysv=ysv, cq=cq: e.dma_start(
                    out=ysv[128:192], in_=T7[0:64, cq].rearrange("p (u w) -> p u w", u=4)), b_T7, False, dram_w=y_src_ds[q])
                S.dma("sp", lambda e, ysv=ysv, cq=cq: e.dma_start(
                    out=ysv[192:256], in_=T6[0:64, cq].rearrange("p (u w) -> p u w", u=4)), b_T6, False, dram_w=y_src_ds[q])
                if q == NQ - 1:
                    C.deferred_ag = lambda q=q: after_y(q)
                else:
                    after_y(q)

    rstate["n"] = 0
    sb_z(0)
    for k in range(NG):
        if k + 1 < NG:
            sb_z(k + 1)
        sb_a(k)
        if k >= 1:
            sb_b(k - 1)
    sb_b(NG - 1)

    return None


STOP = None


def build_fused():
    nc = bass.Bass("TRN2", target_bir_lowering=False)
    with ExitStack() as gst:
        C = Ctx(nc, gst)
        S = C.S
        xT = C.din("xT", [D, TQ], F32)
        gains = C.din("gains", [128, DEPTH + 1, 8], F32)
        win = [C.din("win%d" % l, [128, 8, NCOL], F32) for l in range(DEPTH)]
        wo = [C.din("wo%d" % l, [128, 8, D], F32) for l in range(DEPTH)]
        ctab = C.din("ctab", [128, SEQ], F32)
        stab = C.din("stab", [128, SEQ], F32)
        lru_p = C.din("lru_p", [64, DEPTH, 8], F32)
        gw = C.din("gw", [64, DEPTH, 128], F32)
        idx = C.din("idx", [128, 32], mybir.dt.int32)
        C.rmat_d = C.din("rmat", [128, 128], F32)
        out = C.dout("out", [D, TQ], F32)
        NTL = TQ // TILE
        hn_src = [C.internal("hn_src%d" % l, [NTL, D, TILE], BF16).ap() for l in range(DEPTH)]
        hn_ag = [C.internal("hn_ag%d" % l, [NTL, NQ, D, TILE], BF16).ap() for l in range(DEPTH)]
        y_src = [C.internal("y_src%d" % l, [NQ, 4, 256, TILE], BF16).ap() for l in range(DEPTH)]
        y_ag = [C.internal("y_ag%d" % l, [NQ, NQ, 4, 256, TILE], BF16).ap() for l in range(DEPTH)]
        x1 = C.internal("x1", [D, TQ], F32).ap()
        d_hn_src = [[S.dram("hn_src%d_%d" % (l, t)) for t in range(NTL)] for l in range(DEPTH)]
        d_hn_ag = [[S.dram("hn_ag%d_%d" % (l, t)) for t in range(NTL)] for l in range(DEPTH)]
        d_y_src = [[S.dram("y_src%d_%d" % (l, q)) for q in range(NQ)] for l in range(DEPTH)]
        d_y_ag = [[S.dram("y_ag%d_%d" % (l, q)) for q in range(NQ)] for l in range(DEPTH)]
        d_x1 = S.dram("x1")
        ps = gst.enter_context(nc.psum_tensor("ps", [128, 8, 512], F32))
        bank = S.bufs(8, "bank")
        for b_ in bank:
            b_.excl = True
        Wg = gst.enter_context(nc.sbuf_tensor("Wg", [128, 8, NCOL], BF16))
        b_Wg = S.buf("Wg")
        wstg = [gst.enter_context(nc.sbuf_tensor("wstg%d" % i_, [128, max(NCOL, D)], F32)) for i_ in range(2)]
        b_wstg = S.bufs(2, "wstg")
        C.G16 = gst.enter_context(nc.sbuf_tensor("G16", [128, SEQ], BF16))
        C.b_G16 = S.buf("G16")
        wob_view = C.G16[:].rearrange("p (k n) -> p k n", k=8)

        def load_W(l):
            for kc in range(8):
                j = kc % 2
                S.dma("sp", lambda e, kc=kc, j=j: e.dma_start(out=wstg[j][:, 0:NCOL], in_=win[l][:, kc, :]), b_wstg[j], True)
                S.op("act", lambda e, kc=kc, j=j: e.copy(Wg[:, kc, :], wstg[j][:, 0:NCOL]), reads=[b_wstg[j]], writes=[b_Wg])

        def load_Wo(l):
            for kc in range(8):
                j = kc % 2
                S.dma("sp", lambda e, kc=kc, j=j: e.dma_start(out=wstg[j][:, 0:D], in_=wo[l][:, kc, :]), b_wstg[j], True)
                S.op("act", lambda e, kc=kc, j=j: e.copy(wob_view[:, kc, :], wstg[j][:, 0:D]),
                     reads=[b_wstg[j]], writes=[C.b_G16])

        idx_sb = gst.enter_context(nc.sbuf_tensor("idx_sb", [128, 32], mybir.dt.int32))
        b_idx = S.buf("idx_sb")
        S.dma("sp", lambda e: e.dma_start(out=idx_sb[:], in_=idx[:, :]), b_idx, True)
        groups = [[0, 1, 2, 3], [4, 5, 6, 7]]

        def allgather(src, dst, dsrc, ddst):
            S.collective(lambda e: e.collective_compute(
                "AllGather", ALU.bypass, replica_groups=groups,
                ins=[src.opt()], outs=[dst.opt()]), dsrc, ddst)

        for l in range(DEPTH + 1):
            if STOP == "start":
                break
            S.barrier()
            final = (l == DEPTH)
            if not final:
                load_W(l)
            with ExitStack() as pst:
                C.st = pst
                x_in, x_in_d = (xT, None) if l <= 1 else (x1, d_x1)
                if final:
                    outr = out.rearrange("(c p) t -> p c t", p=128)
                    hn_tile = lambda t: (outr[:, :, t * TILE:(t + 1) * TILE], None)
                    after = None
                else:
                    hn_tile = lambda t, l=l: (hn_src[l][t].rearrange("(c p) w -> p c w", p=128), d_hn_src[l][t])
                    after = lambda t, l=l: allgather(hn_src[l][t], hn_ag[l][t].rearrange("q f w -> (q f) w"),
                                                     d_hn_src[l][t], d_hn_ag[l][t])
                emit_phase_N(
                    C, ps, bank, x_in, x_in_d,
                    y_ag[l - 1].rearrange("a g u f w -> (a g u f) w") if l >= 1 else None,
                    d_y_ag[l - 1] if l >= 1 else None, idx_sb, b_idx,
                    (wob_view, C.b_G16) if l >= 1 else None, gains[:, l, :], hn_tile, after,
                    x1 if (l >= 1 and not final) else None, d_x1 if (l >= 1 and not final) else None, final)
                S.barrier()
            if final or STOP == "N%d" % l:
                break
            with ExitStack() as pst:
                C.st = pst
                after_y = lambda q, l=l: allgather(y_src[l][q].rearrange("u f w -> (u f) w"),
                                                   y_ag[l][q].rearrange("g u f w -> (g u f) w"),
                                                   d_y_src[l][q], d_y_ag[l][q])
                C.prefetch_wo = lambda l=l: load_Wo(l)
                emit_phase_M(C, ps, bank, hn_ag[l], d_hn_ag[l], (Wg, b_Wg), ctab, stab,
                             lru_p[:, l, :], gw[:, l, :], y_src[l], d_y_src[l], after_y)
                S.barrier()
            if STOP == "M%d" % l:
                break
        S.barrier()
        S.emit()
    return nc


_AQ, _AK, _AV, _AG, _BX, _BG, _CQ, _CK, _CV, _CG = 0, 512, 1024, 1536, 2048, 2304, 2560, 2816, 3072, 3328


def _win_cols(g):
    def rng(base, n):
        return list(range(base, base + n))

    def swap(base):
        cols = []
        for h in range(2):
            hb = base + (2 * g + h) * 64
            cols += rng(hb + 8, 8) + rng(hb, 8) + rng(hb + 16, 48)
        return cols

    cols = []
    cols += rng(_AQ + 128 * g, 128)
    cols += rng(_AK + 128 * g, 128)
    cols += rng(_AV + 128 * g, 128)
    cols += rng(_AG + 128 * g, 128)
    cols += rng(_CG + 64 * g, 64) + rng(_CQ + 64 * g, 64)
    cols += rng(_BX + 64 * g, 64) + rng(_CK + 64 * g, 64)
    cols += rng(_BG + 64 * g, 64) + rng(_CV + 64 * g, 64)
    return np.asarray(cols)


def _wout_rows():
    rows = []
    for g in range(NQ):
        rows += list(range(128 * g, 128 * g + 128))
        rows += list(range(512 + 64 * g, 512 + 64 * g + 64))
        rows += list(range(768 + 64 * g, 768 + 64 * g + 64))
    return np.asarray(rows)


def _rot_matrix():
    r = np.zeros((128, 128), np.float32)
    for h in range(2):
        for d in range(8):
            r[h * 64 + d + 8, h * 64 + d] = 1.0
            r[h * 64 + d, h * 64 + d + 8] = 1.0
    return r


def _rope_tables():
    pos = np.arange(SEQ, dtype=np.float32)
    inv_freq = (np.float64(ROPE_THETA) ** (-np.arange(0, 16, 2, dtype=np.float64) / 16.0)).astype(np.float32)
    ang = (pos[:, None] * inv_freq[None, :]).astype(np.float32).astype(np.float64)
    cos = np.cos(ang).astype(np.float32).T
    sin = np.sin(ang).astype(np.float32).T
    ct = np.ones((128, SEQ), np.float32)
    st = np.zeros((128, SEQ), np.float32)
    for h in range(2):
        ct[h * 64:h * 64 + 8] = cos
        ct[h * 64 + 8:h * 64 + 16] = cos
        st[h * 64:h * 64 + 8] = -sin
        st[h * 64 + 8:h * 64 + 16] = sin
    return ct, st


_PROG = {}


def _kc_layout(w):
    n = w.shape[1]
    return np.ascontiguousarray(w.reshape(8, 128, n).transpose(1, 0, 2))


def kernel(x, norm_gain, w_in, conv_w, conv_b, gate_a_w, gate_a_b, gate_x_w, gate_x_b,
           lru_lambda, w_out, final_gain):
    inp = dict(x=x, norm_gain=norm_gain, w_in=w_in, conv_w=conv_w, conv_b=conv_b, gate_a_w=gate_a_w,
               gate_a_b=gate_a_b, gate_x_w=gate_x_w, gate_x_b=gate_x_b, lru_lambda=lru_lambda,
               w_out=w_out, final_gain=final_gain)
    inp = {k: np.asarray(v, dtype=np.float32) for k, v in inp.items()}
    ct, st = _rope_tables()
    rmat = _rot_matrix()
    rows = _wout_rows()
    gains = np.stack([inp["norm_gain"][l].reshape(8, 128).T for l in range(DEPTH)]
                     + [inp["final_gain"].reshape(8, 128).T], axis=1)
    gains = np.ascontiguousarray(gains.astype(np.float32))
    wo = [_kc_layout(inp["w_out"][l][rows, :]) for l in range(DEPTH)]
    maps = []
    for c in range(8):
        b, i = divmod(c, NQ)
        xs = inp["x"][b].reshape(NQ, NQ, TILE, D)[:, i]
        m = {"xT": np.ascontiguousarray(xs.reshape(TQ, D).T), "gains": gains,
             "ctab": ct, "stab": st, "rmat": rmat}
        cols = _win_cols(i)
        for l in range(DEPTH):
            m["win%d" % l] = _kc_layout(inp["w_in"][l][:, cols])
            m["wo%d" % l] = wo[l]
        lru_p = np.stack([np.stack(
            [inp["conv_w"][l][k, 64 * i:64 * i + 64] for k in range(4)]
            + [inp["conv_b"][l][64 * i:64 * i + 64], inp["gate_a_b"][l][i], inp["gate_x_b"][l][i],
               inp["lru_lambda"][l][64 * i:64 * i + 64]], axis=1) for l in range(DEPTH)], axis=1)
        m["lru_p"] = np.ascontiguousarray(lru_p.astype(np.float32))
        gw = np.stack([np.concatenate([inp["gate_a_w"][l][i], inp["gate_x_w"][l][i]], axis=1)
                       for l in range(DEPTH)], axis=1)
        m["gw"] = np.ascontiguousarray(gw.astype(np.float32))
        idx = np.zeros((128, 32), np.int32)
        p = np.arange(128)
        for kc in range(8):
            g, half = divmod(kc, 2)
            for jj in range(4):
                idx[:, kc * 4 + jj] = ((jj * NQ + g) * 4 + i) * 256 + half * 128 + p
        m["idx"] = idx
        maps.append(m)
    if "nc" not in _PROG:
        _PROG["nc"] = build_fused()
    res = run_bass_kernel_spmd(_PROG["nc"], maps, core_ids=list(range(8))).results
    out = np.empty((B, SEQ, D), np.float32)
    for c in range(8):
        b, i = divmod(c, NQ)
        out[b].reshape(NQ, NQ, TILE, D)[:, i] = res[c]["out"].T.reshape(NQ, TILE, D)
    return out
```

```python
import math
from contextlib import ExitStack

import numpy as np
import ml_dtypes

import concourse.bass as bass
import concourse.mybir as mybir
from concourse.bass_utils import run_bass_kernel_spmd

F32 = mybir.dt.float32
BF16 = mybir.dt.bfloat16
AF = mybir.ActivationFunctionType
ALU = mybir.AluOpType

D = 1024
B = 2
SEQ = 8192
DEPTH = 2
HD = 64
NQ = 4
TQ = SEQ // NQ
TILE = 512
NCOL = 7 * 128
F32R = mybir.dt.float32r
EPS = 1e-6
ROPE_THETA = 500000.0
ENGS = ("sp", "act", "dve", "pool", "pe")


class Buf:
    __slots__ = ("name", "w", "r", "dsem", "dcount", "dma_w", "dma_r", "excl")

    def __init__(self, name):
        self.name = name
        self.excl = False
        self.w = None
        self.r = {}
        self.dsem = None
        self.dcount = 0
        self.dma_w = 0
        self.dma_r = 0


class DramBuf:
    def __init__(self, name):
        self.name = name
        self.events = []


class Sched:
    def __init__(self, nc, stack):
        self.nc = nc
        self.stack = stack
        self.ops = {e: [] for e in ENGS}
        self.count = {e: 0 for e in ENGS}
        self.seen = {e: {f: 0 for f in ENGS} for e in ENGS}
        self.seen_d = {e: {} for e in ENGS}
        self.sem = {}
        for e in ("act", "dve", "pool", "pe"):
            self.sem[e] = stack.enter_context(nc.semaphore("s_" + e))
        self.nbuf = 0
        self.dbufs = []

    def dram(self, name):
        return DramBuf(name)

    def buf(self, name=None):
        self.nbuf += 1
        return Buf(name or ("b%d" % self.nbuf))

    def bufs(self, n, name="b"):
        return [self.buf("%s%d" % (name, i)) for i in range(n)]

    def alias(self, old, name=None):
        b = self.buf(name)
        b.w = old.w
        b.r = dict(old.r)
        b.dsem = old.dsem
        b.dcount = old.dcount
        b.dma_w = old.dma_w
        b.dma_r = max(old.dma_r, 0)
        return b

    def _dsem(self, b):
        if b.dsem is None:
            self.nsem = getattr(self, "nsem", 0) + 1
            b.dsem = self.stack.enter_context(self.nc.semaphore("d%d_%s" % (self.nsem, b.name)))
            self.dbufs.append(b)
        return b.dsem

    def _need(self, eng, waits, dep):
        if dep is None:
            return
        e2, idx = dep
        if e2 == eng and eng == "pe":
            return
        if self.seen[eng][e2] >= idx:
            return
        self.seen[eng][e2] = idx
        waits.append((self.sem[e2], idx))

    def _need_d(self, eng, waits, b, val):
        if val <= 0:
            return
        key = id(b.dsem)
        if self.seen_d[eng].get(key, 0) >= val:
            return
        self.seen_d[eng][key] = val
        waits.append((b.dsem, val))

    def _deps(self, eng, reads, writes):
        waits = []
        for b in reads:
            if b.w is not None:
                self._need(eng, waits, b.w)
            if b.dma_w:
                self._need_d(eng, waits, b, b.dma_w)
            if b.excl:
                for e2, idx in b.r.items():
                    if e2 != eng:
                        self._need(eng, waits, (e2, idx))
        for b in writes:
            if b.w is not None:
                self._need(eng, waits, b.w)
            if b.dma_w:
                self._need_d(eng, waits, b, b.dma_w)
            for e2, idx in b.r.items():
                self._need(eng, waits, (e2, idx))
            if b.dma_r:
                self._need_d(eng, waits, b, b.dma_r)
        return waits

    def op(self, eng, fn, reads=(), writes=()):
        waits = self._deps(eng, reads, writes)
        self.count[eng] += 1
        idx = self.count[eng]
        self.ops[eng].append((waits, fn, (self.sem[eng], 1)))
        for b in reads:
            b.r[eng] = idx
        for b in writes:
            b.w = (eng, idx)
            b.r = {}
            b.dma_w = 0
            b.dma_r = 0
        return idx

    def _need_ev(self, eng, waits, sem, val):
        key = id(sem)
        if self.seen_d[eng].get(key, 0) >= val:
            return
        self.seen_d[eng][key] = val
        waits.append((sem, val))

    def dma(self, eng, fn, sb, sb_is_dst, dram_r=None, dram_w=None, extra_reads=()):
        self._dsem(sb)
        reads = list(extra_reads) + ([] if sb_is_dst else [sb])
        writes = [sb] if sb_is_dst else []
        waits = self._deps(eng, reads, writes)
        if dram_r is not None:
            for dr in (dram_r if isinstance(dram_r, (list, tuple)) else [dram_r]):
                for sem, val in dr.events:
                    self._need_ev(eng, waits, sem, val)
        sb.dcount += 16
        self.ops[eng].append((waits, fn, (sb.dsem, 16)))
        if sb_is_dst:
            sb.w = None
            sb.r = {}
            sb.dma_w = sb.dcount
            sb.dma_r = 0
        else:
            sb.dma_r = sb.dcount
        if dram_w is not None:
            dram_w.events = [(s_, v_) for (s_, v_) in dram_w.events if s_ is not sb.dsem]
            dram_w.events.append((sb.dsem, sb.dcount))

    def collective(self, fn, src, dst):
        waits = []
        for sem, val in src.events:
            self._need_ev("pool", waits, sem, val)
        sem = self.stack.enter_context(self.nc.semaphore("cc_" + dst.name))
        self.ops["pool"].append((waits, fn, (sem, 1)))
        dst.events = [(sem, 1)]

    def barrier(self):
        for eng in ENGS:
            waits = []
            for e2 in ("act", "dve", "pool", "pe"):
                if e2 != eng and self.count[e2] > self.seen[eng][e2]:
                    self.seen[eng][e2] = self.count[e2]
                    waits.append((self.sem[e2], self.count[e2]))
            for b in self.dbufs:
                if b.dcount:
                    self._need_ev(eng, waits, b.dsem, b.dcount)
            if waits:
                self.ops[eng].append((waits, None, None))

    def wait_all_dma(self, eng, bufs):
        waits = []
        seen = set()
        for b in bufs:
            if b.dsem is not None and b.dcount and id(b.dsem) not in seen:
                seen.add(id(b.dsem))
                waits.append((b.dsem, b.dcount))
        self.ops[eng].append((waits, None, None))

    def emit(self):
        nc = self.nc

        def run(engobj, lst):
            for waits, fn, inc in lst:
                for s_, v_ in waits:
                    engobj.wait_ge(s_, v_)
                if fn is not None:
                    fn(engobj).then_inc(inc[0], inc[1])

        with nc.Block() as block:
            if self.ops["sp"]:
                @block.sync
                def _(e):
                    run(e, self.ops["sp"])
            if self.ops["act"]:
                @block.scalar
                def _(e):
                    run(e, self.ops["act"])
            if self.ops["dve"]:
                @block.vector
                def _(e):
                    run(e, self.ops["dve"])
            if self.ops["pool"]:
                @block.gpsimd
                def _(e):
                    run(e, self.ops["pool"])
            if self.ops["pe"]:
                @block.tensor
                def _(e):
                    run(e, self.ops["pe"])


class Ctx:
    def __init__(self, nc, stack):
        self.nc = nc
        self.st = stack
        self.S = Sched(nc, stack)
        self.uid = 0

    def sb(self, name, shape, dt):
        self.uid += 1
        return self.st.enter_context(self.nc.sbuf_tensor("%s_%d" % (name, self.uid), shape, dt))

    def internal(self, name, shape, dt):
        return self.nc.dram_tensor(name, shape, dt)

    def din(self, name, shape, dt):
        return self.nc.dram_tensor(name, shape, dt, kind="ExternalInput").ap()

    def dout(self, name, shape, dt):
        return self.nc.dram_tensor(name, shape, dt, kind="ExternalOutput").ap()


def emit_phase_N(C, ps, bank, x_in, x_in_d, y_ag, y_ag_ds, idx_sb, b_idx, wo, gain, hn_tile, after_store,
                 x_out, x_out_d, final):
    S = C.S
    nt = TQ // TILE
    has_proj = y_ag is not None
    ones = C.sb("n_ones", [128, 128], F32)
    b_ones = S.buf("n_ones")
    S.op("pool", lambda e: e.memset(ones[:], 1.0), writes=[b_ones])
    g_sb = C.sb("n_gain", [128, 8], F32)
    b_g = S.buf("n_gain")
    S.dma("sp", lambda e: e.dma_start(out=g_sb[:], in_=gain), b_g, True)
    if has_proj:
        ysb = C.sb("n_ysb", [128, 8, TQ], BF16)
        b_ysb = [S.buf("n_ysb%d" % jj) for jj in range(TQ // TILE)]
        for jj in range(TQ // TILE):
            for kc in range(8):
                S.dma("pool", lambda e, kc=kc, jj=jj: e.indirect_dma_start(
                    out=ysb[:, kc, jj * TILE:(jj + 1) * TILE], out_offset=None, in_=y_ag[:, :],
                    in_offset=bass.IndirectOffsetOnAxis(ap=idx_sb[:, kc * 4 + jj:kc * 4 + jj + 1], axis=0)),
                    b_ysb[jj], True, dram_r=y_ag_ds[jj], extra_reads=[b_idx])
        wob, b_wob = wo
    xt = [C.sb("n_x%d" % i, [128, 8, TILE], F32) for i in range(2)]
    b_xt = S.bufs(2, "n_x")
    sq = [C.sb("n_sq%d" % i, [128, TILE], F32) for i in range(2)]
    b_sq = S.bufs(2, "n_sq")
    rstd = C.sb("n_rstd", [128, TILE], F32)
    b_rstd = S.buf("n_rstd")
    odt = F32 if final else BF16
    ht = [C.sb("n_h%d" % i, [128, 8, TILE], odt) for i in range(2)]
    b_ht = S.bufs(2, "n_h")
    xTr = x_in.rearrange("(c p) t -> p c t", p=128)
    if x_out is not None:
        xor = x_out.rearrange("(c p) t -> p c t", p=128)
    nb = 0
    for t in range(nt):
        j = t % 2
        ts = slice(t * TILE, (t + 1) * TILE)
        S.dma("sp", lambda e, j=j, ts=ts: e.dma_start(out=xt[j][:], in_=xTr[:, :, ts]), b_xt[j], True, dram_r=x_in_d)
        if has_proj:
            for fc in range(8):
                bk = nb % 4
                nb += 1
                for kc in range(8):
                    S.op("pe", lambda e, bk=bk, kc=kc, fc=fc, ts=ts: e.matmul(
                        ps[:, bk, :], wob[:, kc, fc * 128:(fc + 1) * 128], ysb[:, kc, ts],
                        start=(kc == 0), stop=(kc == 7)),
                        reads=[b_wob, b_ysb[t]], writes=[bank[bk]])
                S.op("dve", lambda e, bk=bk, fc=fc, j=j: e.tensor_tensor(
                    xt[j][:, fc, :], ps[:, bk, :], xt[j][:, fc, :], ALU.add),
                    reads=[bank[bk], b_xt[j]], writes=[b_xt[j]])
            if x_out is not None:
                S.dma("act", lambda e, j=j, ts=ts: e.dma_start(out=xor[:, :, ts], in_=xt[j][:]), b_xt[j], False,
                      dram_w=x_out_d)
        for fc in range(8):
            k = fc % 2
            S.op("act", lambda e, k=k, fc=fc, j=j: e.activation(sq[k][:], xt[j][:, fc, :], AF.Square),
                 reads=[b_xt[j]], writes=[b_sq[k]])
            S.op("pe", lambda e, k=k, fc=fc: e.matmul(ps[:, 4, :], ones[:], sq[k][:], start=(fc == 0), stop=(fc == 7)),
                 reads=[b_ones, b_sq[k]], writes=[bank[4]])
        S.op("act", lambda e: e.activation(rstd[:], ps[:, 4, :], AF.Ln, bias=EPS, scale=1.0 / D),
             reads=[bank[4]], writes=[b_rstd])
        S.op("act", lambda e: e.activation(rstd[:], rstd[:], AF.Exp, scale=-0.5), reads=[b_rstd], writes=[b_rstd])
        for fc in range(8):
            S.op("dve", lambda e, fc=fc, j=j: e.scalar_tensor_tensor(
                ht[j][:, fc, :], xt[j][:, fc, :], g_sb[:, fc:fc + 1], rstd[:], ALU.mult, ALU.mult),
                reads=[b_xt[j], b_g, b_rstd], writes=[b_ht[j]])
        h_ap, h_d = hn_tile(t)
        S.dma("act", lambda e, j=j, h_ap=h_ap: e.dma_start(out=h_ap, in_=ht[j][:]), b_ht[j], False, dram_w=h_d)
        if after_store is not None:
            after_store(t)
    return b_xt + b_ht


def emit_phase_M(C, ps, bank, hn_ag, hn_ag_ds, win, ctab, stab, lru_p, gw, y_src, y_src_ds, after_y):
    S = C.S
    nc = C.nc
    NT = SEQ // TILE
    psb_ = [ps[:, 6, :].bitcast(BF16), ps[:, 7, :].bitcast(BF16)]

    class _PSB:
        def __getitem__(self, idx):
            p, b, c = idx
            return psb_[b][p, c]
    psb = _PSB()

    ident = C.sb("ident", [128, 128], BF16); b_ident = S.buf("ident")
    S.op("pool", lambda e: e.memset(ident[:], 1.0), writes=[b_ident])
    S.op("pool", lambda e: e.affine_select(ident[:], ident[:], [[-1, 128]], ALU.is_equal, 0.0, base=0, channel_multiplier=1),
         reads=[b_ident], writes=[b_ident])
    ntri = C.sb("ntri", [128, 128], BF16); b_ntri = S.buf("ntri")
    S.op("pool", lambda e: e.memset(ntri[:], -1.0), writes=[b_ntri])
    S.op("pool", lambda e: e.affine_select(ntri[:], ntri[:], [[-1, 128]], ALU.is_ge, 0.0, base=0, channel_multiplier=1),
         reads=[b_ntri], writes=[b_ntri])
    nones = C.sb("nones", [128, 128], BF16); b_nones = S.buf("nones")
    S.op("pool", lambda e: e.memset(nones[:], -1.0), writes=[b_nones])
    dmask = C.sb("dmask", [128, 256], BF16); b_dmask = S.buf("dmask")
    S.op("pool", lambda e: e.memset(dmask[:], 1.0), writes=[b_dmask])
    S.op("pool", lambda e: e.affine_select(dmask[:, 0:128], dmask[:, 0:128], [[-1, 128]], ALU.is_ge, 0.0, base=0, channel_multiplier=1),
         reads=[b_dmask], writes=[b_dmask])
    S.op("pool", lambda e: e.affine_select(dmask[:, 128:256], dmask[:, 128:256], [[1, 128]], ALU.is_ge, 0.0, base=0, channel_multiplier=-1),
         reads=[b_dmask], writes=[b_dmask])
    wm = C.sb("wm", [128, 896], BF16); b_wm = S.buf("wm")
    S.op("pool", lambda e: e.memset(wm[:], 1.0), writes=[b_wm])
    S.op("pool", lambda e: e.affine_select(wm[:], wm[:], [[1, 896]], ALU.is_ge, 0.0, base=-385, channel_multiplier=-1),
         reads=[b_wm], writes=[b_wm])

    lp = C.sb("lp", [64, 8], F32); b_lp = S.buf("lp")
    S.dma("sp", lambda e: e.dma_start(out=lp[:], in_=lru_p[:, :]), b_lp, True)
    lc = C.sb("lc", [64, 4], F32); b_lc = S.buf("lc")
    S.op("act", lambda e: e.activation(lc[:, 2:3], lp[:, 7:8], AF.Exp, scale=-1.0), reads=[b_lp], writes=[b_lc])
    S.op("act", lambda e: e.activation(lc[:, 3:4], lc[:, 2:3], AF.Ln, bias=1.0), reads=[b_lc], writes=[b_lc])
    S.op("act", lambda e: e.mul(lc[:, 0:1], lc[:, 3:4], -8.0), reads=[b_lc], writes=[b_lc])
    S.op("act", lambda e: e.mul(lc[:, 1:2], lc[:, 3:4], -16.0), reads=[b_lc], writes=[b_lc])
    gwf = C.sb("gwf", [64, 128], F32); b_gwf = S.buf("gwf")
    S.dma("sp", lambda e: e.dma_start(out=gwf[:], in_=gw[:, :]), b_gwf, True)
    gwb = C.sb("gwb", [64, 128], BF16); b_gwb = S.buf("gwb")
    S.op("act", lambda e: e.copy(gwb[:], gwf[:]), reads=[b_gwf], writes=[b_gwb])

    rmat = C.sb("rmat", [128, 128], F32); b_rmat = S.buf("rmat")
    S.dma("sp", lambda e: e.dma_start(out=rmat[:], in_=C.rmat_d[:, :]), b_rmat, True)
    qr = C.sb("qr", [128, TILE], F32); b_qr = S.buf("qr")

    W, b_W = win

    dq = C.sb("dq", [128, SEQ], BF16); b_dq = S.buf("dq")
    dk = C.sb("dk", [128, SEQ], BF16); b_dk = S.buf("dk")
    dv = C.sb("dv", [128, SEQ], BF16); b_dv = S.buf("dv")
    dgs = C.sb("dgs", [128, SEQ], BF16); b_dgs = S.buf("dgs")
    T6 = C.sb("T6", [128, SEQ], BF16); b_T6 = S.buf("T6")
    T7 = C.sb("T7", [128, SEQ], BF16); b_T7 = S.buf("T7")
    T8, b_T8 = C.G16, C.b_G16
    svb = C.sb("svb", [128, SEQ // 128, HD], BF16); b_svb = S.buf("svb")

    hnb = [C.sb("hnb%d" % i, [128, 8, TILE], BF16) for i in range(2)]
    b_hnb = S.bufs(2, "hnb")
    ctb = [C.sb("ctb%d" % i, [128, TILE], F32) for i in range(2)]
    b_ctb = S.bufs(2, "ctb")
    stb = [C.sb("stb%d" % i, [128, TILE], F32) for i in range(2)]
    b_stb = S.bufs(2, "stb")
    rt1 = [C.sb("rt1_%d" % i, [128, TILE], F32) for i in range(2)]
    b_rt1 = S.bufs(2, "rt1")
    rt2 = [C.sb("rt2_%d" % i, [128, TILE], F32) for i in range(2)]
    b_rt2 = S.bufs(2, "rt2")
    XT = [C.sb("XT%d" % i, [64, TILE + 3], F32) for i in range(2)]
    b_XT = S.bufs(2, "XT")
    xc = C.sb("xc", [64, TILE], F32); b_xc = S.buf("xc")
    xcb = C.sb("xcb", [64, TILE], BF16); b_xcb = S.buf("xcb")
    lr = C.sb("lr", [64, TILE], F32); b_lr = S.buf("lr")
    li = C.sb("li", [64, TILE], F32); b_li = S.buf("li")
    la = C.sb("la", [64, TILE], F32); b_la = S.buf("la")
    la2 = C.sb("la2", [64, TILE], F32); b_la2 = S.buf("la2")
    lh = [C.sb("lh%d" % i, [64, TILE], F32) for i in range(2)]
    b_lh = S.bufs(2, "lh")
    S.op("pool", lambda e: e.memset(XT[0][:, 0:3], 0.0), writes=[b_XT[0]])

    hnq = hn_ag.rearrange("t q (c p) w -> t q p c w", p=128)

    def lru_and_v(t):
        j = t % 2
        ts = slice(t * TILE, (t + 1) * TILE)
        pb = t % 2
        for q in range(4):
            c0 = t * TILE + q * 128
            S.op("pe", lambda e, q=q, c0=c0, pb=pb: e.transpose(
                psb[:, pb, q * 64:(q + 1) * 64], T8[64:128, c0:c0 + 128], ident[64:128, 64:128]),
                reads=[b_T8, b_ident], writes=[bank[6 + pb]])
        S.op("dve", lambda e, t=t, pb=pb: e.tensor_copy(
            svb[:, t * 4:(t + 1) * 4, :], psb[:, pb, 0:256].rearrange("p (a b) -> p a b", a=4)),
            reads=[bank[6 + pb]], writes=[b_svb])
        X = XT[j]
        if t + 1 < NT:
            S.op("pool", lambda e, j=j: e.tensor_copy(XT[1 - j][:, 0:3], XT[j][:, TILE:TILE + 3]),
                 reads=[b_XT[j]], writes=[b_XT[1 - j]])
        S.op("dve", lambda e, X=X: e.tensor_scalar(xc[:], X[:, 3:TILE + 3], lp[:, 3:4], lp[:, 4:5], ALU.mult, ALU.add),
             reads=[b_XT[j], b_lp], writes=[b_xc])
        for k in range(3):
            S.op("dve", lambda e, X=X, k=k: e.scalar_tensor_tensor(
                xc[:], X[:, k:k + TILE], lp[:, k:k + 1], xc[:], ALU.mult, ALU.add),
                reads=[b_XT[j], b_lp, b_xc], writes=[b_xc])
        S.op("pool", lambda e: e.tensor_copy(xcb[:], xc[:]), reads=[b_xc], writes=[b_xcb])
        S.op("pe", lambda e: e.matmul(ps[0:64, 4, :], gwb[:, 0:64], xcb[:], start=True, stop=True),
             reads=[b_gwb, b_xcb], writes=[bank[4]])
        S.op("pe", lambda e: e.matmul(ps[0:64, 5, :], gwb[:, 64:128], xcb[:], start=True, stop=True),
             reads=[b_gwb, b_xcb], writes=[bank[5]])
        S.op("act", lambda e: e.activation(lr[:], ps[0:64, 4, :], AF.Sigmoid, bias=lp[:, 5:6]),
             reads=[bank[4], b_lp], writes=[b_lr])
        S.op("act", lambda e: e.activation(li[:], ps[0:64, 5, :], AF.Sigmoid, bias=lp[:, 6:7]),
             reads=[bank[5], b_lp], writes=[b_li])
        S.op("act", lambda e: e.activation(la[:], lr[:], AF.Exp, scale=lc[:, 0:1]), reads=[b_lr, b_lc], writes=[b_la])
        S.op("act", lambda e: e.activation(la2[:], lr[:], AF.Exp, scale=lc[:, 1:2]), reads=[b_lr, b_lc], writes=[b_la2])
        S.op("dve", lambda e: e.tensor_scalar_min(la2[:], la2[:], 1.0 - 1e-7), reads=[b_la2], writes=[b_la2])
        S.op("act", lambda e: e.activation(la2[:], la2[:], AF.Ln, bias=1.0, scale=-1.0), reads=[b_la2], writes=[b_la2])
        S.op("act", lambda e: e.activation(la2[:], la2[:], AF.Exp, scale=0.5), reads=[b_la2], writes=[b_la2])
        S.op("dve", lambda e: e.tensor_tensor(li[:], li[:], xc[:], ALU.mult), reads=[b_li, b_xc], writes=[b_li])
        S.op("dve", lambda e: e.tensor_tensor(li[:], li[:], la2[:], ALU.mult), reads=[b_li, b_la2], writes=[b_li])
        if t == 0:
            S.op("dve", lambda e, j=j: e.tensor_tensor_scan(lh[j][:], la[:], li[:], 0.0, ALU.mult, ALU.add),
                 reads=[b_la, b_li], writes=[b_lh[j]])
        else:
            S.op("dve", lambda e, j=j: e.tensor_tensor_scan(lh[j][:], la[:], li[:], lh[1 - j][:, TILE - 1:TILE], ALU.mult, ALU.add),
                 reads=[b_la, b_li, b_lh[1 - j]], writes=[b_lh[j]])
        S.op("pool", lambda e, j=j, ts=ts: e.tensor_tensor(T7[0:64, ts], lh[j][:], T8[0:64, ts], ALU.mult),
             reads=[b_lh[j], b_T8], writes=[b_T7])

    nbk = 0
    for t in range(NT):
        j = t % 2
        ts = slice(t * TILE, (t + 1) * TILE)
        S.dma("sp", lambda e, j=j, t=t: e.dma_start(
            out=hnb[j][:], in_=hnq[t // 4][t % 4]), b_hnb[j], True, dram_r=hn_ag_ds[t // 4])
        S.dma("sp", lambda e, j=j, ts=ts: e.dma_start(out=ctb[j][:], in_=ctab[:, ts]), b_ctb[j], True)
        S.dma("sp", lambda e, j=j, ts=ts: e.dma_start(out=stb[j][:], in_=stab[:, ts]), b_stb[j], True)
        pending = []
        for gi in range(7):
            bk = nbk % 4
            nbk += 1
            for kc in range(8):
                S.op("pe", lambda e, bk=bk, kc=kc, gi=gi, j=j: e.matmul(
                    ps[:, bk, :], W[:, kc, gi * 128:(gi + 1) * 128], hnb[j][:, kc, :],
                    start=(kc == 0), stop=(kc == 7)),
                    reads=[b_W, b_hnb[j]], writes=[bank[bk]])
            for fn in pending:
                fn()
            pending = []
            if gi in (0, 1):
                dst, bdst = (dq, b_dq) if gi == 0 else (dk, b_dk)
                S.op("dve", lambda e, bk=bk, j=j: e.tensor_tensor(rt1[j][:], ps[:, bk, :], ctb[j][:], ALU.mult),
                     reads=[bank[bk], b_ctb[j]], writes=[b_rt1[j]])
                S.op("act", lambda e, bk=bk: e.copy(qr[:], ps[:, bk, :]), reads=[bank[bk]], writes=[b_qr])

                def rot(j=j, ts=ts, dst=dst, bdst=bdst):
                    nonlocal nbk
                    bkr = nbk % 4
                    nbk += 1
                    S.op("pe", lambda e, bkr=bkr: e.matmul(
                        ps[:, bkr, :], rmat[:], qr[:], start=True, stop=True),
                        reads=[b_rmat, b_qr], writes=[bank[bkr]])
                    S.op("dve", lambda e, bkr=bkr, j=j: e.tensor_tensor(rt2[j][:], ps[:, bkr, :], stb[j][:], ALU.mult),
                         reads=[bank[bkr], b_stb[j]], writes=[b_rt2[j]])
                    S.op("pool", lambda e, j=j, ts=ts, dst=dst: e.tensor_tensor(dst[:, ts], rt1[j][:], rt2[j][:], ALU.add),
                         reads=[b_rt1[j], b_rt2[j]], writes=[bdst])
                pending.append(rot)
            elif gi == 2:
                S.op("act", lambda e, bk=bk, ts=ts: e.copy(dv[:, ts], ps[:, bk, :]), reads=[bank[bk]], writes=[b_dv])
            elif gi == 3:
                S.op("act", lambda e, bk=bk, ts=ts: e.activation(dgs[:, ts], ps[:, bk, :], AF.Silu),
                     reads=[bank[bk]], writes=[b_dgs])
            elif gi == 4:
                S.op("act", lambda e, bk=bk, ts=ts: e.activation(T6[0:64, ts], ps[0:64, bk, :], AF.Silu),
                     reads=[bank[bk]], writes=[b_T6])
                S.op("dve", lambda e, bk=bk, ts=ts: e.tensor_scalar_mul(T6[64:128, ts], ps[64:128, bk, :], 0.125),
                     reads=[bank[bk]], writes=[b_T6])
            elif gi == 5:
                S.op("act", lambda e, bk=bk, j=j: e.copy(XT[j][:, 3:TILE + 3], ps[0:64, bk, :]),
                     reads=[bank[bk]], writes=[b_XT[j]])
                S.op("dve", lambda e, bk=bk, ts=ts: e.tensor_copy(T7[64:128, ts], ps[64:128, bk, :]),
                     reads=[bank[bk]], writes=[b_T7])
            elif gi == 6:
                S.op("act", lambda e, bk=bk, ts=ts: e.activation(T8[0:64, ts], ps[0:64, bk, :], AF.Silu),
                     reads=[bank[bk]], writes=[b_T8])
                S.op("dve", lambda e, bk=bk, ts=ts: e.tensor_copy(T8[64:128, ts], ps[64:128, bk, :]),
                     reads=[bank[bk]], writes=[b_T8])
            if gi == 3 and t >= 1:
                lru_and_v(t - 1)
    lru_and_v(NT - 1)

    C.prefetch_wo()

    vaug = [C.sb("vaug%d" % i, [128, 2, 128], BF16) for i in range(3)]
    b_vaug = S.bufs(3, "vaug")
    for i in range(3):
        S.op("pool", lambda e, i=i: e.memset(vaug[i][:, :, 64:128], 1.0), writes=[b_vaug[i]])
    pt = [C.sb("pt%d" % i, [128, 256], BF16) for i in range(2)]
    b_pt = S.bufs(2, "pt")
    rcp_l = [rt2[0], rt1[1]]; b_rcp_l = [S.alias(b_rt2[0], "rcp0"), S.alias(b_rt1[1], "rcp1")]
    nrm_l = [rt2[1], ctb[0]]; b_nrm_l = [S.alias(b_rt2[1], "nrm0"), S.alias(b_ctb[0], "nrm1")]
    SEG = 2048
    acc_banks = [0, 1, 2, 3]
    for seg in range(SEQ // SEG):
        for h in range(2):
            hr = slice(h * 64, (h + 1) * 64)
            qbs = []
            for d in (1, 4, 16):
                nmb = SEG // d // 128
                for r in range(d):
                    for mbl in range(nmb):
                        qbs.append((d, r, seg * nmb + mbl))
            started = [False] * 4

            def cols(d, r, mb):
                s0 = d * 128 * mb + r
                return slice(s0, s0 + d * 127 + 1, d)

            def stage_a(k):
                d, r, mb = qbs[k]
                sb_ = 4 + (k % 2)
                vb = k % 2
                qc = cols(d, r, mb)
                blocks = ([(0, mb - 1)] if mb >= 1 else []) + [(1, mb)]
                for half, kmb in blocks:
                    kc_ = cols(d, r, kmb)
                    S.op("pe", lambda e, sb_=sb_, half=half, kc_=kc_, qc=qc, hr=hr: e.matmul(
                        ps[:, sb_, half * 128:(half + 1) * 128], dk[hr, kc_], dq[hr, qc], start=True, stop=True),
                        reads=[b_dk, b_dq], writes=[bank[sb_]])
                for half, kmb in blocks:
                    kc_ = cols(d, r, kmb)
                    S.op("pe", lambda e, vb=vb, half=half, kc_=kc_, hr=hr: e.transpose(
                        psb[:, vb, half * 64:(half + 1) * 64], dv[hr, kc_], ident[hr, hr]),
                        reads=[b_dv, b_ident], writes=[bank[6 + vb]])
                lo = 0 if mb >= 1 else 1
                v3 = k % 3
                S.op("dve", lambda e, vb=vb, lo=lo, v3=v3: e.tensor_copy(
                    vaug[v3][:, lo:2, 0:64], psb[:, vb, lo * 64:128].rearrange("p (a b) -> p a b", b=64)),
                    reads=[bank[6 + vb]], writes=[b_vaug[v3]])

            def stage_b1(k):
                d, r, mb = qbs[k]
                sb_ = 4 + (k % 2)
                vb = k % 2
                lo = 0 if mb >= 1 else 128
                S.op("act", lambda e, sb_=sb_, vb=vb, lo=lo: e.activation(
                    pt[vb][:, lo:256], ps[:, sb_, lo:256], AF.Exp, scale=0.125),
                    reads=[bank[sb_]], writes=[b_pt[vb]])
                S.op("dve", lambda e, vb=vb, lo=lo: e.tensor_tensor(
                    pt[vb][:, lo:256], pt[vb][:, lo:256], dmask[:, lo:256], ALU.mult),
                    reads=[b_pt[vb], b_dmask], writes=[b_pt[vb]])

            def stage_b2(k):
                d, r, mb = qbs[k]
                vb = k % 2
                halves = ([0] if mb >= 1 else []) + [1]
                t0 = d * 128 * mb + r - seg * SEG
                if d == 16:
                    pieces = [(bkk, slice(r, 512, 16), slice(32 * bkk, 32 * bkk + 32)) for bkk in range(4)]
                else:
                    bkk = t0 // 512
                    o0 = t0 - bkk * 512
                    pieces = [(bkk, slice(o0, o0 + d * 127 + 1, d), slice(0, 128))]
                for half in halves:
                    for bkk, oc, pc in pieces:
                        st_ = not started[bkk]
                        started[bkk] = True
                        S.op("pe", lambda e, vb=vb, half=half, bkk=bkk, oc=oc, pc=pc, st_=st_, v3=k % 3: e.matmul(
                            ps[:, bkk, oc], vaug[v3][:, half, :], pt[vb][:, half * 128 + pc.start:half * 128 + pc.stop],
                            start=st_, stop=False, skip_group_check=True),
                            reads=[b_vaug[k % 3], b_pt[vb]], writes=[bank[bkk]])

            nq_ = len(qbs)
            stage_a(0)
            stage_a(1)
            stage_b1(0)
            for k in range(nq_):
                if k + 2 < nq_:
                    stage_a(k + 2)
                if k + 1 < nq_:
                    stage_b1(k + 1)
                stage_b2(k)
            for bkk in range(4):
                cs = slice(seg * SEG + bkk * 512, seg * SEG + (bkk + 1) * 512)
                ro = slice(0, 64) if h == 0 else slice(64, 128)
                rcp, b_rcp, nrm, b_nrm = rcp_l[bkk % 2], b_rcp_l[bkk % 2], nrm_l[bkk % 2], b_nrm_l[bkk % 2]
                S.op("act", lambda e, bkk=bkk, ro=ro, rcp=rcp: e.activation(rcp[ro, :], ps[64:128, bkk, :], AF.Ln),
                     reads=[bank[bkk]], writes=[b_rcp])
                S.op("act", lambda e, ro=ro, rcp=rcp: e.activation(rcp[ro, :], rcp[ro, :], AF.Exp, scale=-1.0),
                     reads=[b_rcp], writes=[b_rcp])
                S.op("dve", lambda e, bkk=bkk, ro=ro, rcp=rcp, nrm=nrm: e.tensor_tensor(nrm[ro, :], ps[0:64, bkk, :], rcp[ro, :], ALU.mult),
                     reads=[bank[bkk], b_rcp], writes=[b_nrm])
                S.op("pool", lambda e, cs=cs, ro=ro, nrm=nrm: e.tensor_tensor(dgs[ro, cs], nrm[ro, :], dgs[ro, cs], ALU.mult),
                     reads=[b_nrm, b_dgs], writes=[b_dgs])

    hv = [hnb[i][:].rearrange("p a b -> p (a b)") for i in range(2)]
    E = [hv[0][:, 2048 * i:2048 * (i + 1)].bitcast(F32).rearrange("p (a b) -> p a b", a=2) for i in range(2)]
    b_E = [S.alias(b_hnb[0], "sbE%d" % i) for i in range(2)]
    SP = [hv[1][:, 1024 * i:1024 * (i + 1)].rearrange("p (a b) -> p a b", a=2) for i in range(3)]
    b_SP = [S.alias(b_hnb[1], "sbSP%d" % i) for i in range(3)]
    WT = [ctb[i][:].bitcast(BF16).rearrange("p (a b) -> p a b", a=2) for i in range(2)]
    b_WT = [S.alias(b_nrm_l[1] if i == 0 else b_ctb[i], "sbWT%d" % i) for i in range(2)]
    RR = []
    b_RR = []
    for src, bsrc in ((stb[0], b_stb[0]), (stb[1], b_stb[1]), (rt1[0], b_rt1[0])):
        v = src[:].bitcast(BF16)
        for hh in range(2):
            RR.append(v[:, hh * TILE:(hh + 1) * TILE])
            b_RR.append(S.alias(bsrc, "sbR%d" % len(RR)))
    groups = []
    for i in range(NT):
        ng = 2 * i + 2
        for gidx in range(ng):
            ka = 4 * i + 3 - 2 * gidx
            groups.append((i, gidx, ng, ka))
    NG = len(groups)
    rstate = {}

    def zb(k):
        return 2 * (k % 3)

    def sb_z(k):
        i, gidx, ng, ka = groups[k]
        qs = slice(i * TILE, (i + 1) * TILE)
        for u in range(2):
            kb = ka - u
            S.op("pe", lambda e, k=k, u=u, kb=kb, qs=qs: e.matmul(
                ps[:, zb(k) + u, :], T7[64:128, kb * 128:(kb + 1) * 128], T6[64:128, qs], start=True, stop=True),
                reads=[b_T7, b_T6], writes=[bank[zb(k) + u]])

    def sb_a(k):
        i, gidx, ng, ka = groups[k]
        e_ = E[k % 2]; be = b_E[k % 2]
        sp = SP[k % 3]; bsp = b_SP[k % 3]
        z0 = zb(k)
        S.op("act", lambda e: e.activation(e_[:], ps[:, z0:z0 + 2, :], AF.Exp),
             reads=[bank[z0], bank[z0 + 1]], writes=[be])
        S.op("act", lambda e: e.activation(sp[:], e_[:], AF.Ln, bias=1.0), reads=[be], writes=[bsp])
        diag = gidx < 2
        if diag:
            for u in range(2):
                a = (ka - u) - 4 * i
                S.op("dve", lambda e, u=u, a=a: e.tensor_tensor(
                    sp[:, u, :], sp[:, u, :], wm[:, 384 - 128 * a:384 - 128 * a + TILE], ALU.mult),
                    reads=[bsp, b_wm], writes=[bsp])
        if gidx == 0:
            r0 = None
            r1 = (sp[:, 0, :], bsp)
        else:
            r0 = rstate["cur"]
            nr = rstate["n"] % 6
            rstate["n"] += 1
            S.op("pool", lambda e, r0=r0, nr=nr: e.tensor_tensor(RR[nr][:], r0[0], sp[:, 0, :], ALU.add),
                 reads=[r0[1], bsp], writes=[b_RR[nr]])
            r1 = (RR[nr][:], b_RR[nr])
        if gidx + 1 < ng:
            nr2 = rstate["n"] % 6
            rstate["n"] += 1
            S.op("dve", lambda e, r1=r1, nr2=nr2: e.tensor_tensor(RR[nr2][:], r1[0], sp[:, 1, :], ALU.add),
                 reads=[r1[1], bsp], writes=[b_RR[nr2]])
            rstate["cur"] = (RR[nr2][:], b_RR[nr2])
        S.op("pe", lambda e: e.matmul(ps[:, z0, :], ntri[:], sp[:, 0, :], start=False, stop=False, skip_group_check=True),
             reads=[b_ntri, bsp], writes=[bank[z0]])
        if r0 is not None:
            S.op("pe", lambda e, r0=r0: e.matmul(ps[:, z0, :], nones[:], r0[0], start=False, stop=False, skip_group_check=True),
                 reads=[b_nones, r0[1]], writes=[bank[z0]])
        S.op("pe", lambda e: e.matmul(ps[:, z0 + 1, :], ntri[:], sp[:, 1, :], start=False, stop=False, skip_group_check=True),
             reads=[b_ntri, bsp], writes=[bank[z0 + 1]])
        S.op("pe", lambda e, r1=r1: e.matmul(ps[:, z0 + 1, :], nones[:], r1[0], start=False, stop=False, skip_group_check=True),
             reads=[b_nones, r1[1]], writes=[bank[z0 + 1]])

    def sb_b(k):
        i, gidx, ng, ka = groups[k]
        wt = WT[k % 2]; bwt = b_WT[k % 2]
        z0 = zb(k)
        ob = 6 + (i % 2)
        qs = slice(i * TILE, (i + 1) * TILE)
        S.op("act", lambda e: e.activation(wt[:], ps[:, z0:z0 + 2, :], AF.Exp),
             reads=[bank[z0], bank[z0 + 1]], writes=[bwt])
        if gidx < 2:
            for u in range(2):
                a = (ka - u) - 4 * i
                S.op("dve", lambda e, u=u, a=a: e.tensor_tensor(
                    wt[:, u, :], wt[:, u, :], wm[:, 384 - 128 * a:384 - 128 * a + TILE], ALU.mult),
                    reads=[bwt, b_wm], writes=[bwt])
        for u in range(2):
            kb = ka - u
            S.op("pe", lambda e, u=u, kb=kb, ob=ob, first=(gidx == 0 and u == 0), last=(gidx == ng - 1 and u == 1): e.matmul(
                ps[0:64, ob, :], svb[:, kb, :], wt[:, u, :], start=first, stop=last),
                reads=[b_svb, bwt], writes=[bank[ob]])
        if gidx == ng - 1:
            S.op("dve", lambda e, ob=ob, qs=qs: e.tensor_tensor(T6[0:64, qs], ps[0:64, ob, :], T6[0:64, qs], ALU.mult),
                 reads=[bank[ob], b_T6], writes=[b_T6])
            if i % 4 == 3:
                q = i // 4
                cq = slice(q * TQ, (q + 1) * TQ)
                ysv = y_src[q].rearrange("u f w -> f u w")
                S.dma("sp", lambda e, ysv=ysv, cq=cq: e.dma_start(
                    out=ysv[0:128], in_=dgs[:, cq].rearrange("p (u w) -> p u w", u=4)), b_dgs, False, dram_w=y_src_ds[q])
                S.dma("sp", lambda e, ysv=ysv, cq=cq: e.dma_start(
                    out=ysv[128:192], in_=T7[0:64, cq].rearrange("p (u w) -> p u w", u=4)), b_T7, False, dram_w=y_src_ds[q])
                S.dma("sp", lambda e, ysv=ysv, cq=cq: e.dma_start(
                    out=ysv[192:256], in_=T6[0:64, cq].rearrange("p (u w) -> p u w", u=4)), b_T6, False, dram_w=y_src_ds[q])
                after_y(q)

    rstate["n"] = 0
    sb_z(0)
    for k in range(NG):
        if k + 1 < NG:
            sb_z(k + 1)
        sb_a(k)
        if k >= 1:
            sb_b(k - 1)
    sb_b(NG - 1)

    return None


STOP = None


def build_fused():
    nc = bass.Bass("TRN2", target_bir_lowering=False)
    with ExitStack() as gst:
        C = Ctx(nc, gst)
        S = C.S
        xT = C.din("xT", [D, TQ], F32)
        gains = C.din("gains", [128, DEPTH + 1, 8], F32)
        win = [C.din("win%d" % l, [128, 8, NCOL], F32) for l in range(DEPTH)]
        wo = [C.din("wo%d" % l, [128, 8, D], F32) for l in range(DEPTH)]
        ctab = C.din("ctab", [128, SEQ], F32)
        stab = C.din("stab", [128, SEQ], F32)
        lru_p = C.din("lru_p", [64, DEPTH, 8], F32)
        gw = C.din("gw", [64, DEPTH, 128], F32)
        idx = C.din("idx", [128, 32], mybir.dt.int32)
        C.rmat_d = C.din("rmat", [128, 128], F32)
        out = C.dout("out", [D, TQ], F32)
        NTL = TQ // TILE
        hn_src = [C.internal("hn_src%d" % l, [NTL, D, TILE], BF16).ap() for l in range(DEPTH)]
        hn_ag = [C.internal("hn_ag%d" % l, [NTL, NQ, D, TILE], BF16).ap() for l in range(DEPTH)]
        y_src = [C.internal("y_src%d" % l, [NQ, 4, 256, TILE], BF16).ap() for l in range(DEPTH)]
        y_ag = [C.internal("y_ag%d" % l, [NQ, NQ, 4, 256, TILE], BF16).ap() for l in range(DEPTH)]
        x1 = C.internal("x1", [D, TQ], F32).ap()
        d_hn_src = [[S.dram("hn_src%d_%d" % (l, t)) for t in range(NTL)] for l in range(DEPTH)]
        d_hn_ag = [[S.dram("hn_ag%d_%d" % (l, t)) for t in range(NTL)] for l in range(DEPTH)]
        d_y_src = [[S.dram("y_src%d_%d" % (l, q)) for q in range(NQ)] for l in range(DEPTH)]
        d_y_ag = [[S.dram("y_ag%d_%d" % (l, q)) for q in range(NQ)] for l in range(DEPTH)]
        d_x1 = S.dram("x1")
        ps = gst.enter_context(nc.psum_tensor("ps", [128, 8, 512], F32))
        bank = S.bufs(8, "bank")
        for b_ in bank:
            b_.excl = True
        Wg = gst.enter_context(nc.sbuf_tensor("Wg", [128, 8, NCOL], BF16))
        b_Wg = S.buf("Wg")
        wstg = [gst.enter_context(nc.sbuf_tensor("wstg%d" % i_, [128, max(NCOL, D)], F32)) for i_ in range(2)]
        b_wstg = S.bufs(2, "wstg")
        C.G16 = gst.enter_context(nc.sbuf_tensor("G16", [128, SEQ], BF16))
        C.b_G16 = S.buf("G16")
        wob_view = C.G16[:].rearrange("p (k n) -> p k n", k=8)

        def load_W(l):
            for kc in range(8):
                j = kc % 2
                S.dma("sp", lambda e, kc=kc, j=j: e.dma_start(out=wstg[j][:, 0:NCOL], in_=win[l][:, kc, :]), b_wstg[j], True)
                S.op("act", lambda e, kc=kc, j=j: e.copy(Wg[:, kc, :], wstg[j][:, 0:NCOL]), reads=[b_wstg[j]], writes=[b_Wg])

        def load_Wo(l):
            for kc in range(8):
                j = kc % 2
                S.dma("sp", lambda e, kc=kc, j=j: e.dma_start(out=wstg[j][:, 0:D], in_=wo[l][:, kc, :]), b_wstg[j], True)
                S.op("act", lambda e, kc=kc, j=j: e.copy(wob_view[:, kc, :], wstg[j][:, 0:D]),
                     reads=[b_wstg[j]], writes=[C.b_G16])

        idx_sb = gst.enter_context(nc.sbuf_tensor("idx_sb", [128, 32], mybir.dt.int32))
        b_idx = S.buf("idx_sb")
        S.dma("sp", lambda e: e.dma_start(out=idx_sb[:], in_=idx[:, :]), b_idx, True)
        groups = [[0, 1, 2, 3], [4, 5, 6, 7]]

        def allgather(src, dst, dsrc, ddst):
            S.collective(lambda e: e.collective_compute(
                "AllGather", ALU.bypass, replica_groups=groups,
                ins=[src.opt()], outs=[dst.opt()], dma_qos="P2"), dsrc, ddst)

        for l in range(DEPTH + 1):
            if STOP == "start":
                break
            S.barrier()
            final = (l == DEPTH)
            if not final:
                load_W(l)
            with ExitStack() as pst:
                C.st = pst
                x_in, x_in_d = (xT, None) if l <= 1 else (x1, d_x1)
                if final:
                    outr = out.rearrange("(c p) t -> p c t", p=128)
                    hn_tile = lambda t: (outr[:, :, t * TILE:(t + 1) * TILE], None)
                    after = None
                else:
                    hn_tile = lambda t, l=l: (hn_src[l][t].rearrange("(c p) w -> p c w", p=128), d_hn_src[l][t])
                    after = lambda t, l=l: allgather(hn_src[l][t], hn_ag[l][t].rearrange("q f w -> (q f) w"),
                                                     d_hn_src[l][t], d_hn_ag[l][t])
                emit_phase_N(
                    C, ps, bank, x_in, x_in_d,
                    y_ag[l - 1].rearrange("a g u f w -> (a g u f) w") if l >= 1 else None,
                    d_y_ag[l - 1] if l >= 1 else None, idx_sb, b_idx,
                    (wob_view, C.b_G16) if l >= 1 else None, gains[:, l, :], hn_tile, after,
                    x1 if (l >= 1 and not final) else None, d_x1 if (l >= 1 and not final) else None, final)
                S.barrier()
            if final or STOP == "N%d" % l:
                break
            with ExitStack() as pst:
                C.st = pst
                after_y = lambda q, l=l: allgather(y_src[l][q].rearrange("u f w -> (u f) w"),
                                                   y_ag[l][q].rearrange("g u f w -> (g u f) w"),
                                                   d_y_src[l][q], d_y_ag[l][q])
                C.prefetch_wo = lambda l=l: load_Wo(l)
                emit_phase_M(C, ps, bank, hn_ag[l], d_hn_ag[l], (Wg, b_Wg), ctab, stab,
                             lru_p[:, l, :], gw[:, l, :], y_src[l], d_y_src[l], after_y)
                S.barrier()
            if STOP == "M%d" % l:
                break
        S.barrier()
        S.emit()
    return nc


_AQ, _AK, _AV, _AG, _BX, _BG, _CQ, _CK, _CV, _CG = 0, 512, 1024, 1536, 2048, 2304, 2560, 2816, 3072, 3328


def _win_cols(g):
    def rng(base, n):
        return list(range(base, base + n))

    def swap(base):
        cols = []
        for h in range(2):
            hb = base + (2 * g + h) * 64
            cols += rng(hb + 8, 8) + rng(hb, 8) + rng(hb + 16, 48)
        return cols

    cols = []
    cols += rng(_AQ + 128 * g, 128)
    cols += rng(_AK + 128 * g, 128)
    cols += rng(_AV + 128 * g, 128)
    cols += rng(_AG + 128 * g, 128)
    cols += rng(_CG + 64 * g, 64) + rng(_CQ + 64 * g, 64)
    cols += rng(_BX + 64 * g, 64) + rng(_CK + 64 * g, 64)
    cols += rng(_BG + 64 * g, 64) + rng(_CV + 64 * g, 64)
    return np.asarray(cols)


def _wout_rows():
    rows = []
    for g in range(NQ):
        rows += list(range(128 * g, 128 * g + 128))
        rows += list(range(512 + 64 * g, 512 + 64 * g + 64))
        rows += list(range(768 + 64 * g, 768 + 64 * g + 64))
    return np.asarray(rows)


def _rot_matrix():
    r = np.zeros((128, 128), np.float32)
    for h in range(2):
        for d in range(8):
            r[h * 64 + d + 8, h * 64 + d] = 1.0
            r[h * 64 + d, h * 64 + d + 8] = 1.0
    return r


def _rope_tables():
    pos = np.arange(SEQ, dtype=np.float32)
    inv_freq = (np.float64(ROPE_THETA) ** (-np.arange(0, 16, 2, dtype=np.float64) / 16.0)).astype(np.float32)
    ang = (pos[:, None] * inv_freq[None, :]).astype(np.float32).astype(np.float64)
    cos = np.cos(ang).astype(np.float32).T
    sin = np.sin(ang).astype(np.float32).T
    ct = np.ones((128, SEQ), np.float32)
    st = np.zeros((128, SEQ), np.float32)
    for h in range(2):
        ct[h * 64:h * 64 + 8] = cos
        ct[h * 64 + 8:h * 64 + 16] = cos
        st[h * 64:h * 64 + 8] = -sin
        st[h * 64 + 8:h * 64 + 16] = sin
    return ct, st


_PROG = {}


def _kc_layout(w):
    n = w.shape[1]
    return np.ascontiguousarray(w.reshape(8, 128, n).transpose(1, 0, 2))


def kernel(x, norm_gain, w_in, conv_w, conv_b, gate_a_w, gate_a_b, gate_x_w, gate_x_b,
           lru_lambda, w_out, final_gain):
    inp = dict(x=x, norm_gain=norm_gain, w_in=w_in, conv_w=conv_w, conv_b=conv_b, gate_a_w=gate_a_w,
               gate_a_b=gate_a_b, gate_x_w=gate_x_w, gate_x_b=gate_x_b, lru_lambda=lru_lambda,
               w_out=w_out, final_gain=final_gain)
    inp = {k: np.asarray(v, dtype=np.float32) for k, v in inp.items()}
    ct, st = _rope_tables()
    rmat = _rot_matrix()
    rows = _wout_rows()
    gains = np.stack([inp["norm_gain"][l].reshape(8, 128).T for l in range(DEPTH)]
                     + [inp["final_gain"].reshape(8, 128).T], axis=1)
    gains = np.ascontiguousarray(gains.astype(np.float32))
    wo = [_kc_layout(inp["w_out"][l][rows, :]) for l in range(DEPTH)]
    maps = []
    for c in range(8):
        b, i = divmod(c, NQ)
        xs = inp["x"][b].reshape(NQ, NQ, TILE, D)[:, i]
        m = {"xT": np.ascontiguousarray(xs.reshape(TQ, D).T), "gains": gains,
             "ctab": ct, "stab": st, "rmat": rmat}
        cols = _win_cols(i)
        for l in range(DEPTH):
            m["win%d" % l] = _kc_layout(inp["w_in"][l][:, cols])
            m["wo%d" % l] = wo[l]
        lru_p = np.stack([np.stack(
            [inp["conv_w"][l][k, 64 * i:64 * i + 64] for k in range(4)]
            + [inp["conv_b"][l][64 * i:64 * i + 64], inp["gate_a_b"][l][i], inp["gate_x_b"][l][i],
               inp["lru_lambda"][l][64 * i:64 * i + 64]], axis=1) for l in range(DEPTH)], axis=1)
        m["lru_p"] = np.ascontiguousarray(lru_p.astype(np.float32))
        gw = np.stack([np.concatenate([inp["gate_a_w"][l][i], inp["gate_x_w"][l][i]], axis=1)
                       for l in range(DEPTH)], axis=1)
        m["gw"] = np.ascontiguousarray(gw.astype(np.float32))
        idx = np.zeros((128, 32), np.int32)
        p = np.arange(128)
        for kc in range(8):
            g, half = divmod(kc, 2)
            for jj in range(4):
                idx[:, kc * 4 + jj] = ((jj * NQ + g) * 4 + i) * 256 + half * 128 + p
        m["idx"] = idx
        maps.append(m)
    if "nc" not in _PROG:
        _PROG["nc"] = build_fused()
    res = run_bass_kernel_spmd(_PROG["nc"], maps, core_ids=list(range(8))).results
    out = np.empty((B, SEQ, D), np.float32)
    for c in range(8):
        b, i = divmod(c, NQ)
        out[b].reshape(NQ, NQ, TILE, D)[:, i] = res[c]["out"].T.reshape(NQ, TILE, D)
    return out
```

```python
import math
from contextlib import ExitStack

import numpy as np
import ml_dtypes

import concourse.bass as bass
import concourse.mybir as mybir
from concourse.bass_utils import run_bass_kernel_spmd

F32 = mybir.dt.float32
BF16 = mybir.dt.bfloat16
AF = mybir.ActivationFunctionType
ALU = mybir.AluOpType

D = 1024
B = 2
SEQ = 8192
DEPTH = 2
HD = 64
NQ = 4
TQ = SEQ // NQ
TILE = 512
NCOL = 7 * 128
F32R = mybir.dt.float32r
EPS = 1e-6
ROPE_THETA = 500000.0
ENGS = ("sp", "act", "dve", "pool", "pe")


class Buf:
    __slots__ = ("name", "w", "r", "dsem", "dcount", "dma_w", "dma_r", "excl")

    def __init__(self, name):
        self.name = name
        self.excl = False
        self.w = None
        self.r = {}
        self.dsem = None
        self.dcount = 0
        self.dma_w = 0
        self.dma_r = 0


class DramBuf:
    def __init__(self, name):
        self.name = name
        self.events = []


class Sched:
    def __init__(self, nc, stack):
        self.nc = nc
        self.stack = stack
        self.ops = {e: [] for e in ENGS}
        self.count = {e: 0 for e in ENGS}
        self.seen = {e: {f: 0 for f in ENGS} for e in ENGS}
        self.seen_d = {e: {} for e in ENGS}
        self.sem = {}
        for e in ("act", "dve", "pool", "pe"):
            self.sem[e] = stack.enter_context(nc.semaphore("s_" + e))
        self.nbuf = 0
        self.dbufs = []
        self.clk = {e: {} for e in ENGS}

    def dram(self, name):
        return DramBuf(name)

    def buf(self, name=None):
        self.nbuf += 1
        return Buf(name or ("b%d" % self.nbuf))

    def bufs(self, n, name="b"):
        return [self.buf("%s%d" % (name, i)) for i in range(n)]

    def alias(self, old, name=None):
        b = self.buf(name)
        b.w = old.w
        b.r = dict(old.r)
        b.dsem = old.dsem
        b.dcount = old.dcount
        b.dma_w = old.dma_w
        b.dma_r = max(old.dma_r, 0)
        return b

    def _dsem(self, b):
        if b.dsem is None:
            self.nsem = getattr(self, "nsem", 0) + 1
            b.dsem = self.stack.enter_context(self.nc.semaphore("d%d_%s" % (self.nsem, b.name)))
            self.dbufs.append(b)
        return b.dsem

    def _need(self, eng, waits, dep):
        if dep is None:
            return
        e2, idx = dep
        if e2 == eng and eng == "pe":
            return
        if self.seen[eng][e2] >= idx:
            return
        self.seen[eng][e2] = idx
        waits.append((self.sem[e2], idx))
        snap = self.clk[e2].get(idx)
        if snap is not None:
            mine = self.seen[eng]
            for f, v in snap.items():
                if f != eng and v > mine[f]:
                    mine[f] = v

    def _need_d(self, eng, waits, b, val):
        if val <= 0:
            return
        key = id(b.dsem)
        if self.seen_d[eng].get(key, 0) >= val:
            return
        self.seen_d[eng][key] = val
        waits.append((b.dsem, val))

    def _deps(self, eng, reads, writes):
        waits = []
        for b in reads:
            if b.w is not None:
                self._need(eng, waits, b.w)
            if b.dma_w:
                self._need_d(eng, waits, b, b.dma_w)
            if b.excl:
                for e2, idx in b.r.items():
                    if e2 != eng:
                        self._need(eng, waits, (e2, idx))
        for b in writes:
            if b.w is not None:
                self._need(eng, waits, b.w)
            if b.dma_w:
                self._need_d(eng, waits, b, b.dma_w)
            for e2, idx in b.r.items():
                self._need(eng, waits, (e2, idx))
            if b.dma_r:
                self._need_d(eng, waits, b, b.dma_r)
        return waits

    def op(self, eng, fn, reads=(), writes=()):
        waits = self._deps(eng, reads, writes)
        self.count[eng] += 1
        idx = self.count[eng]
        self.ops[eng].append((waits, fn, (self.sem[eng], 1)))
        self.clk[eng][idx] = dict(self.seen[eng])
        for b in reads:
            b.r[eng] = idx
        for b in writes:
            b.w = (eng, idx)
            b.r = {}
            b.dma_w = 0
            b.dma_r = 0
        return idx

    def _need_ev(self, eng, waits, sem, val):
        key = id(sem)
        if self.seen_d[eng].get(key, 0) >= val:
            return
        self.seen_d[eng][key] = val
        waits.append((sem, val))

    def dma(self, eng, fn, sb, sb_is_dst, dram_r=None, dram_w=None, extra_reads=()):
        self._dsem(sb)
        reads = list(extra_reads) + ([] if sb_is_dst else [sb])
        writes = [sb] if sb_is_dst else []
        waits = self._deps(eng, reads, writes)
        if dram_r is not None:
            for dr in (dram_r if isinstance(dram_r, (list, tuple)) else [dram_r]):
                for sem, val in dr.events:
                    self._need_ev(eng, waits, sem, val)
        sb.dcount += 16
        self.ops[eng].append((waits, fn, (sb.dsem, 16)))
        if sb_is_dst:
            sb.w = None
            sb.r = {}
            sb.dma_w = sb.dcount
            sb.dma_r = 0
        else:
            sb.dma_r = sb.dcount
        if dram_w is not None:
            dram_w.events = [(s_, v_) for (s_, v_) in dram_w.events if s_ is not sb.dsem]
            dram_w.events.append((sb.dsem, sb.dcount))

    def collective(self, fn, src, dst):
        waits = []
        for sem, val in src.events:
            self._need_ev("pool", waits, sem, val)
        sem = self.stack.enter_context(self.nc.semaphore("cc_" + dst.name))
        self.ops["pool"].append((waits, fn, (sem, 1)))
        dst.events = [(sem, 1)]

    def barrier(self):
        for eng in ENGS:
            waits = []
            for e2 in ("act", "dve", "pool", "pe"):
                if e2 != eng and self.count[e2] > self.seen[eng][e2]:
                    self.seen[eng][e2] = self.count[e2]
                    waits.append((self.sem[e2], self.count[e2]))
            for b in self.dbufs:
                if b.dcount:
                    self._need_ev(eng, waits, b.dsem, b.dcount)
            if waits:
                self.ops[eng].append((waits, None, None))

    def wait_all_dma(self, eng, bufs):
        waits = []
        seen = set()
        for b in bufs:
            if b.dsem is not None and b.dcount and id(b.dsem) not in seen:
                seen.add(id(b.dsem))
                waits.append((b.dsem, b.dcount))
        self.ops[eng].append((waits, None, None))

    def emit(self):
        nc = self.nc

        def run(engobj, lst):
            for waits, fn, inc in lst:
                for s_, v_ in waits:
                    engobj.wait_ge(s_, v_)
                if fn is not None:
                    fn(engobj).then_inc(inc[0], inc[1])

        with nc.Block() as block:
            if self.ops["sp"]:
                @block.sync
                def _(e):
                    run(e, self.ops["sp"])
            if self.ops["act"]:
                @block.scalar
                def _(e):
                    run(e, self.ops["act"])
            if self.ops["dve"]:
                @block.vector
                def _(e):
                    run(e, self.ops["dve"])
            if self.ops["pool"]:
                @block.gpsimd
                def _(e):
                    run(e, self.ops["pool"])
            if self.ops["pe"]:
                @block.tensor
                def _(e):
                    run(e, self.ops["pe"])


class Ctx:
    def __init__(self, nc, stack):
        self.nc = nc
        self.st = stack
        self.S = Sched(nc, stack)
        self.uid = 0

    def sb(self, name, shape, dt):
        self.uid += 1
        return self.st.enter_context(self.nc.sbuf_tensor("%s_%d" % (name, self.uid), shape, dt))

    def internal(self, name, shape, dt):
        return self.nc.dram_tensor(name, shape, dt)

    def din(self, name, shape, dt):
        return self.nc.dram_tensor(name, shape, dt, kind="ExternalInput").ap()

    def dout(self, name, shape, dt):
        return self.nc.dram_tensor(name, shape, dt, kind="ExternalOutput").ap()


def emit_phase_N(C, ps, bank, x_in, x_in_d, y_ag, y_ag_ds, idx_sb, b_idx, wo, gain, hn_tile, after_store,
                 x_out, x_out_d, final):
    S = C.S
    nt = TQ // TILE
    has_proj = y_ag is not None
    ones = C.sb("n_ones", [128, 128], F32)
    b_ones = S.buf("n_ones")
    S.op("pool", lambda e: e.memset(ones[:], 1.0), writes=[b_ones])
    g_sb = C.sb("n_gain", [128, 8], F32)
    b_g = S.buf("n_gain")
    S.dma("sp", lambda e: e.dma_start(out=g_sb[:], in_=gain), b_g, True)
    if has_proj:
        ysb = C.sb("n_ysb", [128, 8, TQ], BF16)
        b_ysb = [S.buf("n_ysb%d" % jj) for jj in range(TQ // TILE)]
        for jj in range(TQ // TILE):
            for kc in range(8):
                S.dma("pool", lambda e, kc=kc, jj=jj: e.indirect_dma_start(
                    out=ysb[:, kc, jj * TILE:(jj + 1) * TILE], out_offset=None, in_=y_ag[:, :],
                    in_offset=bass.IndirectOffsetOnAxis(ap=idx_sb[:, kc * 4 + jj:kc * 4 + jj + 1], axis=0)),
                    b_ysb[jj], True, dram_r=y_ag_ds[jj], extra_reads=[b_idx])
        wob, b_wob = wo
    xt = [C.sb("n_x%d" % i, [128, 8, TILE], F32) for i in range(2)]
    b_xt = S.bufs(2, "n_x")
    sq = [C.sb("n_sq%d" % i, [128, TILE], F32) for i in range(2)]
    b_sq = S.bufs(2, "n_sq")
    rstd = C.sb("n_rstd", [128, TILE], F32)
    b_rstd = S.buf("n_rstd")
    odt = F32 if final else BF16
    ht = [C.sb("n_h%d" % i, [128, 8, TILE], odt) for i in range(2)]
    b_ht = S.bufs(2, "n_h")
    xTr = x_in.rearrange("(c p) t -> p c t", p=128)
    if x_out is not None:
        xor = x_out.rearrange("(c p) t -> p c t", p=128)
    nb = 0
    for t in range(nt):
        j = t % 2
        ts = slice(t * TILE, (t + 1) * TILE)
        S.dma("sp", lambda e, j=j, ts=ts: e.dma_start(out=xt[j][:], in_=xTr[:, :, ts]), b_xt[j], True, dram_r=x_in_d)
        if has_proj:
            for fc in range(8):
                bk = nb % 4
                nb += 1
                for kc in range(8):
                    S.op("pe", lambda e, bk=bk, kc=kc, fc=fc, ts=ts: e.matmul(
                        ps[:, bk, :], wob[:, kc, fc * 128:(fc + 1) * 128], ysb[:, kc, ts],
                        start=(kc == 0), stop=(kc == 7)),
                        reads=[b_wob, b_ysb[t]], writes=[bank[bk]])
                S.op("dve", lambda e, bk=bk, fc=fc, j=j: e.tensor_tensor(
                    xt[j][:, fc, :], ps[:, bk, :], xt[j][:, fc, :], ALU.add),
                    reads=[bank[bk], b_xt[j]], writes=[b_xt[j]])
            if x_out is not None:
                S.dma("act", lambda e, j=j, ts=ts: e.dma_start(out=xor[:, :, ts], in_=xt[j][:]), b_xt[j], False,
                      dram_w=x_out_d)
        for fc in range(8):
            k = fc % 2
            S.op("act", lambda e, k=k, fc=fc, j=j: e.activation(sq[k][:], xt[j][:, fc, :], AF.Square),
                 reads=[b_xt[j]], writes=[b_sq[k]])
            S.op("pe", lambda e, k=k, fc=fc: e.matmul(ps[:, 4, :], ones[:], sq[k][:], start=(fc == 0), stop=(fc == 7)),
                 reads=[b_ones, b_sq[k]], writes=[bank[4]])
        S.op("act", lambda e: e.activation(rstd[:], ps[:, 4, :], AF.Ln, bias=EPS, scale=1.0 / D),
             reads=[bank[4]], writes=[b_rstd])
        S.op("act", lambda e: e.activation(rstd[:], rstd[:], AF.Exp, scale=-0.5), reads=[b_rstd], writes=[b_rstd])
        for fc in range(8):
            S.op("dve", lambda e, fc=fc, j=j: e.scalar_tensor_tensor(
                ht[j][:, fc, :], xt[j][:, fc, :], g_sb[:, fc:fc + 1], rstd[:], ALU.mult, ALU.mult),
                reads=[b_xt[j], b_g, b_rstd], writes=[b_ht[j]])
        h_ap, h_d = hn_tile(t)
        S.dma("act", lambda e, j=j, h_ap=h_ap: e.dma_start(out=h_ap, in_=ht[j][:]), b_ht[j], False, dram_w=h_d)
        if after_store is not None:
            after_store(t)
    return b_xt + b_ht


def emit_phase_M(C, ps, bank, hn_ag, hn_ag_ds, win, ctab, stab, lru_p, gw, y_src, y_src_ds, after_y):
    S = C.S
    nc = C.nc
    NT = SEQ // TILE
    psb_ = [ps[:, 6, :].bitcast(BF16), ps[:, 7, :].bitcast(BF16)]

    class _PSB:
        def __getitem__(self, idx):
            p, b, c = idx
            return psb_[b][p, c]
    psb = _PSB()

    ident = C.sb("ident", [128, 128], BF16); b_ident = S.buf("ident")
    S.op("pool", lambda e: e.memset(ident[:], 1.0), writes=[b_ident])
    S.op("pool", lambda e: e.affine_select(ident[:], ident[:], [[-1, 128]], ALU.is_equal, 0.0, base=0, channel_multiplier=1),
         reads=[b_ident], writes=[b_ident])
    ntri = C.sb("ntri", [128, 128], BF16); b_ntri = S.buf("ntri")
    S.op("pool", lambda e: e.memset(ntri[:], -1.0), writes=[b_ntri])
    S.op("pool", lambda e: e.affine_select(ntri[:], ntri[:], [[-1, 128]], ALU.is_ge, 0.0, base=0, channel_multiplier=1),
         reads=[b_ntri], writes=[b_ntri])
    nones = C.sb("nones", [128, 128], BF16); b_nones = S.buf("nones")
    S.op("pool", lambda e: e.memset(nones[:], -1.0), writes=[b_nones])
    dmask = C.sb("dmask", [128, 256], BF16); b_dmask = S.buf("dmask")
    S.op("pool", lambda e: e.memset(dmask[:], 1.0), writes=[b_dmask])
    S.op("pool", lambda e: e.affine_select(dmask[:, 0:128], dmask[:, 0:128], [[-1, 128]], ALU.is_ge, 0.0, base=0, channel_multiplier=1),
         reads=[b_dmask], writes=[b_dmask])
    S.op("pool", lambda e: e.affine_select(dmask[:, 128:256], dmask[:, 128:256], [[1, 128]], ALU.is_ge, 0.0, base=0, channel_multiplier=-1),
         reads=[b_dmask], writes=[b_dmask])
    wm = C.sb("wm", [128, 896], BF16); b_wm = S.buf("wm")
    S.op("pool", lambda e: e.memset(wm[:], 1.0), writes=[b_wm])
    S.op("pool", lambda e: e.affine_select(wm[:], wm[:], [[1, 896]], ALU.is_ge, 0.0, base=-385, channel_multiplier=-1),
         reads=[b_wm], writes=[b_wm])

    lp = C.sb("lp", [64, 8], F32); b_lp = S.buf("lp")
    S.dma("sp", lambda e: e.dma_start(out=lp[:], in_=lru_p[:, :]), b_lp, True)
    lc = C.sb("lc", [64, 4], F32); b_lc = S.buf("lc")
    S.op("act", lambda e: e.activation(lc[:, 2:3], lp[:, 7:8], AF.Exp, scale=-1.0), reads=[b_lp], writes=[b_lc])
    S.op("act", lambda e: e.activation(lc[:, 3:4], lc[:, 2:3], AF.Ln, bias=1.0), reads=[b_lc], writes=[b_lc])
    S.op("act", lambda e: e.mul(lc[:, 0:1], lc[:, 3:4], -8.0), reads=[b_lc], writes=[b_lc])
    S.op("act", lambda e: e.mul(lc[:, 1:2], lc[:, 3:4], -16.0), reads=[b_lc], writes=[b_lc])
    gwf = C.sb("gwf", [64, 128], F32); b_gwf = S.buf("gwf")
    S.dma("sp", lambda e: e.dma_start(out=gwf[:], in_=gw[:, :]), b_gwf, True)
    gwb = C.sb("gwb", [64, 128], BF16); b_gwb = S.buf("gwb")
    S.op("act", lambda e: e.copy(gwb[:], gwf[:]), reads=[b_gwf], writes=[b_gwb])

    rmat = C.sb("rmat", [128, 128], F32); b_rmat = S.buf("rmat")
    S.dma("sp", lambda e: e.dma_start(out=rmat[:], in_=C.rmat_d[:, :]), b_rmat, True)
    qr = C.sb("qr", [128, TILE], F32); b_qr = S.buf("qr")

    W, b_W = win

    dq = C.sb("dq", [128, SEQ], BF16); b_dq = S.buf("dq")
    dk = C.sb("dk", [128, SEQ], BF16); b_dk = S.buf("dk")
    dv = C.sb("dv", [128, SEQ], BF16); b_dv = S.buf("dv")
    dgs = C.sb("dgs", [128, SEQ], BF16); b_dgs = S.buf("dgs")
    T6 = C.sb("T6", [128, SEQ], BF16); b_T6 = S.buf("T6")
    T7 = C.sb("T7", [128, SEQ], BF16); b_T7 = S.buf("T7")
    T8, b_T8 = C.G16, C.b_G16
    svb = C.sb("svb", [128, SEQ // 128, HD], BF16); b_svb = S.buf("svb")

    hnb = [C.sb("hnb%d" % i, [128, 8, TILE], BF16) for i in range(2)]
    b_hnb = S.bufs(2, "hnb")
    ctb = [C.sb("ctb%d" % i, [128, TILE], F32) for i in range(2)]
    b_ctb = S.bufs(2, "ctb")
    stb = [C.sb("stb%d" % i, [128, TILE], F32) for i in range(2)]
    b_stb = S.bufs(2, "stb")
    rt1 = [C.sb("rt1_%d" % i, [128, TILE], F32) for i in range(2)]
    b_rt1 = S.bufs(2, "rt1")
    rt2 = [C.sb("rt2_%d" % i, [128, TILE], F32) for i in range(2)]
    b_rt2 = S.bufs(2, "rt2")
    XT = [C.sb("XT%d" % i, [64, TILE + 3], F32) for i in range(2)]
    b_XT = S.bufs(2, "XT")
    xc = C.sb("xc", [64, TILE], F32); b_xc = S.buf("xc")
    xcb = C.sb("xcb", [64, TILE], BF16); b_xcb = S.buf("xcb")
    lr = C.sb("lr", [64, TILE], F32); b_lr = S.buf("lr")
    li = C.sb("li", [64, TILE], F32); b_li = S.buf("li")
    la = C.sb("la", [64, TILE], F32); b_la = S.buf("la")
    la2 = C.sb("la2", [64, TILE], F32); b_la2 = S.buf("la2")
    lh = [C.sb("lh%d" % i, [64, TILE], F32) for i in range(2)]
    b_lh = S.bufs(2, "lh")
    S.op("pool", lambda e: e.memset(XT[0][:, 0:3], 0.0), writes=[b_XT[0]])

    hnq = hn_ag.rearrange("t q (c p) w -> t q p c w", p=128)

    def lru_and_v(t):
        j = t % 2
        ts = slice(t * TILE, (t + 1) * TILE)
        pb = t % 2
        for q in range(4):
            c0 = t * TILE + q * 128
            S.op("pe", lambda e, q=q, c0=c0, pb=pb: e.transpose(
                psb[:, pb, q * 64:(q + 1) * 64], T8[64:128, c0:c0 + 128], ident[64:128, 64:128]),
                reads=[b_T8, b_ident], writes=[bank[6 + pb]])
        S.op("dve", lambda e, t=t, pb=pb: e.tensor_copy(
            svb[:, t * 4:(t + 1) * 4, :], psb[:, pb, 0:256].rearrange("p (a b) -> p a b", a=4)),
            reads=[bank[6 + pb]], writes=[b_svb])
        X = XT[j]
        if t + 1 < NT:
            S.op("pool", lambda e, j=j: e.tensor_copy(XT[1 - j][:, 0:3], XT[j][:, TILE:TILE + 3]),
                 reads=[b_XT[j]], writes=[b_XT[1 - j]])
        S.op("dve", lambda e, X=X: e.tensor_scalar(xc[:], X[:, 3:TILE + 3], lp[:, 3:4], lp[:, 4:5], ALU.mult, ALU.add),
             reads=[b_XT[j], b_lp], writes=[b_xc])
        for k in range(3):
            S.op("dve", lambda e, X=X, k=k: e.scalar_tensor_tensor(
                xc[:], X[:, k:k + TILE], lp[:, k:k + 1], xc[:], ALU.mult, ALU.add),
                reads=[b_XT[j], b_lp, b_xc], writes=[b_xc])
        S.op("pool", lambda e: e.tensor_copy(xcb[:], xc[:]), reads=[b_xc], writes=[b_xcb])
        S.op("pe", lambda e: e.matmul(ps[0:64, 4, :], gwb[:, 0:64], xcb[:], start=True, stop=True),
             reads=[b_gwb, b_xcb], writes=[bank[4]])
        S.op("pe", lambda e: e.matmul(ps[0:64, 5, :], gwb[:, 64:128], xcb[:], start=True, stop=True),
             reads=[b_gwb, b_xcb], writes=[bank[5]])
        S.op("act", lambda e: e.activation(lr[:], ps[0:64, 4, :], AF.Sigmoid, bias=lp[:, 5:6]),
             reads=[bank[4], b_lp], writes=[b_lr])
        S.op("act", lambda e: e.activation(li[:], ps[0:64, 5, :], AF.Sigmoid, bias=lp[:, 6:7]),
             reads=[bank[5], b_lp], writes=[b_li])
        S.op("act", lambda e: e.activation(la[:], lr[:], AF.Exp, scale=lc[:, 0:1]), reads=[b_lr, b_lc], writes=[b_la])
        S.op("act", lambda e: e.activation(la2[:], lr[:], AF.Exp, scale=lc[:, 1:2]), reads=[b_lr, b_lc], writes=[b_la2])
        S.op("dve", lambda e: e.tensor_scalar_min(la2[:], la2[:], 1.0 - 1e-7), reads=[b_la2], writes=[b_la2])
        S.op("act", lambda e: e.activation(la2[:], la2[:], AF.Ln, bias=1.0, scale=-1.0), reads=[b_la2], writes=[b_la2])
        S.op("act", lambda e: e.activation(la2[:], la2[:], AF.Exp, scale=0.5), reads=[b_la2], writes=[b_la2])
        S.op("dve", lambda e: e.tensor_tensor(li[:], li[:], xc[:], ALU.mult), reads=[b_li, b_xc], writes=[b_li])
        S.op("dve", lambda e: e.tensor_tensor(li[:], li[:], la2[:], ALU.mult), reads=[b_li, b_la2], writes=[b_li])
        if t == 0:
            S.op("dve", lambda e, j=j: e.tensor_tensor_scan(lh[j][:], la[:], li[:], 0.0, ALU.mult, ALU.add),
                 reads=[b_la, b_li], writes=[b_lh[j]])
        else:
            S.op("dve", lambda e, j=j: e.tensor_tensor_scan(lh[j][:], la[:], li[:], lh[1 - j][:, TILE - 1:TILE], ALU.mult, ALU.add),
                 reads=[b_la, b_li, b_lh[1 - j]], writes=[b_lh[j]])
        S.op("pool", lambda e, j=j, ts=ts: e.tensor_tensor(T7[0:64, ts], lh[j][:], T8[0:64, ts], ALU.mult),
             reads=[b_lh[j], b_T8], writes=[b_T7])

    nbk = 0
    for t in range(NT):
        j = t % 2
        ts = slice(t * TILE, (t + 1) * TILE)
        S.dma("sp", lambda e, j=j, t=t: e.dma_start(
            out=hnb[j][:], in_=hnq[t // 4][t % 4]), b_hnb[j], True, dram_r=hn_ag_ds[t // 4])
        S.dma("sp", lambda e, j=j, ts=ts: e.dma_start(out=ctb[j][:], in_=ctab[:, ts]), b_ctb[j], True)
        S.dma("sp", lambda e, j=j, ts=ts: e.dma_start(out=stb[j][:], in_=stab[:, ts]), b_stb[j], True)
        pending = []
        for gi in range(7):
            bk = nbk % 4
            nbk += 1
            for kc in range(8):
                S.op("pe", lambda e, bk=bk, kc=kc, gi=gi, j=j: e.matmul(
                    ps[:, bk, :], W[:, kc, gi * 128:(gi + 1) * 128], hnb[j][:, kc, :],
                    start=(kc == 0), stop=(kc == 7)),
                    reads=[b_W, b_hnb[j]], writes=[bank[bk]])
            for fn in pending:
                fn()
            pending = []
            if gi in (0, 1):
                dst, bdst = (dq, b_dq) if gi == 0 else (dk, b_dk)
                S.op("dve", lambda e, bk=bk, j=j: e.tensor_tensor(rt1[j][:], ps[:, bk, :], ctb[j][:], ALU.mult),
                     reads=[bank[bk], b_ctb[j]], writes=[b_rt1[j]])
                S.op("act", lambda e, bk=bk: e.copy(qr[:], ps[:, bk, :]), reads=[bank[bk]], writes=[b_qr])

                def rot(j=j, ts=ts, dst=dst, bdst=bdst):
                    nonlocal nbk
                    bkr = nbk % 4
                    nbk += 1
                    S.op("pe", lambda e, bkr=bkr: e.matmul(
                        ps[:, bkr, :], rmat[:], qr[:], start=True, stop=True),
                        reads=[b_rmat, b_qr], writes=[bank[bkr]])
                    S.op("dve", lambda e, bkr=bkr, j=j: e.tensor_tensor(rt2[j][:], ps[:, bkr, :], stb[j][:], ALU.mult),
                         reads=[bank[bkr], b_stb[j]], writes=[b_rt2[j]])
                    S.op("pool", lambda e, j=j, ts=ts, dst=dst: e.tensor_tensor(dst[:, ts], rt1[j][:], rt2[j][:], ALU.add),
                         reads=[b_rt1[j], b_rt2[j]], writes=[bdst])
                pending.append(rot)
            elif gi == 2:
                S.op("act", lambda e, bk=bk, ts=ts: e.copy(dv[:, ts], ps[:, bk, :]), reads=[bank[bk]], writes=[b_dv])
            elif gi == 3:
                S.op("act", lambda e, bk=bk, ts=ts: e.activation(dgs[:, ts], ps[:, bk, :], AF.Silu),
                     reads=[bank[bk]], writes=[b_dgs])
            elif gi == 4:
                S.op("act", lambda e, bk=bk, ts=ts: e.activation(T6[0:64, ts], ps[0:64, bk, :], AF.Silu),
                     reads=[bank[bk]], writes=[b_T6])
                S.op("dve", lambda e, bk=bk, ts=ts: e.tensor_scalar_mul(T6[64:128, ts], ps[64:128, bk, :], 0.125),
                     reads=[bank[bk]], writes=[b_T6])
            elif gi == 5:
                S.op("act", lambda e, bk=bk, j=j: e.copy(XT[j][:, 3:TILE + 3], ps[0:64, bk, :]),
                     reads=[bank[bk]], writes=[b_XT[j]])
                S.op("dve", lambda e, bk=bk, ts=ts: e.tensor_copy(T7[64:128, ts], ps[64:128, bk, :]),
                     reads=[bank[bk]], writes=[b_T7])
            elif gi == 6:
                S.op("act", lambda e, bk=bk, ts=ts: e.activation(T8[0:64, ts], ps[0:64, bk, :], AF.Silu),
                     reads=[bank[bk]], writes=[b_T8])
                S.op("dve", lambda e, bk=bk, ts=ts: e.tensor_copy(T8[64:128, ts], ps[64:128, bk, :]),
                     reads=[bank[bk]], writes=[b_T8])
            if gi == 3 and t >= 1:
                lru_and_v(t - 1)
    lru_and_v(NT - 1)

    C.prefetch_wo()

    vaug = [C.sb("vaug%d" % i, [128, 2, 128], BF16) for i in range(3)]
    b_vaug = S.bufs(3, "vaug")
    for i in range(3):
        S.op("pool", lambda e, i=i: e.memset(vaug[i][:, :, 64:128], 1.0), writes=[b_vaug[i]])
    pt = [C.sb("pt%d" % i, [128, 256], BF16) for i in range(2)]
    b_pt = S.bufs(2, "pt")
    rcp_l = [rt2[0], rt1[1]]; b_rcp_l = [S.alias(b_rt2[0], "rcp0"), S.alias(b_rt1[1], "rcp1")]
    nrm_l = [rt2[1], ctb[0]]; b_nrm_l = [S.alias(b_rt2[1], "nrm0"), S.alias(b_ctb[0], "nrm1")]
    SEG = 2048
    acc_banks = [0, 1, 2, 3]
    for seg in range(SEQ // SEG):
        for h in range(2):
            hr = slice(h * 64, (h + 1) * 64)
            qbs = []
            for d in (1, 4, 16):
                nmb = SEG // d // 128
                for r in range(d):
                    for mbl in range(nmb):
                        qbs.append((d, r, seg * nmb + mbl))
            started = [False] * 4

            def cols(d, r, mb):
                s0 = d * 128 * mb + r
                return slice(s0, s0 + d * 127 + 1, d)

            def stage_a(k):
                d, r, mb = qbs[k]
                sb_ = 4 + (k % 2)
                vb = k % 2
                qc = cols(d, r, mb)
                blocks = ([(0, mb - 1)] if mb >= 1 else []) + [(1, mb)]
                for half, kmb in blocks:
                    kc_ = cols(d, r, kmb)
                    S.op("pe", lambda e, sb_=sb_, half=half, kc_=kc_, qc=qc, hr=hr: e.matmul(
                        ps[:, sb_, half * 128:(half + 1) * 128], dk[hr, kc_], dq[hr, qc], start=True, stop=True),
                        reads=[b_dk, b_dq], writes=[bank[sb_]])
                for half, kmb in blocks:
                    kc_ = cols(d, r, kmb)
                    S.op("pe", lambda e, vb=vb, half=half, kc_=kc_, hr=hr: e.transpose(
                        psb[:, vb, half * 64:(half + 1) * 64], dv[hr, kc_], ident[hr, hr]),
                        reads=[b_dv, b_ident], writes=[bank[6 + vb]])
                lo = 0 if mb >= 1 else 1
                v3 = k % 3
                S.op("dve", lambda e, vb=vb, lo=lo, v3=v3: e.tensor_copy(
                    vaug[v3][:, lo:2, 0:64], psb[:, vb, lo * 64:128].rearrange("p (a b) -> p a b", b=64)),
                    reads=[bank[6 + vb]], writes=[b_vaug[v3]])

            def stage_b1(k):
                d, r, mb = qbs[k]
                sb_ = 4 + (k % 2)
                vb = k % 2
                lo = 0 if mb >= 1 else 128
                S.op("act", lambda e, sb_=sb_, vb=vb, lo=lo: e.activation(
                    pt[vb][:, lo:256], ps[:, sb_, lo:256], AF.Exp, scale=0.125),
                    reads=[bank[sb_]], writes=[b_pt[vb]])
                S.op("dve", lambda e, vb=vb, lo=lo: e.tensor_tensor(
                    pt[vb][:, lo:256], pt[vb][:, lo:256], dmask[:, lo:256], ALU.mult),
                    reads=[b_pt[vb], b_dmask], writes=[b_pt[vb]])

            def stage_b2(k):
                d, r, mb = qbs[k]
                vb = k % 2
                halves = ([0] if mb >= 1 else []) + [1]
                t0 = d * 128 * mb + r - seg * SEG
                if d == 16:
                    pieces = [(bkk, slice(r, 512, 16), slice(32 * bkk, 32 * bkk + 32)) for bkk in range(4)]
                else:
                    bkk = t0 // 512
                    o0 = t0 - bkk * 512
                    pieces = [(bkk, slice(o0, o0 + d * 127 + 1, d), slice(0, 128))]
                for half in halves:
                    for bkk, oc, pc in pieces:
                        st_ = not started[bkk]
                        started[bkk] = True
                        S.op("pe", lambda e, vb=vb, half=half, bkk=bkk, oc=oc, pc=pc, st_=st_, v3=k % 3: e.matmul(
                            ps[:, bkk, oc], vaug[v3][:, half, :], pt[vb][:, half * 128 + pc.start:half * 128 + pc.stop],
                            start=st_, stop=False, skip_group_check=True),
                            reads=[b_vaug[k % 3], b_pt[vb]], writes=[bank[bkk]])

            nq_ = len(qbs)
            stage_a(0)
            stage_a(1)
            stage_b1(0)
            for k in range(nq_):
                if k + 2 < nq_:
                    stage_a(k + 2)
                if k + 1 < nq_:
                    stage_b1(k + 1)
                stage_b2(k)
            for bkk in range(4):
                cs = slice(seg * SEG + bkk * 512, seg * SEG + (bkk + 1) * 512)
                ro = slice(0, 64) if h == 0 else slice(64, 128)
                rcp, b_rcp, nrm, b_nrm = rcp_l[bkk % 2], b_rcp_l[bkk % 2], nrm_l[bkk % 2], b_nrm_l[bkk % 2]
                S.op("act", lambda e, bkk=bkk, ro=ro, rcp=rcp: e.activation(rcp[ro, :], ps[64:128, bkk, :], AF.Ln),
                     reads=[bank[bkk]], writes=[b_rcp])
                S.op("act", lambda e, ro=ro, rcp=rcp: e.activation(rcp[ro, :], rcp[ro, :], AF.Exp, scale=-1.0),
                     reads=[b_rcp], writes=[b_rcp])
                S.op("dve", lambda e, bkk=bkk, ro=ro, rcp=rcp, nrm=nrm: e.tensor_tensor(nrm[ro, :], ps[0:64, bkk, :], rcp[ro, :], ALU.mult),
                     reads=[bank[bkk], b_rcp], writes=[b_nrm])
                S.op("pool", lambda e, cs=cs, ro=ro, nrm=nrm: e.tensor_tensor(dgs[ro, cs], nrm[ro, :], dgs[ro, cs], ALU.mult),
                     reads=[b_nrm, b_dgs], writes=[b_dgs])

    hv = [hnb[i][:].rearrange("p a b -> p (a b)") for i in range(2)]
    E = [hv[0][:, 2048 * i:2048 * (i + 1)].bitcast(F32).rearrange("p (a b) -> p a b", a=2) for i in range(2)]
    b_E = [S.alias(b_hnb[0], "sbE%d" % i) for i in range(2)]
    SP = [hv[1][:, 1024 * i:1024 * (i + 1)].rearrange("p (a b) -> p a b", a=2) for i in range(3)]
    b_SP = [S.alias(b_hnb[1], "sbSP%d" % i) for i in range(3)]
    WT = [ctb[i][:].bitcast(BF16).rearrange("p (a b) -> p a b", a=2) for i in range(2)]
    b_WT = [S.alias(b_nrm_l[1] if i == 0 else b_ctb[i], "sbWT%d" % i) for i in range(2)]
    RR = []
    b_RR = []
    for src, bsrc in ((stb[0], b_stb[0]), (stb[1], b_stb[1]), (rt1[0], b_rt1[0])):
        v = src[:].bitcast(BF16)
        for hh in range(2):
            RR.append(v[:, hh * TILE:(hh + 1) * TILE])
            b_RR.append(S.alias(bsrc, "sbR%d" % len(RR)))
    groups = []
    for i in range(NT):
        ng = 2 * i + 2
        for gidx in range(ng):
            ka = 4 * i + 3 - 2 * gidx
            groups.append((i, gidx, ng, ka))
    NG = len(groups)
    rstate = {}

    def zb(k):
        return 2 * (k % 3)

    def sb_z(k):
        i, gidx, ng, ka = groups[k]
        qs = slice(i * TILE, (i + 1) * TILE)
        for u in range(2):
            kb = ka - u
            S.op("pe", lambda e, k=k, u=u, kb=kb, qs=qs: e.matmul(
                ps[:, zb(k) + u, :], T7[64:128, kb * 128:(kb + 1) * 128], T6[64:128, qs], start=True, stop=True),
                reads=[b_T7, b_T6], writes=[bank[zb(k) + u]])

    def sb_a(k):
        i, gidx, ng, ka = groups[k]
        e_ = E[k % 2]; be = b_E[k % 2]
        sp = SP[k % 3]; bsp = b_SP[k % 3]
        z0 = zb(k)
        S.op("act", lambda e: e.activation(e_[:], ps[:, z0:z0 + 2, :], AF.Exp),
             reads=[bank[z0], bank[z0 + 1]], writes=[be])
        S.op("act", lambda e: e.activation(sp[:], e_[:], AF.Ln, bias=1.0), reads=[be], writes=[bsp])
        diag = gidx < 2
        if diag:
            for u in range(2):
                a = (ka - u) - 4 * i
                S.op("dve", lambda e, u=u, a=a: e.tensor_tensor(
                    sp[:, u, :], sp[:, u, :], wm[:, 384 - 128 * a:384 - 128 * a + TILE], ALU.mult),
                    reads=[bsp, b_wm], writes=[bsp])
        if gidx == 0:
            r0 = None
            r1 = (sp[:, 0, :], bsp)
        else:
            r0 = rstate["cur"]
            nr = rstate["n"] % 6
            rstate["n"] += 1
            S.op("pool", lambda e, r0=r0, nr=nr: e.tensor_tensor(RR[nr][:], r0[0], sp[:, 0, :], ALU.add),
                 reads=[r0[1], bsp], writes=[b_RR[nr]])
            r1 = (RR[nr][:], b_RR[nr])
        if gidx + 1 < ng:
            nr2 = rstate["n"] % 6
            rstate["n"] += 1
            S.op("dve", lambda e, r1=r1, nr2=nr2: e.tensor_tensor(RR[nr2][:], r1[0], sp[:, 1, :], ALU.add),
                 reads=[r1[1], bsp], writes=[b_RR[nr2]])
            rstate["cur"] = (RR[nr2][:], b_RR[nr2])
        S.op("pe", lambda e: e.matmul(ps[:, z0, :], ntri[:], sp[:, 0, :], start=False, stop=False, skip_group_check=True),
             reads=[b_ntri, bsp], writes=[bank[z0]])
        if r0 is not None:
            S.op("pe", lambda e, r0=r0: e.matmul(ps[:, z0, :], nones[:], r0[0], start=False, stop=False, skip_group_check=True),
                 reads=[b_nones, r0[1]], writes=[bank[z0]])
        S.op("pe", lambda e: e.matmul(ps[:, z0 + 1, :], ntri[:], sp[:, 1, :], start=False, stop=False, skip_group_check=True),
             reads=[b_ntri, bsp], writes=[bank[z0 + 1]])
        S.op("pe", lambda e, r1=r1: e.matmul(ps[:, z0 + 1, :], nones[:], r1[0], start=False, stop=False, skip_group_check=True),
             reads=[b_nones, r1[1]], writes=[bank[z0 + 1]])

    def sb_b(k):
        i, gidx, ng, ka = groups[k]
        wt = WT[k % 2]; bwt = b_WT[k % 2]
        z0 = zb(k)
        ob = 6 + (i % 2)
        qs = slice(i * TILE, (i + 1) * TILE)
        S.op("act", lambda e: e.activation(wt[:], ps[:, z0:z0 + 2, :], AF.Exp),
             reads=[bank[z0], bank[z0 + 1]], writes=[bwt])
        if gidx < 2:
            for u in range(2):
                a = (ka - u) - 4 * i
                S.op("dve", lambda e, u=u, a=a: e.tensor_tensor(
                    wt[:, u, :], wt[:, u, :], wm[:, 384 - 128 * a:384 - 128 * a + TILE], ALU.mult),
                    reads=[bwt, b_wm], writes=[bwt])
        for u in range(2):
            kb = ka - u
            S.op("pe", lambda e, u=u, kb=kb, ob=ob, first=(gidx == 0 and u == 0), last=(gidx == ng - 1 and u == 1): e.matmul(
                ps[0:64, ob, :], svb[:, kb, :], wt[:, u, :], start=first, stop=last),
                reads=[b_svb, bwt], writes=[bank[ob]])
        if gidx == ng - 1:
            S.op("dve", lambda e, ob=ob, qs=qs: e.tensor_tensor(T6[0:64, qs], ps[0:64, ob, :], T6[0:64, qs], ALU.mult),
                 reads=[bank[ob], b_T6], writes=[b_T6])
            if i % 4 == 3:
                q = i // 4
                cq = slice(q * TQ, (q + 1) * TQ)
                ysv = y_src[q].rearrange("u f w -> f u w")
                S.dma("sp", lambda e, ysv=ysv, cq=cq: e.dma_start(
                    out=ysv[0:128], in_=dgs[:, cq].rearrange("p (u w) -> p u w", u=4)), b_dgs, False, dram_w=y_src_ds[q])
                S.dma("sp", lambda e, ysv=ysv, cq=cq: e.dma_start(
                    out=ysv[128:192], in_=T7[0:64, cq].rearrange("p (u w) -> p u w", u=4)), b_T7, False, dram_w=y_src_ds[q])
                S.dma("sp", lambda e, ysv=ysv, cq=cq: e.dma_start(
                    out=ysv[192:256], in_=T6[0:64, cq].rearrange("p (u w) -> p u w", u=4)), b_T6, False, dram_w=y_src_ds[q])
                after_y(q)

    rstate["n"] = 0
    sb_z(0)
    for k in range(NG):
        if k + 1 < NG:
            sb_z(k + 1)
        sb_a(k)
        if k >= 1:
            sb_b(k - 1)
    sb_b(NG - 1)

    return None


STOP = None


def build_fused():
    nc = bass.Bass("TRN2", target_bir_lowering=False)
    with ExitStack() as gst:
        C = Ctx(nc, gst)
        S = C.S
        xT = C.din("xT", [D, TQ], F32)
        gains = C.din("gains", [128, DEPTH + 1, 8], F32)
        win = [C.din("win%d" % l, [128, 8, NCOL], F32) for l in range(DEPTH)]
        wo = [C.din("wo%d" % l, [128, 8, D], F32) for l in range(DEPTH)]
        ctab = C.din("ctab", [128, SEQ], F32)
        stab = C.din("stab", [128, SEQ], F32)
        lru_p = C.din("lru_p", [64, DEPTH, 8], F32)
        gw = C.din("gw", [64, DEPTH, 128], F32)
        idx = C.din("idx", [128, 32], mybir.dt.int32)
        C.rmat_d = C.din("rmat", [128, 128], F32)
        out = C.dout("out", [D, TQ], F32)
        NTL = TQ // TILE
        hn_src = [C.internal("hn_src%d" % l, [NTL, D, TILE], BF16).ap() for l in range(DEPTH)]
        hn_ag = [C.internal("hn_ag%d" % l, [NTL, NQ, D, TILE], BF16).ap() for l in range(DEPTH)]
        y_src = [C.internal("y_src%d" % l, [NQ, 4, 256, TILE], BF16).ap() for l in range(DEPTH)]
        y_ag = [C.internal("y_ag%d" % l, [NQ, NQ, 4, 256, TILE], BF16).ap() for l in range(DEPTH)]
        x1 = C.internal("x1", [D, TQ], F32).ap()
        d_hn_src = [[S.dram("hn_src%d_%d" % (l, t)) for t in range(NTL)] for l in range(DEPTH)]
        d_hn_ag = [[S.dram("hn_ag%d_%d" % (l, t)) for t in range(NTL)] for l in range(DEPTH)]
        d_y_src = [[S.dram("y_src%d_%d" % (l, q)) for q in range(NQ)] for l in range(DEPTH)]
        d_y_ag = [[S.dram("y_ag%d_%d" % (l, q)) for q in range(NQ)] for l in range(DEPTH)]
        d_x1 = S.dram("x1")
        ps = gst.enter_context(nc.psum_tensor("ps", [128, 8, 512], F32))
        bank = S.bufs(8, "bank")
        for b_ in bank:
            b_.excl = True
        Wg = gst.enter_context(nc.sbuf_tensor("Wg", [128, 8, NCOL], BF16))
        b_Wg = S.buf("Wg")
        wstg = [gst.enter_context(nc.sbuf_tensor("wstg%d" % i_, [128, max(NCOL, D)], F32)) for i_ in range(2)]
        b_wstg = S.bufs(2, "wstg")
        C.G16 = gst.enter_context(nc.sbuf_tensor("G16", [128, SEQ], BF16))
        C.b_G16 = S.buf("G16")
        wob_view = C.G16[:].rearrange("p (k n) -> p k n", k=8)

        def load_W(l):
            for kc in range(8):
                j = kc % 2
                S.dma("sp", lambda e, kc=kc, j=j: e.dma_start(out=wstg[j][:, 0:NCOL], in_=win[l][:, kc, :]), b_wstg[j], True)
                S.op("act", lambda e, kc=kc, j=j: e.copy(Wg[:, kc, :], wstg[j][:, 0:NCOL]), reads=[b_wstg[j]], writes=[b_Wg])

        def load_Wo(l):
            for kc in range(8):
                j = kc % 2
                S.dma("sp", lambda e, kc=kc, j=j: e.dma_start(out=wstg[j][:, 0:D], in_=wo[l][:, kc, :]), b_wstg[j], True)
                S.op("act", lambda e, kc=kc, j=j: e.copy(wob_view[:, kc, :], wstg[j][:, 0:D]),
                     reads=[b_wstg[j]], writes=[C.b_G16])

        idx_sb = gst.enter_context(nc.sbuf_tensor("idx_sb", [128, 32], mybir.dt.int32))
        b_idx = S.buf("idx_sb")
        S.dma("sp", lambda e: e.dma_start(out=idx_sb[:], in_=idx[:, :]), b_idx, True)
        groups = [[0, 1, 2, 3], [4, 5, 6, 7]]

        def allgather(src, dst, dsrc, ddst):
            S.collective(lambda e: e.collective_compute(
                "AllGather", ALU.bypass, replica_groups=groups,
                ins=[src.opt()], outs=[dst.opt()], dma_qos="P2"), dsrc, ddst)

        for l in range(DEPTH + 1):
            if STOP == "start":
                break
            S.barrier()
            final = (l == DEPTH)
            if not final:
                load_W(l)
            with ExitStack() as pst:
                C.st = pst
                x_in, x_in_d = (xT, None) if l <= 1 else (x1, d_x1)
                if final:
                    outr = out.rearrange("(c p) t -> p c t", p=128)
                    hn_tile = lambda t: (outr[:, :, t * TILE:(t + 1) * TILE], None)
                    after = None
                else:
                    hn_tile = lambda t, l=l: (hn_src[l][t].rearrange("(c p) w -> p c w", p=128), d_hn_src[l][t])
                    after = lambda t, l=l: allgather(hn_src[l][t], hn_ag[l][t].rearrange("q f w -> (q f) w"),
                                                     d_hn_src[l][t], d_hn_ag[l][t])
                emit_phase_N(
                    C, ps, bank, x_in, x_in_d,
                    y_ag[l - 1].rearrange("a g u f w -> (a g u f) w") if l >= 1 else None,
                    d_y_ag[l - 1] if l >= 1 else None, idx_sb, b_idx,
                    (wob_view, C.b_G16) if l >= 1 else None, gains[:, l, :], hn_tile, after,
                    x1 if (l >= 1 and not final) else None, d_x1 if (l >= 1 and not final) else None, final)
                S.barrier()
            if final or STOP == "N%d" % l:
                break
            with ExitStack() as pst:
                C.st = pst
                after_y = lambda q, l=l: allgather(y_src[l][q].rearrange("u f w -> (u f) w"),
                                                   y_ag[l][q].rearrange("g u f w -> (g u f) w"),
                                                   d_y_src[l][q], d_y_ag[l][q])
                C.prefetch_wo = lambda l=l: load_Wo(l)
                emit_phase_M(C, ps, bank, hn_ag[l], d_hn_ag[l], (Wg, b_Wg), ctab, stab,
                             lru_p[:, l, :], gw[:, l, :], y_src[l], d_y_src[l], after_y)
                S.barrier()
            if STOP == "M%d" % l:
                break
        S.barrier()
        S.emit()
    return nc


_AQ, _AK, _AV, _AG, _BX, _BG, _CQ, _CK, _CV, _CG = 0, 512, 1024, 1536, 2048, 2304, 2560, 2816, 3072, 3328


def _win_cols(g):
    def rng(base, n):
        return list(range(base, base + n))

    def swap(base):
        cols = []
        for h in range(2):
            hb = base + (2 * g + h) * 64
            cols += rng(hb + 8, 8) + rng(hb, 8) + rng(hb + 16, 48)
        return cols

    cols = []
    cols += rng(_AQ + 128 * g, 128)
    cols += rng(_AK + 128 * g, 128)
    cols += rng(_AV + 128 * g, 128)
    cols += rng(_AG + 128 * g, 128)
    cols += rng(_CG + 64 * g, 64) + rng(_CQ + 64 * g, 64)
    cols += rng(_BX + 64 * g, 64) + rng(_CK + 64 * g, 64)
    cols += rng(_BG + 64 * g, 64) + rng(_CV + 64 * g, 64)
    return np.asarray(cols)


def _wout_rows():
    rows = []
    for g in range(NQ):
        rows += list(range(128 * g, 128 * g + 128))
        rows += list(range(512 + 64 * g, 512 + 64 * g + 64))
        rows += list(range(768 + 64 * g, 768 + 64 * g + 64))
    return np.asarray(rows)


def _rot_matrix():
    r = np.zeros((128, 128), np.float32)
    for h in range(2):
        for d in range(8):
            r[h * 64 + d + 8, h * 64 + d] = 1.0
            r[h * 64 + d, h * 64 + d + 8] = 1.0
    return r


def _rope_tables():
    pos = np.arange(SEQ, dtype=np.float32)
    inv_freq = (np.float64(ROPE_THETA) ** (-np.arange(0, 16, 2, dtype=np.float64) / 16.0)).astype(np.float32)
    ang = (pos[:, None] * inv_freq[None, :]).astype(np.float32).astype(np.float64)
    cos = np.cos(ang).astype(np.float32).T
    sin = np.sin(ang).astype(np.float32).T
    ct = np.ones((128, SEQ), np.float32)
    st = np.zeros((128, SEQ), np.float32)
    for h in range(2):
        ct[h * 64:h * 64 + 8] = cos
        ct[h * 64 + 8:h * 64 + 16] = cos
        st[h * 64:h * 64 + 8] = -sin
        st[h * 64 + 8:h * 64 + 16] = sin
    return ct, st


_PROG = {}


def _kc_layout(w):
    n = w.shape[1]
    return np.ascontiguousarray(w.reshape(8, 128, n).transpose(1, 0, 2))


def kernel(x, norm_gain, w_in, conv_w, conv_b, gate_a_w, gate_a_b, gate_x_w, gate_x_b,
           lru_lambda, w_out, final_gain):
    inp = dict(x=x, norm_gain=norm_gain, w_in=w_in, conv_w=conv_w, conv_b=conv_b, gate_a_w=gate_a_w,
               gate_a_b=gate_a_b, gate_x_w=gate_x_w, gate_x_b=gate_x_b, lru_lambda=lru_lambda,
               w_out=w_out, final_gain=final_gain)
    inp = {k: np.asarray(v, dtype=np.float32) for k, v in inp.items()}
    ct, st = _rope_tables()
    rmat = _rot_matrix()
    rows = _wout_rows()
    gains = np.stack([inp["norm_gain"][l].reshape(8, 128).T for l in range(DEPTH)]
                     + [inp["final_gain"].reshape(8, 128).T], axis=1)
    gains = np.ascontiguousarray(gains.astype(np.float32))
    wo = [_kc_layout(inp["w_out"][l][rows, :]) for l in range(DEPTH)]
    maps = []
    for c in range(8):
        b, i = divmod(c, NQ)
        xs = inp["x"][b].reshape(NQ, NQ, TILE, D)[:, i]
        m = {"xT": np.ascontiguousarray(xs.reshape(TQ, D).T), "gains": gains,
             "ctab": ct, "stab": st, "rmat": rmat}
        cols = _win_cols(i)
        for l in range(DEPTH):
            m["win%d" % l] = _kc_layout(inp["w_in"][l][:, cols])
            m["wo%d" % l] = wo[l]
        lru_p = np.stack([np.stack(
            [inp["conv_w"][l][k, 64 * i:64 * i + 64] for k in range(4)]
            + [inp["conv_b"][l][64 * i:64 * i + 64], inp["gate_a_b"][l][i], inp["gate_x_b"][l][i],
               inp["lru_lambda"][l][64 * i:64 * i + 64]], axis=1) for l in range(DEPTH)], axis=1)
        m["lru_p"] = np.ascontiguousarray(lru_p.astype(np.float32))
        gw = np.stack([np.concatenate([inp["gate_a_w"][l][i], inp["gate_x_w"][l][i]], axis=1)
                       for l in range(DEPTH)], axis=1)
        m["gw"] = np.ascontiguousarray(gw.astype(np.float32))
        idx = np.zeros((128, 32), np.int32)
        p = np.arange(128)
        for kc in range(8):
            g, half = divmod(kc, 2)
            for jj in range(4):
                idx[:, kc * 4 + jj] = ((jj * NQ + g) * 4 + i) * 256 + half * 128 + p
        m["idx"] = idx
        maps.append(m)
    if "nc" not in _PROG:
        _PROG["nc"] = build_fused()
    res = run_bass_kernel_spmd(_PROG["nc"], maps, core_ids=list(range(8))).results
    out = np.empty((B, SEQ, D), np.float32)
    for c in range(8):
        b, i = divmod(c, NQ)
        out[b].reshape(NQ, NQ, TILE, D)[:, i] = res[c]["out"].T.reshape(NQ, TILE, D)
    return out
```

```python
import math
from contextlib import ExitStack

import numpy as np
import ml_dtypes

import concourse.bass as bass
import concourse.mybir as mybir
from concourse.bass_utils import run_bass_kernel_spmd

F32 = mybir.dt.float32
BF16 = mybir.dt.bfloat16
AF = mybir.ActivationFunctionType
ALU = mybir.AluOpType

D = 1024
B = 2
SEQ = 8192
DEPTH = 2
HD = 64
NQ = 4
TQ = SEQ // NQ
TILE = 512
NCOL = 7 * 128
F32R = mybir.dt.float32r
EPS = 1e-6
ROPE_THETA = 500000.0
ENGS = ("sp", "act", "dve", "pool", "pe")


class Buf:
    __slots__ = ("name", "w", "r", "dsem", "dcount", "dma_w", "dma_r", "excl")

    def __init__(self, name):
        self.name = name
        self.excl = False
        self.w = None
        self.r = {}
        self.dsem = None
        self.dcount = 0
        self.dma_w = 0
        self.dma_r = 0


class DramBuf:
    def __init__(self, name):
        self.name = name
        self.events = []


class Sched:
    def __init__(self, nc, stack):
        self.nc = nc
        self.stack = stack
        self.ops = {e: [] for e in ENGS}
        self.count = {e: 0 for e in ENGS}
        self.seen = {e: {f: 0 for f in ENGS} for e in ENGS}
        self.seen_d = {e: {} for e in ENGS}
        self.sem = {}
        for e in ("act", "dve", "pool", "pe"):
            self.sem[e] = stack.enter_context(nc.semaphore("s_" + e))
        self.nbuf = 0
        self.dbufs = []
        self.clk = {e: {} for e in ENGS}

    def dram(self, name):
        return DramBuf(name)

    def buf(self, name=None):
        self.nbuf += 1
        return Buf(name or ("b%d" % self.nbuf))

    def bufs(self, n, name="b"):
        return [self.buf("%s%d" % (name, i)) for i in range(n)]

    def alias(self, old, name=None):
        b = self.buf(name)
        b.w = old.w
        b.r = dict(old.r)
        b.dsem = old.dsem
        b.dcount = old.dcount
        b.dma_w = old.dma_w
        b.dma_r = max(old.dma_r, 0)
        return b

    def _dsem(self, b):
        if b.dsem is None:
            self.nsem = getattr(self, "nsem", 0) + 1
            b.dsem = self.stack.enter_context(self.nc.semaphore("d%d_%s" % (self.nsem, b.name)))
            self.dbufs.append(b)
        return b.dsem

    def _need(self, eng, waits, dep):
        if dep is None:
            return
        e2, idx = dep
        if e2 == eng and eng == "pe":
            return
        if self.seen[eng][e2] >= idx:
            return
        self.seen[eng][e2] = idx
        waits.append((self.sem[e2], idx))
        snap = self.clk[e2].get(idx)
        if snap is not None:
            mine = self.seen[eng]
            for f, v in snap.items():
                if f != eng and v > mine[f]:
                    mine[f] = v

    def _need_d(self, eng, waits, b, val):
        if val <= 0:
            return
        key = id(b.dsem)
        if self.seen_d[eng].get(key, 0) >= val:
            return
        self.seen_d[eng][key] = val
        waits.append((b.dsem, val))

    def _deps(self, eng, reads, writes):
        waits = []
        for b in reads:
            if b.w is not None:
                self._need(eng, waits, b.w)
            if b.dma_w:
                self._need_d(eng, waits, b, b.dma_w)
            if b.excl:
                for e2, idx in b.r.items():
                    if e2 != eng:
                        self._need(eng, waits, (e2, idx))
        for b in writes:
            if b.w is not None:
                self._need(eng, waits, b.w)
            if b.dma_w:
                self._need_d(eng, waits, b, b.dma_w)
            for e2, idx in b.r.items():
                self._need(eng, waits, (e2, idx))
            if b.dma_r:
                self._need_d(eng, waits, b, b.dma_r)
        return waits

    def op(self, eng, fn, reads=(), writes=()):
        waits = self._deps(eng, reads, writes)
        self.count[eng] += 1
        idx = self.count[eng]
        self.ops[eng].append((waits, fn, (self.sem[eng], 1)))
        self.clk[eng][idx] = dict(self.seen[eng])
        for b in reads:
            b.r[eng] = idx
        for b in writes:
            b.w = (eng, idx)
            b.r = {}
            b.dma_w = 0
            b.dma_r = 0
        return idx

    def _need_ev(self, eng, waits, sem, val):
        key = id(sem)
        if self.seen_d[eng].get(key, 0) >= val:
            return
        self.seen_d[eng][key] = val
        waits.append((sem, val))

    def dma(self, eng, fn, sb, sb_is_dst, dram_r=None, dram_w=None, extra_reads=()):
        self._dsem(sb)
        reads = list(extra_reads) + ([] if sb_is_dst else [sb])
        writes = [sb] if sb_is_dst else []
        waits = self._deps(eng, reads, writes)
        if dram_r is not None:
            for dr in (dram_r if isinstance(dram_r, (list, tuple)) else [dram_r]):
                for sem, val in dr.events:
                    self._need_ev(eng, waits, sem, val)
        sb.dcount += 16
        self.ops[eng].append((waits, fn, (sb.dsem, 16)))
        if sb_is_dst:
            sb.w = None
            sb.r = {}
            sb.dma_w = sb.dcount
            sb.dma_r = 0
        else:
            sb.dma_r = sb.dcount
        if dram_w is not None:
            dram_w.events = [(s_, v_) for (s_, v_) in dram_w.events if s_ is not sb.dsem]
            dram_w.events.append((sb.dsem, sb.dcount))

    def collective(self, fn, src, dst):
        waits = []
        for sem, val in src.events:
            self._need_ev("pool", waits, sem, val)
        sem = self.stack.enter_context(self.nc.semaphore("cc_" + dst.name))
        self.ops["pool"].append((waits, fn, (sem, 1)))
        dst.events = [(sem, 1)]

    def barrier(self):
        for eng in ENGS:
            waits = []
            for e2 in ("act", "dve", "pool", "pe"):
                if e2 != eng and self.count[e2] > self.seen[eng][e2]:
                    self.seen[eng][e2] = self.count[e2]
                    waits.append((self.sem[e2], self.count[e2]))
            for b in self.dbufs:
                if b.dcount:
                    self._need_ev(eng, waits, b.dsem, b.dcount)
            if waits:
                self.ops[eng].append((waits, None, None))

    def wait_all_dma(self, eng, bufs):
        waits = []
        seen = set()
        for b in bufs:
            if b.dsem is not None and b.dcount and id(b.dsem) not in seen:
                seen.add(id(b.dsem))
                waits.append((b.dsem, b.dcount))
        self.ops[eng].append((waits, None, None))

    def emit(self):
        nc = self.nc

        def run(engobj, lst):
            for waits, fn, inc in lst:
                for s_, v_ in waits:
                    engobj.wait_ge(s_, v_)
                if fn is not None:
                    fn(engobj).then_inc(inc[0], inc[1])

        with nc.Block() as block:
            if self.ops["sp"]:
                @block.sync
                def _(e):
                    run(e, self.ops["sp"])
            if self.ops["act"]:
                @block.scalar
                def _(e):
                    run(e, self.ops["act"])
            if self.ops["dve"]:
                @block.vector
                def _(e):
                    run(e, self.ops["dve"])
            if self.ops["pool"]:
                @block.gpsimd
                def _(e):
                    run(e, self.ops["pool"])
            if self.ops["pe"]:
                @block.tensor
                def _(e):
                    run(e, self.ops["pe"])


class Ctx:
    def __init__(self, nc, stack):
        self.nc = nc
        self.st = stack
        self.S = Sched(nc, stack)
        self.uid = 0

    def sb(self, name, shape, dt):
        self.uid += 1
        return self.st.enter_context(self.nc.sbuf_tensor("%s_%d" % (name, self.uid), shape, dt))

    def internal(self, name, shape, dt):
        return self.nc.dram_tensor(name, shape, dt)

    def din(self, name, shape, dt):
        return self.nc.dram_tensor(name, shape, dt, kind="ExternalInput").ap()

    def dout(self, name, shape, dt):
        return self.nc.dram_tensor(name, shape, dt, kind="ExternalOutput").ap()


def emit_phase_N(C, ps, bank, x_in, x_in_d, y_ag, y_ag_ds, idx_sb, b_idx, wo, gain, hn_tile, after_store,
                 x_out, x_out_d, final):
    S = C.S
    nt = TQ // TILE
    has_proj = y_ag is not None
    ones = C.sb("n_ones", [128, 128], F32)
    b_ones = S.buf("n_ones")
    S.op("pool", lambda e: e.memset(ones[:], 1.0), writes=[b_ones])
    g_sb = C.sb("n_gain", [128, 8], F32)
    b_g = S.buf("n_gain")
    S.dma("sp", lambda e: e.dma_start(out=g_sb[:], in_=gain), b_g, True)
    if has_proj:
        ysb = C.sb("n_ysb", [128, 8, TQ], BF16)
        b_ysb = [S.buf("n_ysb%d" % jj) for jj in range(TQ // TILE)]
        for jj in range(TQ // TILE):
            for kc in range(8):
                S.dma("pool", lambda e, kc=kc, jj=jj: e.indirect_dma_start(
                    out=ysb[:, kc, jj * TILE:(jj + 1) * TILE], out_offset=None, in_=y_ag[:, :],
                    in_offset=bass.IndirectOffsetOnAxis(ap=idx_sb[:, kc * 4 + jj:kc * 4 + jj + 1], axis=0)),
                    b_ysb[jj], True, dram_r=y_ag_ds[jj], extra_reads=[b_idx])
        wob, b_wob = wo
    xt = [C.sb("n_x%d" % i, [128, 8, TILE], F32) for i in range(2)]
    b_xt = S.bufs(2, "n_x")
    sq = [C.sb("n_sq%d" % i, [128, TILE], F32) for i in range(2)]
    b_sq = S.bufs(2, "n_sq")
    rstd = C.sb("n_rstd", [128, TILE], F32)
    b_rstd = S.buf("n_rstd")
    odt = F32 if final else BF16
    ht = [C.sb("n_h%d" % i, [128, 8, TILE], odt) for i in range(2)]
    b_ht = S.bufs(2, "n_h")
    xTr = x_in.rearrange("(c p) t -> p c t", p=128)
    if x_out is not None:
        xor = x_out.rearrange("(c p) t -> p c t", p=128)
    nb = 0
    for t in range(nt):
        j = t % 2
        ts = slice(t * TILE, (t + 1) * TILE)
        S.dma("sp", lambda e, j=j, ts=ts: e.dma_start(out=xt[j][:], in_=xTr[:, :, ts]), b_xt[j], True, dram_r=x_in_d)
        if has_proj:
            for fc in range(8):
                bk = nb % 4
                nb += 1
                for kc in range(8):
                    S.op("pe", lambda e, bk=bk, kc=kc, fc=fc, ts=ts: e.matmul(
                        ps[:, bk, :], wob[:, kc, fc * 128:(fc + 1) * 128], ysb[:, kc, ts],
                        start=(kc == 0), stop=(kc == 7)),
                        reads=[b_wob, b_ysb[t]], writes=[bank[bk]])
                S.op("dve", lambda e, bk=bk, fc=fc, j=j: e.tensor_tensor(
                    xt[j][:, fc, :], ps[:, bk, :], xt[j][:, fc, :], ALU.add),
                    reads=[bank[bk], b_xt[j]], writes=[b_xt[j]])
            if x_out is not None:
                S.dma("act", lambda e, j=j, ts=ts: e.dma_start(out=xor[:, :, ts], in_=xt[j][:]), b_xt[j], False,
                      dram_w=x_out_d)
        for fc in range(8):
            k = fc % 2
            S.op("act", lambda e, k=k, fc=fc, j=j: e.activation(sq[k][:], xt[j][:, fc, :], AF.Square),
                 reads=[b_xt[j]], writes=[b_sq[k]])
            S.op("pe", lambda e, k=k, fc=fc: e.matmul(ps[:, 4, :], ones[:], sq[k][:], start=(fc == 0), stop=(fc == 7)),
                 reads=[b_ones, b_sq[k]], writes=[bank[4]])
        S.op("act", lambda e: e.activation(rstd[:], ps[:, 4, :], AF.Ln, bias=EPS, scale=1.0 / D),
             reads=[bank[4]], writes=[b_rstd])
        S.op("act", lambda e: e.activation(rstd[:], rstd[:], AF.Exp, scale=-0.5), reads=[b_rstd], writes=[b_rstd])
        for fc in range(8):
            S.op("dve", lambda e, fc=fc, j=j: e.scalar_tensor_tensor(
                ht[j][:, fc, :], xt[j][:, fc, :], g_sb[:, fc:fc + 1], rstd[:], ALU.mult, ALU.mult),
                reads=[b_xt[j], b_g, b_rstd], writes=[b_ht[j]])
        h_ap, h_d = hn_tile(t)
        S.dma("act", lambda e, j=j, h_ap=h_ap: e.dma_start(out=h_ap, in_=ht[j][:]), b_ht[j], False, dram_w=h_d)
        if after_store is not None:
            after_store(t)
    return b_xt + b_ht


def emit_phase_M(C, ps, bank, hn_ag, hn_ag_ds, win, ctab, stab, lru_p, gw, y_src, y_src_ds, after_y):
    S = C.S
    nc = C.nc
    NT = SEQ // TILE
    psb_ = [ps[:, 6, :].bitcast(BF16), ps[:, 7, :].bitcast(BF16)]

    class _PSB:
        def __getitem__(self, idx):
            p, b, c = idx
            return psb_[b][p, c]
    psb = _PSB()

    ident = C.sb("ident", [128, 128], BF16); b_ident = S.buf("ident")
    S.op("pool", lambda e: e.memset(ident[:], 1.0), writes=[b_ident])
    S.op("pool", lambda e: e.affine_select(ident[:], ident[:], [[-1, 128]], ALU.is_equal, 0.0, base=0, channel_multiplier=1),
         reads=[b_ident], writes=[b_ident])
    ntri = C.sb("ntri", [128, 128], BF16); b_ntri = S.buf("ntri")
    S.op("pool", lambda e: e.memset(ntri[:], -1.0), writes=[b_ntri])
    S.op("pool", lambda e: e.affine_select(ntri[:], ntri[:], [[-1, 128]], ALU.is_ge, 0.0, base=0, channel_multiplier=1),
         reads=[b_ntri], writes=[b_ntri])
    nones = C.sb("nones", [128, 128], BF16); b_nones = S.buf("nones")
    S.op("pool", lambda e: e.memset(nones[:], -1.0), writes=[b_nones])
    dmask = C.sb("dmask", [128, 256], BF16); b_dmask = S.buf("dmask")
    S.op("pool", lambda e: e.memset(dmask[:], 1.0), writes=[b_dmask])
    S.op("pool", lambda e: e.affine_select(dmask[:, 0:128], dmask[:, 0:128], [[-1, 128]], ALU.is_ge, 0.0, base=0, channel_multiplier=1),
         reads=[b_dmask], writes=[b_dmask])
    S.op("pool", lambda e: e.affine_select(dmask[:, 128:256], dmask[:, 128:256], [[1, 128]], ALU.is_ge, 0.0, base=0, channel_multiplier=-1),
         reads=[b_dmask], writes=[b_dmask])
    wm = C.sb("wm", [128, 896], BF16); b_wm = S.buf("wm")
    S.op("pool", lambda e: e.memset(wm[:], 1.0), writes=[b_wm])
    S.op("pool", lambda e: e.affine_select(wm[:], wm[:], [[1, 896]], ALU.is_ge, 0.0, base=-385, channel_multiplier=-1),
         reads=[b_wm], writes=[b_wm])

    lp = C.sb("lp", [64, 8], F32); b_lp = S.buf("lp")
    S.dma("sp", lambda e: e.dma_start(out=lp[:], in_=lru_p[:, :]), b_lp, True)
    lc = C.sb("lc", [64, 4], F32); b_lc = S.buf("lc")
    S.op("act", lambda e: e.activation(lc[:, 2:3], lp[:, 7:8], AF.Exp, scale=-1.0), reads=[b_lp], writes=[b_lc])
    S.op("act", lambda e: e.activation(lc[:, 3:4], lc[:, 2:3], AF.Ln, bias=1.0), reads=[b_lc], writes=[b_lc])
    S.op("act", lambda e: e.mul(lc[:, 0:1], lc[:, 3:4], -8.0), reads=[b_lc], writes=[b_lc])
    S.op("act", lambda e: e.mul(lc[:, 1:2], lc[:, 3:4], -16.0), reads=[b_lc], writes=[b_lc])
    gwf = C.sb("gwf", [64, 128], F32); b_gwf = S.buf("gwf")
    S.dma("sp", lambda e: e.dma_start(out=gwf[:], in_=gw[:, :]), b_gwf, True)
    gwb = C.sb("gwb", [64, 128], BF16); b_gwb = S.buf("gwb")
    S.op("act", lambda e: e.copy(gwb[:], gwf[:]), reads=[b_gwf], writes=[b_gwb])

    rmat = C.sb("rmat", [128, 128], F32); b_rmat = S.buf("rmat")
    S.dma("sp", lambda e: e.dma_start(out=rmat[:], in_=C.rmat_d[:, :]), b_rmat, True)
    qr = C.sb("qr", [128, TILE], F32); b_qr = S.buf("qr")

    W, b_W = win

    dq = C.sb("dq", [128, SEQ], BF16); b_dq = S.buf("dq")
    dk = C.sb("dk", [128, SEQ], BF16); b_dk = S.buf("dk")
    dv = C.sb("dv", [128, SEQ], BF16); b_dv = S.buf("dv")
    dgs = C.sb("dgs", [128, SEQ], BF16); b_dgs = S.buf("dgs")
    T6 = C.sb("T6", [128, SEQ], BF16); b_T6 = S.buf("T6")
    T7 = C.sb("T7", [128, SEQ], BF16); b_T7 = S.buf("T7")
    T8, b_T8 = C.G16, C.b_G16
    svb = C.sb("svb", [128, SEQ // 128, HD], BF16); b_svb = S.buf("svb")

    hnb = [C.sb("hnb%d" % i, [128, 8, TILE], BF16) for i in range(2)]
    b_hnb = S.bufs(2, "hnb")
    ctb = [C.sb("ctb%d" % i, [128, TILE], F32) for i in range(2)]
    b_ctb = S.bufs(2, "ctb")
    stb = [C.sb("stb%d" % i, [128, TILE], F32) for i in range(2)]
    b_stb = S.bufs(2, "stb")
    rt1 = [C.sb("rt1_%d" % i, [128, TILE], F32) for i in range(2)]
    b_rt1 = S.bufs(2, "rt1")
    rt2 = [C.sb("rt2_%d" % i, [128, TILE], F32) for i in range(2)]
    b_rt2 = S.bufs(2, "rt2")
    XT = [C.sb("XT%d" % i, [64, TILE + 3], F32) for i in range(2)]
    b_XT = S.bufs(2, "XT")
    xc = C.sb("xc", [64, TILE], F32); b_xc = S.buf("xc")
    xcb = C.sb("xcb", [64, TILE], BF16); b_xcb = S.buf("xcb")
    lr = C.sb("lr", [64, TILE], F32); b_lr = S.buf("lr")
    li = C.sb("li", [64, TILE], F32); b_li = S.buf("li")
    la = C.sb("la", [64, TILE], F32); b_la = S.buf("la")
    la2 = C.sb("la2", [64, TILE], F32); b_la2 = S.buf("la2")
    lh = [C.sb("lh%d" % i, [64, TILE], F32) for i in range(2)]
    b_lh = S.bufs(2, "lh")
    S.op("pool", lambda e: e.memset(XT[0][:, 0:3], 0.0), writes=[b_XT[0]])

    hnq = hn_ag.rearrange("t q (c p) w -> t q p c w", p=128)

    def lru_and_v(t):
        j = t % 2
        ts = slice(t * TILE, (t + 1) * TILE)
        pb = t % 2
        for q in range(4):
            c0 = t * TILE + q * 128
            S.op("pe", lambda e, q=q, c0=c0, pb=pb: e.transpose(
                psb[:, pb, q * 64:(q + 1) * 64], T8[64:128, c0:c0 + 128], ident[64:128, 64:128]),
                reads=[b_T8, b_ident], writes=[bank[6 + pb]])
        S.op("dve", lambda e, t=t, pb=pb: e.tensor_copy(
            svb[:, t * 4:(t + 1) * 4, :], psb[:, pb, 0:256].rearrange("p (a b) -> p a b", a=4)),
            reads=[bank[6 + pb]], writes=[b_svb])
        X = XT[j]
        if t + 1 < NT:
            S.op("pool", lambda e, j=j: e.tensor_copy(XT[1 - j][:, 0:3], XT[j][:, TILE:TILE + 3]),
                 reads=[b_XT[j]], writes=[b_XT[1 - j]])
        S.op("dve", lambda e, X=X: e.tensor_scalar(xc[:], X[:, 3:TILE + 3], lp[:, 3:4], lp[:, 4:5], ALU.mult, ALU.add),
             reads=[b_XT[j], b_lp], writes=[b_xc])
        for k in range(3):
            S.op("dve", lambda e, X=X, k=k: e.scalar_tensor_tensor(
                xc[:], X[:, k:k + TILE], lp[:, k:k + 1], xc[:], ALU.mult, ALU.add),
                reads=[b_XT[j], b_lp, b_xc], writes=[b_xc])
        S.op("pool", lambda e: e.tensor_copy(xcb[:], xc[:]), reads=[b_xc], writes=[b_xcb])
        S.op("pe", lambda e: e.matmul(ps[0:64, 4, :], gwb[:, 0:64], xcb[:], start=True, stop=True),
             reads=[b_gwb, b_xcb], writes=[bank[4]])
        S.op("pe", lambda e: e.matmul(ps[0:64, 5, :], gwb[:, 64:128], xcb[:], start=True, stop=True),
             reads=[b_gwb, b_xcb], writes=[bank[5]])
        S.op("act", lambda e: e.activation(lr[:], ps[0:64, 4, :], AF.Sigmoid, bias=lp[:, 5:6]),
             reads=[bank[4], b_lp], writes=[b_lr])
        S.op("act", lambda e: e.activation(li[:], ps[0:64, 5, :], AF.Sigmoid, bias=lp[:, 6:7]),
             reads=[bank[5], b_lp], writes=[b_li])
        S.op("act", lambda e: e.activation(la[:], lr[:], AF.Exp, scale=lc[:, 0:1]), reads=[b_lr, b_lc], writes=[b_la])
        S.op("act", lambda e: e.activation(la2[:], lr[:], AF.Exp, scale=lc[:, 1:2]), reads=[b_lr, b_lc], writes=[b_la2])
        S.op("dve", lambda e: e.tensor_scalar_min(la2[:], la2[:], 1.0 - 1e-7), reads=[b_la2], writes=[b_la2])
        S.op("act", lambda e: e.activation(la2[:], la2[:], AF.Ln, bias=1.0, scale=-1.0), reads=[b_la2], writes=[b_la2])
        S.op("act", lambda e: e.activation(la2[:], la2[:], AF.Exp, scale=0.5), reads=[b_la2], writes=[b_la2])
        S.op("dve", lambda e: e.tensor_tensor(li[:], li[:], xc[:], ALU.mult), reads=[b_li, b_xc], writes=[b_li])
        S.op("dve", lambda e: e.tensor_tensor(li[:], li[:], la2[:], ALU.mult), reads=[b_li, b_la2], writes=[b_li])
        if t == 0:
            S.op("dve", lambda e, j=j: e.tensor_tensor_scan(lh[j][:], la[:], li[:], 0.0, ALU.mult, ALU.add),
                 reads=[b_la, b_li], writes=[b_lh[j]])
        else:
            S.op("dve", lambda e, j=j: e.tensor_tensor_scan(lh[j][:], la[:], li[:], lh[1 - j][:, TILE - 1:TILE], ALU.mult, ALU.add),
                 reads=[b_la, b_li, b_lh[1 - j]], writes=[b_lh[j]])
        S.op("pool", lambda e, j=j, ts=ts: e.tensor_tensor(T7[0:64, ts], lh[j][:], T8[0:64, ts], ALU.mult),
             reads=[b_lh[j], b_T8], writes=[b_T7])

    nbk = 0
    for t in range(NT):
        j = t % 2
        ts = slice(t * TILE, (t + 1) * TILE)
        S.dma("sp", lambda e, j=j, t=t: e.dma_start(
            out=hnb[j][:], in_=hnq[t // 4][t % 4]), b_hnb[j], True, dram_r=hn_ag_ds[t // 4])
        S.dma("sp", lambda e, j=j, ts=ts: e.dma_start(out=ctb[j][:], in_=ctab[:, ts]), b_ctb[j], True)
        S.dma("sp", lambda e, j=j, ts=ts: e.dma_start(out=stb[j][:], in_=stab[:, ts]), b_stb[j], True)
        pending = []
        for gi in range(7):
            bk = nbk % 4
            nbk += 1
            for kc in range(8):
                S.op("pe", lambda e, bk=bk, kc=kc, gi=gi, j=j: e.matmul(
                    ps[:, bk, :], W[:, kc, gi * 128:(gi + 1) * 128], hnb[j][:, kc, :],
                    start=(kc == 0), stop=(kc == 7)),
                    reads=[b_W, b_hnb[j]], writes=[bank[bk]])
            for fn in pending:
                fn()
            pending = []
            if gi in (0, 1):
                dst, bdst = (dq, b_dq) if gi == 0 else (dk, b_dk)
                S.op("dve", lambda e, bk=bk, j=j: e.tensor_tensor(rt1[j][:], ps[:, bk, :], ctb[j][:], ALU.mult),
                     reads=[bank[bk], b_ctb[j]], writes=[b_rt1[j]])
                S.op("act", lambda e, bk=bk: e.copy(qr[:], ps[:, bk, :]), reads=[bank[bk]], writes=[b_qr])

                def rot(j=j, ts=ts, dst=dst, bdst=bdst):
                    nonlocal nbk
                    bkr = nbk % 4
                    nbk += 1
                    S.op("pe", lambda e, bkr=bkr: e.matmul(
                        ps[:, bkr, :], rmat[:], qr[:], start=True, stop=True),
                        reads=[b_rmat, b_qr], writes=[bank[bkr]])
                    S.op("dve", lambda e, bkr=bkr, j=j: e.tensor_tensor(rt2[j][:], ps[:, bkr, :], stb[j][:], ALU.mult),
                         reads=[bank[bkr], b_stb[j]], writes=[b_rt2[j]])
                    S.op("pool", lambda e, j=j, ts=ts, dst=dst: e.tensor_tensor(dst[:, ts], rt1[j][:], rt2[j][:], ALU.add),
                         reads=[b_rt1[j], b_rt2[j]], writes=[bdst])
                pending.append(rot)
            elif gi == 2:
                S.op("act", lambda e, bk=bk, ts=ts: e.copy(dv[:, ts], ps[:, bk, :]), reads=[bank[bk]], writes=[b_dv])
            elif gi == 3:
                S.op("act", lambda e, bk=bk, ts=ts: e.activation(dgs[:, ts], ps[:, bk, :], AF.Silu),
                     reads=[bank[bk]], writes=[b_dgs])
            elif gi == 4:
                S.op("act", lambda e, bk=bk, ts=ts: e.activation(T6[0:64, ts], ps[0:64, bk, :], AF.Silu),
                     reads=[bank[bk]], writes=[b_T6])
                S.op("dve", lambda e, bk=bk, ts=ts: e.tensor_scalar_mul(T6[64:128, ts], ps[64:128, bk, :], 0.125),
                     reads=[bank[bk]], writes=[b_T6])
            elif gi == 5:
                S.op("act", lambda e, bk=bk, j=j: e.copy(XT[j][:, 3:TILE + 3], ps[0:64, bk, :]),
                     reads=[bank[bk]], writes=[b_XT[j]])
                S.op("dve", lambda e, bk=bk, ts=ts: e.tensor_copy(T7[64:128, ts], ps[64:128, bk, :]),
                     reads=[bank[bk]], writes=[b_T7])
            elif gi == 6:
                S.op("act", lambda e, bk=bk, ts=ts: e.activation(T8[0:64, ts], ps[0:64, bk, :], AF.Silu),
                     reads=[bank[bk]], writes=[b_T8])
                S.op("dve", lambda e, bk=bk, ts=ts: e.tensor_copy(T8[64:128, ts], ps[64:128, bk, :]),
                     reads=[bank[bk]], writes=[b_T8])
            if gi == 3 and t >= 1:
                lru_and_v(t - 1)
    lru_and_v(NT - 1)

    C.prefetch_wo()

    vaug = [C.sb("vaug%d" % i, [128, 2, 128], BF16) for i in range(3)]
    b_vaug = S.bufs(3, "vaug")
    for i in range(3):
        S.op("pool", lambda e, i=i: e.memset(vaug[i][:, :, 64:128], 1.0), writes=[b_vaug[i]])
    pt = [C.sb("pt%d" % i, [128, 256], BF16) for i in range(2)]
    b_pt = S.bufs(2, "pt")
    rcp_l = [rt2[0], rt1[1]]; b_rcp_l = [S.alias(b_rt2[0], "rcp0"), S.alias(b_rt1[1], "rcp1")]
    nrm_l = [rt2[1], ctb[0]]; b_nrm_l = [S.alias(b_rt2[1], "nrm0"), S.alias(b_ctb[0], "nrm1")]
    SEG = 2048
    acc_banks = [0, 1, 2, 3]
    for seg in range(SEQ // SEG):
        for h in range(2):
            hr = slice(h * 64, (h + 1) * 64)
            qbs = []
            for d in (1, 4, 16):
                nmb = SEG // d // 128
                for r in range(d):
                    for mbl in range(nmb):
                        qbs.append((d, r, seg * nmb + mbl))
            started = [False] * 4

            def cols(d, r, mb):
                s0 = d * 128 * mb + r
                return slice(s0, s0 + d * 127 + 1, d)

            def stage_a(k):
                d, r, mb = qbs[k]
                sb_ = 4 + (k % 2)
                vb = k % 2
                qc = cols(d, r, mb)
                blocks = ([(0, mb - 1)] if mb >= 1 else []) + [(1, mb)]
                for half, kmb in blocks:
                    kc_ = cols(d, r, kmb)
                    S.op("pe", lambda e, sb_=sb_, half=half, kc_=kc_, qc=qc, hr=hr: e.matmul(
                        ps[:, sb_, half * 128:(half + 1) * 128], dk[hr, kc_], dq[hr, qc], start=True, stop=True),
                        reads=[b_dk, b_dq], writes=[bank[sb_]])
                for half, kmb in blocks:
                    kc_ = cols(d, r, kmb)
                    S.op("pe", lambda e, vb=vb, half=half, kc_=kc_, hr=hr: e.transpose(
                        psb[:, vb, half * 64:(half + 1) * 64], dv[hr, kc_], ident[hr, hr]),
                        reads=[b_dv, b_ident], writes=[bank[6 + vb]])
                lo = 0 if mb >= 1 else 1
                v3 = k % 3
                S.op("dve", lambda e, vb=vb, lo=lo, v3=v3: e.tensor_copy(
                    vaug[v3][:, lo:2, 0:64], psb[:, vb, lo * 64:128].rearrange("p (a b) -> p a b", b=64)),
                    reads=[bank[6 + vb]], writes=[b_vaug[v3]])

            def stage_b1(k):
                d, r, mb = qbs[k]
                sb_ = 4 + (k % 2)
                vb = k % 2
                lo = 0 if mb >= 1 else 128
                S.op("act", lambda e, sb_=sb_, vb=vb, lo=lo: e.activation(
                    pt[vb][:, lo:256], ps[:, sb_, lo:256], AF.Exp, scale=0.125),
                    reads=[bank[sb_]], writes=[b_pt[vb]])
                S.op("dve", lambda e, vb=vb, lo=lo: e.tensor_tensor(
                    pt[vb][:, lo:256], pt[vb][:, lo:256], dmask[:, lo:256], ALU.mult),
                    reads=[b_pt[vb], b_dmask], writes=[b_pt[vb]])

            def stage_b2(k):
                d, r, mb = qbs[k]
                vb = k % 2
                halves = ([0] if mb >= 1 else []) + [1]
                t0 = d * 128 * mb + r - seg * SEG
                if d == 16:
                    pieces = [(bkk, slice(r, 512, 16), slice(32 * bkk, 32 * bkk + 32)) for bkk in range(4)]
                else:
                    bkk = t0 // 512
                    o0 = t0 - bkk * 512
                    pieces = [(bkk, slice(o0, o0 + d * 127 + 1, d), slice(0, 128))]
                for half in halves:
                    for bkk, oc, pc in pieces:
                        st_ = not started[bkk]
                        started[bkk] = True
                        S.op("pe", lambda e, vb=vb, half=half, bkk=bkk, oc=oc, pc=pc, st_=st_, v3=k % 3: e.matmul(
                            ps[:, bkk, oc], vaug[v3][:, half, :], pt[vb][:, half * 128 + pc.start:half * 128 + pc.stop],
                            start=st_, stop=False, skip_group_check=True),
                            reads=[b_vaug[k % 3], b_pt[vb]], writes=[bank[bkk]])

            nq_ = len(qbs)
            stage_a(0)
            stage_a(1)
            stage_b1(0)
            for k in range(nq_):
                if k + 2 < nq_:
                    stage_a(k + 2)
                if k + 1 < nq_:
                    stage_b1(k + 1)
                stage_b2(k)
            for bkk in range(4):
                cs = slice(seg * SEG + bkk * 512, seg * SEG + (bkk + 1) * 512)
                ro = slice(0, 64) if h == 0 else slice(64, 128)
                rcp, b_rcp, nrm, b_nrm = rcp_l[bkk % 2], b_rcp_l[bkk % 2], nrm_l[bkk % 2], b_nrm_l[bkk % 2]
                S.op("act", lambda e, bkk=bkk, ro=ro, rcp=rcp: e.activation(rcp[ro, :], ps[64:128, bkk, :], AF.Ln),
                     reads=[bank[bkk]], writes=[b_rcp])
                S.op("act", lambda e, ro=ro, rcp=rcp: e.activation(rcp[ro, :], rcp[ro, :], AF.Exp, scale=-1.0),
                     reads=[b_rcp], writes=[b_rcp])
                S.op("dve", lambda e, bkk=bkk, ro=ro, rcp=rcp, nrm=nrm: e.tensor_tensor(nrm[ro, :], ps[0:64, bkk, :], rcp[ro, :], ALU.mult),
                     reads=[bank[bkk], b_rcp], writes=[b_nrm])
                S.op("pool", lambda e, cs=cs, ro=ro, nrm=nrm: e.tensor_tensor(dgs[ro, cs], nrm[ro, :], dgs[ro, cs], ALU.mult),
                     reads=[b_nrm, b_dgs], writes=[b_dgs])

    hv = [hnb[i][:].rearrange("p a b -> p (a b)") for i in range(2)]
    E = [hv[0][:, 2048 * i:2048 * (i + 1)].bitcast(F32).rearrange("p (a b) -> p a b", a=2) for i in range(2)]
    b_E = [S.alias(b_hnb[0], "sbE%d" % i) for i in range(2)]
    SP = [hv[1][:, 1024 * i:1024 * (i + 1)].rearrange("p (a b) -> p a b", a=2) for i in range(3)]
    b_SP = [S.alias(b_hnb[1], "sbSP%d" % i) for i in range(3)]
    WT = [ctb[i][:].bitcast(BF16).rearrange("p (a b) -> p a b", a=2) for i in range(2)]
    b_WT = [S.alias(b_nrm_l[1] if i == 0 else b_ctb[i], "sbWT%d" % i) for i in range(2)]
    RR = []
    b_RR = []
    for src, bsrc in ((stb[0], b_stb[0]), (stb[1], b_stb[1]), (rt1[0], b_rt1[0])):
        v = src[:].bitcast(BF16)
        for hh in range(2):
            RR.append(v[:, hh * TILE:(hh + 1) * TILE])
            b_RR.append(S.alias(bsrc, "sbR%d" % len(RR)))
    groups = []
    for i in range(NT):
        ng = 2 * i + 2
        for gidx in range(ng):
            ka = 4 * i + 3 - 2 * gidx
            groups.append((i, gidx, ng, ka))
    NG = len(groups)
    rstate = {}

    def zb(k):
        return 2 * (k % 3)

    def sb_z(k):
        i, gidx, ng, ka = groups[k]
        qs = slice(i * TILE, (i + 1) * TILE)
        for u in range(2):
            kb = ka - u
            S.op("pe", lambda e, k=k, u=u, kb=kb, qs=qs: e.matmul(
                ps[:, zb(k) + u, :], T7[64:128, kb * 128:(kb + 1) * 128], T6[64:128, qs], start=True, stop=True),
                reads=[b_T7, b_T6], writes=[bank[zb(k) + u]])

    def sb_a(k):
        i, gidx, ng, ka = groups[k]
        e_ = E[k % 2]; be = b_E[k % 2]
        sp = SP[k % 3]; bsp = b_SP[k % 3]
        z0 = zb(k)
        S.op("act", lambda e: e.activation(e_[:], ps[:, z0:z0 + 2, :], AF.Exp),
             reads=[bank[z0], bank[z0 + 1]], writes=[be])
        S.op("act", lambda e: e.activation(sp[:], e_[:], AF.Ln, bias=1.0), reads=[be], writes=[bsp])
        diag = gidx < 2
        if diag:
            for u in range(2):
                a = (ka - u) - 4 * i
                S.op("dve", lambda e, u=u, a=a: e.tensor_tensor(
                    sp[:, u, :], sp[:, u, :], wm[:, 384 - 128 * a:384 - 128 * a + TILE], ALU.mult),
                    reads=[bsp, b_wm], writes=[bsp])
        if gidx == 0:
            r0 = None
            r1 = (sp[:, 0, :], bsp)
        else:
            r0 = rstate["cur"]
            nr = rstate["n"] % 6
            rstate["n"] += 1
            S.op("pool", lambda e, r0=r0, nr=nr: e.tensor_tensor(RR[nr][:], r0[0], sp[:, 0, :], ALU.add),
                 reads=[r0[1], bsp], writes=[b_RR[nr]])
            r1 = (RR[nr][:], b_RR[nr])
        if gidx + 1 < ng:
            nr2 = rstate["n"] % 6
            rstate["n"] += 1
            S.op("dve", lambda e, r1=r1, nr2=nr2: e.tensor_tensor(RR[nr2][:], r1[0], sp[:, 1, :], ALU.add),
                 reads=[r1[1], bsp], writes=[b_RR[nr2]])
            rstate["cur"] = (RR[nr2][:], b_RR[nr2])
        S.op("pe", lambda e: e.matmul(ps[:, z0, :], ntri[:], sp[:, 0, :], start=False, stop=False, skip_group_check=True),
             reads=[b_ntri, bsp], writes=[bank[z0]])
        if r0 is not None:
            S.op("pe", lambda e, r0=r0: e.matmul(ps[:, z0, :], nones[:], r0[0], start=False, stop=False, skip_group_check=True),
                 reads=[b_nones, r0[1]], writes=[bank[z0]])
        S.op("pe", lambda e: e.matmul(ps[:, z0 + 1, :], ntri[:], sp[:, 1, :], start=False, stop=False, skip_group_check=True),
             reads=[b_ntri, bsp], writes=[bank[z0 + 1]])
        S.op("pe", lambda e, r1=r1: e.matmul(ps[:, z0 + 1, :], nones[:], r1[0], start=False, stop=False, skip_group_check=True),
             reads=[b_nones, r1[1]], writes=[bank[z0 + 1]])

    def sb_b(k):
        i, gidx, ng, ka = groups[k]
        wt = WT[k % 2]; bwt = b_WT[k % 2]
        z0 = zb(k)
        ob = 6 + (i % 2)
        qs = slice(i * TILE, (i + 1) * TILE)
        S.op("act", lambda e: e.activation(wt[:], ps[:, z0:z0 + 2, :], AF.Exp),
             reads=[bank[z0], bank[z0 + 1]], writes=[bwt])
        if gidx < 2:
            for u in range(2):
                a = (ka - u) - 4 * i
                S.op("dve", lambda e, u=u, a=a: e.tensor_tensor(
                    wt[:, u, :], wt[:, u, :], wm[:, 384 - 128 * a:384 - 128 * a + TILE], ALU.mult),
                    reads=[bwt, b_wm], writes=[bwt])
        for u in range(2):
            kb = ka - u
            S.op("pe", lambda e, u=u, kb=kb, ob=ob, first=(gidx == 0 and u == 0), last=(gidx == ng - 1 and u == 1): e.matmul(
                ps[0:64, ob, :], svb[:, kb, :], wt[:, u, :], start=first, stop=last),
                reads=[b_svb, bwt], writes=[bank[ob]])
        if gidx == ng - 1:
            S.op("dve", lambda e, ob=ob, qs=qs: e.tensor_tensor(T6[0:64, qs], ps[0:64, ob, :], T6[0:64, qs], ALU.mult),
                 reads=[bank[ob], b_T6], writes=[b_T6])
            if i % 4 == 3:
                q = i // 4
                cq = slice(q * TQ, (q + 1) * TQ)
                ysv = y_src[q].rearrange("u f w -> f u w")
                S.dma("sp", lambda e, ysv=ysv, cq=cq: e.dma_start(
                    out=ysv[0:128], in_=dgs[:, cq].rearrange("p (u w) -> p u w", u=4)), b_dgs, False, dram_w=y_src_ds[q])
                S.dma("sp", lambda e, ysv=ysv, cq=cq: e.dma_start(
                    out=ysv[128:192], in_=T7[0:64, cq].rearrange("p (u w) -> p u w", u=4)), b_T7, False, dram_w=y_src_ds[q])
                S.dma("sp", lambda e, ysv=ysv, cq=cq: e.dma_start(
                    out=ysv[192:256], in_=T6[0:64, cq].rearrange("p (u w) -> p u w", u=4)), b_T6, False, dram_w=y_src_ds[q])
                after_y(q)

    rstate["n"] = 0
    sb_z(0)
    for k in range(NG):
        if k + 1 < NG:
            sb_z(k + 1)
        sb_a(k)
        if k >= 1:
            sb_b(k - 1)
    sb_b(NG - 1)

    return None


STOP = None


def build_fused():
    nc = bass.Bass("TRN2", target_bir_lowering=False)
    with ExitStack() as gst:
        C = Ctx(nc, gst)
        S = C.S
        xT = C.din("xT", [D, TQ], F32)
        gains = C.din("gains", [128, DEPTH + 1, 8], F32)
        win = [C.din("win%d" % l, [128, 8, NCOL], F32) for l in range(DEPTH)]
        wo = [C.din("wo%d" % l, [128, 8, D], F32) for l in range(DEPTH)]
        ctab = C.din("ctab", [128, SEQ], F32)
        stab = C.din("stab", [128, SEQ], F32)
        lru_p = C.din("lru_p", [64, DEPTH, 8], F32)
        gw = C.din("gw", [64, DEPTH, 128], F32)
        idx = C.din("idx", [128, 32], mybir.dt.int32)
        C.rmat_d = C.din("rmat", [128, 128], F32)
        out = C.dout("out", [D, TQ], F32)
        NTL = TQ // TILE
        hn_src = [C.internal("hn_src%d" % l, [NTL, D, TILE], BF16).ap() for l in range(DEPTH)]
        hn_ag = [C.internal("hn_ag%d" % l, [NTL, NQ, D, TILE], BF16).ap() for l in range(DEPTH)]
        y_src = [C.internal("y_src%d" % l, [NQ, 4, 256, TILE], BF16).ap() for l in range(DEPTH)]
        y_ag = [C.internal("y_ag%d" % l, [NQ, NQ, 4, 256, TILE], BF16).ap() for l in range(DEPTH)]
        x1 = C.internal("x1", [D, TQ], F32).ap()
        d_hn_src = [[S.dram("hn_src%d_%d" % (l, t)) for t in range(NTL)] for l in range(DEPTH)]
        d_hn_ag = [[S.dram("hn_ag%d_%d" % (l, t)) for t in range(NTL)] for l in range(DEPTH)]
        d_y_src = [[S.dram("y_src%d_%d" % (l, q)) for q in range(NQ)] for l in range(DEPTH)]
        d_y_ag = [[S.dram("y_ag%d_%d" % (l, q)) for q in range(NQ)] for l in range(DEPTH)]
        d_x1 = S.dram("x1")
        ps = gst.enter_context(nc.psum_tensor("ps", [128, 8, 512], F32))
        bank = S.bufs(8, "bank")
        for b_ in bank:
            b_.excl = True
        Wg = gst.enter_context(nc.sbuf_tensor("Wg", [128, 8, NCOL], BF16))
        b_Wg = S.buf("Wg")
        wstg = [gst.enter_context(nc.sbuf_tensor("wstg%d" % i_, [128, max(NCOL, D)], F32)) for i_ in range(2)]
        b_wstg = S.bufs(2, "wstg")
        C.G16 = gst.enter_context(nc.sbuf_tensor("G16", [128, SEQ], BF16))
        C.b_G16 = S.buf("G16")
        wob_view = C.G16[:].rearrange("p (k n) -> p k n", k=8)

        def load_W(l):
            for kc in range(8):
                j = kc % 2
                S.dma("sp", lambda e, kc=kc, j=j: e.dma_start(out=wstg[j][:, 0:NCOL], in_=win[l][:, kc, :]), b_wstg[j], True)
                S.op("act", lambda e, kc=kc, j=j: e.copy(Wg[:, kc, :], wstg[j][:, 0:NCOL]), reads=[b_wstg[j]], writes=[b_Wg])

        def load_Wo(l):
            for kc in range(8):
                j = kc % 2
                S.dma("sp", lambda e, kc=kc, j=j: e.dma_start(out=wstg[j][:, 0:D], in_=wo[l][:, kc, :]), b_wstg[j], True)
                S.op("act", lambda e, kc=kc, j=j: e.copy(wob_view[:, kc, :], wstg[j][:, 0:D]),
                     reads=[b_wstg[j]], writes=[C.b_G16])

        idx_sb = gst.enter_context(nc.sbuf_tensor("idx_sb", [128, 32], mybir.dt.int32))
        b_idx = S.buf("idx_sb")
        S.dma("sp", lambda e: e.dma_start(out=idx_sb[:], in_=idx[:, :]), b_idx, True)
        groups = [[0, 1, 2, 3], [4, 5, 6, 7]]

        def allgather(src, dst, dsrc, ddst):
            S.collective(lambda e: e.collective_compute(
                "AllGather", ALU.bypass, replica_groups=groups,
                ins=[src.opt()], outs=[dst.opt()], dma_qos="P3"), dsrc, ddst)

        for l in range(DEPTH + 1):
            if STOP == "start":
                break
            S.barrier()
            final = (l == DEPTH)
            if not final:
                load_W(l)
            with ExitStack() as pst:
                C.st = pst
                x_in, x_in_d = (xT, None) if l <= 1 else (x1, d_x1)
                if final:
                    outr = out.rearrange("(c p) t -> p c t", p=128)
                    hn_tile = lambda t: (outr[:, :, t * TILE:(t + 1) * TILE], None)
                    after = None
                else:
                    hn_tile = lambda t, l=l: (hn_src[l][t].rearrange("(c p) w -> p c w", p=128), d_hn_src[l][t])
                    after = lambda t, l=l: allgather(hn_src[l][t], hn_ag[l][t].rearrange("q f w -> (q f) w"),
                                                     d_hn_src[l][t], d_hn_ag[l][t])
                emit_phase_N(
                    C, ps, bank, x_in, x_in_d,
                    y_ag[l - 1].rearrange("a g u f w -> (a g u f) w") if l >= 1 else None,
                    d_y_ag[l - 1] if l >= 1 else None, idx_sb, b_idx,
                    (wob_view, C.b_G16) if l >= 1 else None, gains[:, l, :], hn_tile, after,
                    x1 if (l >= 1 and not final) else None, d_x1 if (l >= 1 and not final) else None, final)
                S.barrier()
            if final or STOP == "N%d" % l:
                break
            with ExitStack() as pst:
                C.st = pst
                after_y = lambda q, l=l: allgather(y_src[l][q].rearrange("u f w -> (u f) w"),
                                                   y_ag[l][q].rearrange("g u f w -> (g u f) w"),
                                                   d_y_src[l][q], d_y_ag[l][q])
                C.prefetch_wo = lambda l=l: load_Wo(l)
                emit_phase_M(C, ps, bank, hn_ag[l], d_hn_ag[l], (Wg, b_Wg), ctab, stab,
                             lru_p[:, l, :], gw[:, l, :], y_src[l], d_y_src[l], after_y)
                S.barrier()
            if STOP == "M%d" % l:
                break
        S.barrier()
        S.emit()
    return nc


_AQ, _AK, _AV, _AG, _BX, _BG, _CQ, _CK, _CV, _CG = 0, 512, 1024, 1536, 2048, 2304, 2560, 2816, 3072, 3328


def _win_cols(g):
    def rng(base, n):
        return list(range(base, base + n))

    def swap(base):
        cols = []
        for h in range(2):
            hb = base + (2 * g + h) * 64
            cols += rng(hb + 8, 8) + rng(hb, 8) + rng(hb + 16, 48)
        return cols

    cols = []
    cols += rng(_AQ + 128 * g, 128)
    cols += rng(_AK + 128 * g, 128)
    cols += rng(_AV + 128 * g, 128)
    cols += rng(_AG + 128 * g, 128)
    cols += rng(_CG + 64 * g, 64) + rng(_CQ + 64 * g, 64)
    cols += rng(_BX + 64 * g, 64) + rng(_CK + 64 * g, 64)
    cols += rng(_BG + 64 * g, 64) + rng(_CV + 64 * g, 64)
    return np.asarray(cols)


def _wout_rows():
    rows = []
    for g in range(NQ):
        rows += list(range(128 * g, 128 * g + 128))
        rows += list(range(512 + 64 * g, 512 + 64 * g + 64))
        rows += list(range(768 + 64 * g, 768 + 64 * g + 64))
    return np.asarray(rows)


def _rot_matrix():
    r = np.zeros((128, 128), np.float32)
    for h in range(2):
        for d in range(8):
            r[h * 64 + d + 8, h * 64 + d] = 1.0
            r[h * 64 + d, h * 64 + d + 8] = 1.0
    return r


def _rope_tables():
    pos = np.arange(SEQ, dtype=np.float32)
    inv_freq = (np.float64(ROPE_THETA) ** (-np.arange(0, 16, 2, dtype=np.float64) / 16.0)).astype(np.float32)
    ang = (pos[:, None] * inv_freq[None, :]).astype(np.float32).astype(np.float64)
    cos = np.cos(ang).astype(np.float32).T
    sin = np.sin(ang).astype(np.float32).T
    ct = np.ones((128, SEQ), np.float32)
    st = np.zeros((128, SEQ), np.float32)
    for h in range(2):
        ct[h * 64:h * 64 + 8] = cos
        ct[h * 64 + 8:h * 64 + 16] = cos
        st[h * 64:h * 64 + 8] = -sin
        st[h * 64 + 8:h * 64 + 16] = sin
    return ct, st


_PROG = {}


def _kc_layout(w):
    n = w.shape[1]
    return np.ascontiguousarray(w.reshape(8, 128, n).transpose(1, 0, 2))


def kernel(x, norm_gain, w_in, conv_w, conv_b, gate_a_w, gate_a_b, gate_x_w, gate_x_b,
           lru_lambda, w_out, final_gain):
    inp = dict(x=x, norm_gain=norm_gain, w_in=w_in, conv_w=conv_w, conv_b=conv_b, gate_a_w=gate_a_w,
               gate_a_b=gate_a_b, gate_x_w=gate_x_w, gate_x_b=gate_x_b, lru_lambda=lru_lambda,
               w_out=w_out, final_gain=final_gain)
    inp = {k: np.asarray(v, dtype=np.float32) for k, v in inp.items()}
    ct, st = _rope_tables()
    rmat = _rot_matrix()
    rows = _wout_rows()
    gains = np.stack([inp["norm_gain"][l].reshape(8, 128).T for l in range(DEPTH)]
                     + [inp["final_gain"].reshape(8, 128).T], axis=1)
    gains = np.ascontiguousarray(gains.astype(np.float32))
    wo = [_kc_layout(inp["w_out"][l][rows, :]) for l in range(DEPTH)]
    maps = []
    for c in range(8):
        b, i = divmod(c, NQ)
        xs = inp["x"][b].reshape(NQ, NQ, TILE, D)[:, i]
        m = {"xT": np.ascontiguousarray(xs.reshape(TQ, D).T), "gains": gains,
             "ctab": ct, "stab": st, "rmat": rmat}
        cols = _win_cols(i)
        for l in range(DEPTH):
            m["win%d" % l] = _kc_layout(inp["w_in"][l][:, cols])
            m["wo%d" % l] = wo[l]
        lru_p = np.stack([np.stack(
            [inp["conv_w"][l][k, 64 * i:64 * i + 64] for k in range(4)]
            + [inp["conv_b"][l][64 * i:64 * i + 64], inp["gate_a_b"][l][i], inp["gate_x_b"][l][i],
               inp["lru_lambda"][l][64 * i:64 * i + 64]], axis=1) for l in range(DEPTH)], axis=1)
        m["lru_p"] = np.ascontiguousarray(lru_p.astype(np.float32))
        gw = np.stack([np.concatenate([inp["gate_a_w"][l][i], inp["gate_x_w"][l][i]], axis=1)
                       for l in range(DEPTH)], axis=1)
        m["gw"] = np.ascontiguousarray(gw.astype(np.float32))
        idx = np.zeros((128, 32), np.int32)
        p = np.arange(128)
        for kc in range(8):
            g, half = divmod(kc, 2)
            for jj in range(4):
                idx[:, kc * 4 + jj] = ((jj * NQ + g) * 4 + i) * 256 + half * 128 + p
        m["idx"] = idx
        maps.append(m)
    if "nc" not in _PROG:
        _PROG["nc"] = build_fused()
    res = run_bass_kernel_spmd(_PROG["nc"], maps, core_ids=list(range(8))).results
    out = np.empty((B, SEQ, D), np.float32)
    for c in range(8):
        b, i = divmod(c, NQ)
        out[b].reshape(NQ, NQ, TILE, D)[:, i] = res[c]["out"].T.reshape(NQ, TILE, D)
    return out
```
